# Optimizing a Trainium2 kernel written in Bass

```python
import math
import jax, jax.numpy as jnp
from jax import lax
import numpy as np

D_MODEL = 1024
BATCH = 4
SEQ = 4096
DEPTH = 1

D_MIX = D_MODEL
D_LRU = D_MIX // 2
D_ATTN = D_MIX - D_LRU
N_LRU_BLOCKS = 8
LRU_BLOCK = D_LRU // N_LRU_BLOCKS
CONV_WIDTH = 4
LRU_C = 8.0
ATTN_HEAD_DIM = 64
N_ATTN_HEADS = D_ATTN // (2 * ATTN_HEAD_DIM)
V_HEAD_DIM = 2 * ATTN_HEAD_DIM
D_FF = 4 * D_MODEL
Q_BLOCK = 128
EPS = 1e-6
D_IN = 3 * D_ATTN + 2 * D_LRU

kernel_name = "hybrid_diffattn_rglru_block"


def _alibi_slopes(n_heads):
    return np.array([2.0 ** (-8.0 * (i + 1) / n_heads) for i in range(n_heads)], dtype=np.float32)


def rmsnorm(x, g):
    xf = x.astype(jnp.float32)
    y = xf * lax.rsqrt(jnp.mean(xf * xf, axis=-1, keepdims=True) + EPS)
    return (y * g.astype(jnp.float32)).astype(x.dtype)


def causal_depthwise_conv(u, w, b):
    S = u.shape[1]
    up = jnp.pad(u, ((0, 0), (CONV_WIDTH - 1, 0), (0, 0)))
    out = b
    for k in range(CONV_WIDTH):
        out = out + up[:, k:k + S] * w[k]
    return out


def rg_lru(u, w_rg, b_rg, w_ig, b_ig, lru_L):
    B, S, _ = u.shape
    ub = u.reshape(B, S, N_LRU_BLOCKS, LRU_BLOCK)
    r = jax.nn.sigmoid(jnp.einsum('bsnc,ncd->bsnd', ub, w_rg) + b_rg).reshape(B, S, D_LRU)
    i = jax.nn.sigmoid(jnp.einsum('bsnc,ncd->bsnd', ub, w_ig) + b_ig).reshape(B, S, D_LRU)
    log_a = LRU_C * r.astype(jnp.float32) * jax.nn.log_sigmoid(lru_L.astype(jnp.float32))
    a = jnp.exp(log_a)
    mult = jnp.sqrt(-jnp.expm1(2.0 * log_a))
    bx = mult * (i * u).astype(jnp.float32)

    def combine(c1, c2):
        a1, b1 = c1
        a2, b2 = c2
        return a1 * a2, a2 * b1 + b2

    _, h = lax.associative_scan(combine, (a, bx), axis=1)
    return h.astype(u.dtype)


def diff_attention(q, k, v, lam, subln_g, lambda_init):
    B, S = q.shape[0], q.shape[1]
    H, d = N_ATTN_HEADS, ATTN_HEAD_DIM
    nb = S // Q_BLOCK
    qb = (q * (d ** -0.5)).reshape(B, nb, Q_BLOCK, H, 2, d).transpose(1, 0, 3, 4, 2, 5)
    kt = k.transpose(0, 2, 3, 1, 4)
    vt = v.transpose(0, 2, 1, 3)
    slopes = jnp.asarray(_alibi_slopes(H))[:, None, None, None]
    kpos = jnp.arange(S)

    def one_block(args):
        blk, qblk = args
        qpos = blk * Q_BLOCK + jnp.arange(Q_BLOCK)
        s = jnp.einsum('bhiqd,bhikd->bhiqk', qblk, kt).astype(jnp.float32)
        dist = (qpos[:, None] - kpos[None, :]).astype(jnp.float32)
        s = jnp.where(dist >= 0, s - slopes * dist, -jnp.inf)
        p = jax.nn.softmax(s, axis=-1)
        p = p[:, :, 0] - lam * p[:, :, 1]
        return jnp.einsum('bhqk,bhkd->bhqd', p.astype(v.dtype), vt)

    o = lax.map(one_block, (jnp.arange(nb), qb))
    o = o.transpose(1, 0, 3, 2, 4).reshape(B, S, H, V_HEAD_DIM)
    o = rmsnorm(o, subln_g) * (1.0 - lambda_init)
    return o.reshape(B, S, H * V_HEAD_DIM)


def setup_inputs(seed: int = 0) -> dict:
    key = jax.random.key(seed)
    ks = jax.random.split(key, 24)
    f32 = jnp.float32
    L = DEPTH

    def nrm(k, shape, scale):
        return jax.random.normal(k, shape, f32) * scale

    a0 = jax.random.uniform(ks[12], (L, D_LRU), f32, 0.9, 0.999)
    return {
        "x": jax.random.normal(ks[0], (BATCH, SEQ, D_MODEL), f32),
        "norm_mix_g": 1.0 + nrm(ks[1], (L, D_MODEL), 0.02),
        "w_in": nrm(ks[2], (L, D_MODEL, D_IN), D_MODEL ** -0.5),
        "conv_w": nrm(ks[3], (L, CONV_WIDTH, D_LRU), CONV_WIDTH ** -0.5),
        "conv_b": nrm(ks[4], (L, D_LRU), 0.02),
        "w_rg": nrm(ks[5], (L, N_LRU_BLOCKS, LRU_BLOCK, LRU_BLOCK), LRU_BLOCK ** -0.5),
        "b_rg": nrm(ks[6], (L, N_LRU_BLOCKS, LRU_BLOCK), 0.02),
        "w_ig": nrm(ks[7], (L, N_LRU_BLOCKS, LRU_BLOCK, LRU_BLOCK), LRU_BLOCK ** -0.5),
        "b_ig": nrm(ks[8], (L, N_LRU_BLOCKS, LRU_BLOCK), 0.02),
        "lru_L": jnp.log(a0) - jnp.log1p(-a0),
        "lambda_q1": nrm(ks[13], (L, ATTN_HEAD_DIM), 0.1),
        "lambda_k1": nrm(ks[14], (L, ATTN_HEAD_DIM), 0.1),
        "lambda_q2": nrm(ks[15], (L, ATTN_HEAD_DIM), 0.1),
        "lambda_k2": nrm(ks[16], (L, ATTN_HEAD_DIM), 0.1),
        "subln_g": 1.0 + nrm(ks[17], (L, V_HEAD_DIM), 0.02),
        "w_out": nrm(ks[18], (L, D_MIX, D_MODEL), D_MIX ** -0.5),
        "norm_mlp_g": 1.0 + nrm(ks[19], (L, D_MODEL), 0.02),
        "w_up": nrm(ks[20], (L, D_MODEL, D_FF), D_MODEL ** -0.5),
        "w_down": nrm(ks[21], (L, D_FF, D_MODEL), D_FF ** -0.5),
        "final_g": 1.0 + nrm(ks[22], (D_MODEL,), 0.02),
    }


def reference(x, norm_mix_g, w_in, conv_w, conv_b, w_rg, b_rg, w_ig, b_ig, lru_L,
              lambda_q1, lambda_k1, lambda_q2, lambda_k2, subln_g, w_out,
              norm_mlp_g, w_up, w_down, final_g):
    B, S, _ = x.shape
    H, d = N_ATTN_HEADS, ATTN_HEAD_DIM
    for l in range(DEPTH):
        h = rmsnorm(x, norm_mix_g[l])
        proj = h @ w_in[l]
        q, k, v, lru_u, lru_g = jnp.split(
            proj, [D_ATTN, 2 * D_ATTN, 3 * D_ATTN, 3 * D_ATTN + D_LRU], axis=-1)

        lambda_init = 0.8 - 0.6 * math.exp(-0.3 * l)
        lam = (jnp.exp(jnp.sum(lambda_q1[l].astype(jnp.float32) * lambda_k1[l].astype(jnp.float32)))
               - jnp.exp(jnp.sum(lambda_q2[l].astype(jnp.float32) * lambda_k2[l].astype(jnp.float32)))
               + lambda_init)
        attn_out = diff_attention(q.reshape(B, S, H, 2, d), k.reshape(B, S, H, 2, d),
                                  v.reshape(B, S, H, V_HEAD_DIM), lam, subln_g[l], lambda_init)

        u = causal_depthwise_conv(lru_u, conv_w[l], conv_b[l])
        lru_out = rg_lru(u, w_rg[l], b_rg[l], w_ig[l], b_ig[l], lru_L[l]) * jax.nn.gelu(lru_g)

        mix = jnp.concatenate([attn_out, lru_out], axis=-1)
        x = x + mix @ w_out[l]

        hm = rmsnorm(x, norm_mlp_g[l])
        x = x + jnp.square(jax.nn.relu(hm @ w_up[l])) @ w_down[l]
    return rmsnorm(x, final_g)
```

```python
import math
from contextlib import ExitStack

import numpy as np
import ml_dtypes
import concourse.bass as bass
import concourse.mybir as mybir
from concourse.bass_utils import run_bass_kernel_spmd

F32 = mybir.dt.float32
BF = mybir.dt.bfloat16
AF = mybir.ActivationFunctionType
ALU = mybir.AluOpType

D = 1024
T = 4096
TO = 2048
NH = 4
EPS = 1e-6
LAMBDA_INIT = 0.8 - 0.6 * math.exp(0.0)
GELU_C = 2.0 * math.sqrt(2.0 / math.pi)
NEG = -30000.0

PC_GMIX, PC_GMLP, PC_GFIN = 0, 8, 16
PC_CW, PC_CB, PC_BRG, PC_BIG, PC_L = 24, 40, 44, 48, 52
PC_SUBG, PC_SEL, PC_LAM = 56, 57, 59
PC_SUBGR = 59 + 256
NPC = 59 + 256 + 128


class Buf:
    __slots__ = ("name", "w", "r", "pr")

    def __init__(self, name):
        self.name = name
        self.w = {}
        self.r = {}
        self.pr = {}


class Eng:
    def __init__(self, name, eng, sem):
        self.name, self.eng, self.sem = name, eng, sem
        self.count = 0
        self.seen = {}


class DSem:
    def __init__(self, name, sem):
        self.name, self.sem = name, sem
        self.count = 0


class KB:
    def __init__(self, nc, es):
        self.nc = nc
        self.es = es
        self.E = {}
        for name, eng in (("pe", nc.tensor), ("act", nc.scalar), ("dve", nc.vector),
                          ("pool", nc.gpsimd), ("sp", nc.sync)):
            self.E[name] = Eng(name, eng, es.enter_context(nc.semaphore("s_" + name)))
        self.dsems = []
        self.nbuf = 0

    def dsem(self, name):
        d = DSem(name, self.es.enter_context(self.nc.semaphore("d_" + name)))
        self.dsems.append(d)
        return d

    def buf(self, name="b"):
        self.nbuf += 1
        return Buf(f"{name}{self.nbuf}")

    def _wait(self, E, st, raw):
        src, val = st
        if src is E and (E.name in ("pe", "sp") or not raw):
            return
        if isinstance(src, DSem):
            val = src.count
        key = id(src)
        if E.seen.get(key, 0) >= val:
            return
        E.eng.wait_ge(src.sem, val)
        E.seen[key] = val

    def _deps(self, E, reads, writes):
        for b in reads:
            for st in b.w.values():
                self._wait(E, st, True)
        for b in writes:
            for st in b.r.values():
                self._wait(E, st, False)
            for st in b.pr.values():
                self._wait(E, st, False)

    def _post(self, src, st, reads, writes):
        for b in reads:
            b.r[id(src)] = st
        for b in writes:
            if b.r:
                b.pr = b.r
                b.w = {}
                b.r = {}
            b.w[id(src)] = st

    def op(self, en, fn, reads=(), writes=(), signal=True):
        E = self.E[en]
        self._deps(E, reads, writes)
        ins = fn(E.eng)
        if signal:
            ins.then_inc(E.sem, 1)
            E.count += 1
            st = (E, E.count)
        else:
            st = (E, E.count + 1)
        self._post(E, st, reads, writes)
        return ins

    def dma(self, out, in_, dsem, reads=(), writes=(), en="sp"):
        E = self.E[en]
        self._deps(E, reads, writes)
        ins = E.eng.dma_start(out=out, in_=in_)
        ins.then_inc(dsem.sem, 16)
        dsem.count += 16
        self._post(dsem, (dsem, dsem.count), reads, writes)

    def barrier(self):
        for E in self.E.values():
            for F in self.E.values():
                if F is not E and F.count > 0:
                    self._wait(E, (F, F.count), True)
            for d in self.dsems:
                if d.count > 0:
                    self._wait(E, (d, d.count), True)


def build_program(debug=False):
    nc = bass.Bass("TRN2", target_bir_lowering=False)
    dbg = {}
    if debug:
        for nm, shp, dt in (("d_hT", [128, 8, 512], BF), ("d_hTo", [128, 8, 512], BF), ("d_lru", [128, 4, TO], BF),
                            ("d_attn", [128, 4, TO], BF), ("d_x1", [128, 8, TO], F32), ("d_x2", [128, 8, TO], F32),
                            ("d_kt", [128, 2, 512], BF), ("d_qt", [128, 2, 512], BF), ("d_v", [128, 4, 129], BF),
                            ("d_sm", [128, 16], F32), ("d_osb", [128, 2, 4, 129], F32)):
            dbg[nm] = nc.dram_tensor(nm, shp, dt, kind="ExternalOutput").ap()

    def din(name, shape, dt=F32):
        return nc.dram_tensor(name, list(shape), dt, kind="ExternalInput").ap()

    xf = din("xf", [D, T])
    xo = din("xo", [D, TO])
    wqkv = din("wqkv", [NH, D, 384])
    wu = din("wu", [D, 512])
    wg = din("wg", [D, 512])
    wout = din("wout", [D, D])
    wup = din("wup", [D, 4 * D])
    wdn = din("wdn", [4 * D, D])
    pvec = din("pvec", [128, NPC])
    wgate = din("wgate", [128, 8, 128])
    cbf = din("cbf", [128, 640], BF)
    kaug = din("kaug", [NH, 4, T], BF)
    qaug = din("qaug", [NH, 4, TO], BF)
    yT = nc.dram_tensor("yT", [D, TO], F32, kind="ExternalOutput").ap()

    xf_v = xf.rearrange("(c p) t -> p c t", p=128)
    xo_v = xo.rearrange("(c p) t -> p c t", p=128)
    yT_v = yT.rearrange("(c p) t -> p c t", p=128)
    wu_v = wu.rearrange("(c p) n -> p c n", p=128)
    wg_v = wg.rearrange("(c p) n -> p c n", p=128)
    wout_v = wout.rearrange("(c p) n -> p c n", p=128)
    wup_v = wup.rearrange("(c p) n -> p c n", p=128)
    wdn_v = wdn.rearrange("(g j p) n -> p g j n", j=4, p=128)

    with ExitStack() as es:
        kb = KB(nc, es)

        def sb(name, shape, dt, stack=es):
            return stack.enter_context(nc.sbuf_tensor(name, list(shape), dt))

        def ps(name, shape, dt=F32, stack=es):
            return stack.enter_context(nc.psum_tensor(name, list(shape), dt))

        pv = sb("pv", [128, NPC], F32)
        cb_sb = sb("cb_sb", [128, 384], BF)
        ones_bf = sb("ones_bf", [128, 128], BF)
        epsc = sb("epsc", [128, 1], F32)
        onec = sb("onec", [128, 1], F32)
        c1 = sb("c1", [128, 4], F32)
        c1x2 = sb("c1x2", [128, 4], F32)
        hbias = sb("hbias", [128, 8], F32)
        qc = sb("qc", [128, 1], F32)
        mhalf = sb("mhalf", [128, 4], F32)
        nlam = sb("nlam", [128, 1], F32)
        lsum = sb("lsum", [128, 4], F32)
        wgate_b = sb("wgate_b", [128, 8, 128], BF)
        attn_mixT = sb("attn_mixT", [128, 4, TO], BF)
        lru_mixT = sb("lru_mixT", [128, 4, TO], BF)
        wst = [sb(f"wst{i}", [128, 512], F32) for i in range(4)]
        wst_b = [kb.buf("wst") for _ in range(4)]
        wst_d = [kb.dsem(f"wst{i}") for i in range(4)]
        wst_i = [0]

        b_lru = kb.buf("lru")
        b_attn = kb.buf("attn")
        ident = cb_sb[:, 0:128]
        maskAB = [cb_sb[:, 128:256], cb_sb[:, 256:384]]

        P = [ps(f"P{i}", [128, 512]) for i in range(8)]
        Pb = [kb.buf("P") for _ in range(8)]
        PT, PTb = P[7], Pb[7]
        PTbf = P[7][:].bitcast(BF)
        ident_f = sb("ident_f", [128, 128], F32)

        s0 = ExitStack()
        with s0:
            ltmp = sb("ltmp", [128, 4, 64], F32, s0)
            wgate_f = sb("wgate_f", [128, 8, 128], F32, s0)
            b_const = kb.buf("const")
            d_const = kb.dsem("const")
            kb.dma(pv[:], pvec[:, :], d_const, writes=[b_const])
            b_sel = kb.buf("sel")
            d_sel = kb.dsem("sel")
            kb.dma(cb_sb[:, 0:128], cbf[:, 0:128], d_const, writes=[kb.buf()])
            kb.dma(cb_sb[:, 128:384], cbf[:, 384:640], d_sel, writes=[b_sel])
            kb.dma(wgate_f[:], wgate[:, :, :], d_const, writes=[kb.buf()])
            b_const.w = {id(d_const): (d_const, d_const.count)}

            def pcol(c, n=1):
                return pv[:, c:c + n]

            b_misc = kb.buf("misc")
            kb.op("pool", lambda e: e.memset(ones_bf[:], 1.0), writes=[b_misc])
            kb.op("pool", lambda e: e.tensor_copy(out=ident_f[:], in_=cb_sb[:, 0:128]), reads=[b_const], writes=[b_misc])
            kb.op("pool", lambda e: e.memset(epsc[:], EPS), writes=[b_misc])
            kb.op("pool", lambda e: e.memset(onec[:], 1.0), writes=[b_misc])
            kb.op("pool", lambda e: e.tensor_copy(out=wgate_b[:], in_=wgate_f[:]), reads=[b_const], writes=[b_misc])
            kb.op("act", lambda e: e.activation(out=c1[:], in_=pcol(PC_L, 4), func=AF.Exp, scale=-1.0),
                  reads=[b_const], writes=[b_misc])
            kb.op("act", lambda e: e.activation(out=c1[:], in_=c1[:], func=AF.Ln, bias=onec[:, 0:1], scale=1.0),
                  reads=[b_misc], writes=[b_misc])
            kb.op("dve", lambda e: e.tensor_scalar(out=c1x2[:], in0=c1[:], scalar1=-4.0, scalar2=None, op0=ALU.mult),
                  reads=[b_misc], writes=[b_misc])
            kb.op("dve", lambda e: e.tensor_scalar(out=c1[:], in0=c1[:], scalar1=-8.0, scalar2=None, op0=ALU.mult),
                  reads=[b_misc], writes=[b_misc])
            kb.op("dve", lambda e: e.tensor_scalar(out=hbias[:], in0=pcol(PC_BRG, 8), scalar1=0.5, scalar2=None,
                                                   op0=ALU.mult),
                  reads=[b_const], writes=[b_misc])
            kb.op("pool", lambda e: e.memset(qc[:], 0.25), writes=[b_misc])
            kb.op("pool", lambda e: e.memset(mhalf[:], -0.5), writes=[b_misc])
            lv = pv[:, PC_LAM:PC_LAM + 256].rearrange("p (a d) -> p a d", d=64)
            kb.op("dve", lambda e: e.tensor_tensor(out=ltmp[:, 0, :], in0=lv[:, 0, :], in1=lv[:, 1, :], op=ALU.mult),
                  reads=[b_const], writes=[b_misc])
            kb.op("dve", lambda e: e.tensor_tensor(out=ltmp[:, 1, :], in0=lv[:, 2, :], in1=lv[:, 3, :], op=ALU.mult),
                  reads=[b_const], writes=[b_misc])
            kb.op("dve", lambda e: e.reduce_sum(out=lsum[:, 0:2], in_=ltmp[:, 0:2, :], axis=mybir.AxisListType.X),
                  reads=[b_misc], writes=[b_misc])
            kb.op("act", lambda e: e.activation(out=lsum[:, 2:4], in_=lsum[:, 0:2], func=AF.Exp),
                  reads=[b_misc], writes=[b_misc])
            kb.op("dve", lambda e: e.scalar_tensor_tensor(out=nlam[:], in0=lsum[:, 3:4], scalar=-LAMBDA_INIT,
                                                          in1=lsum[:, 2:3], op0=ALU.add, op1=ALU.subtract),
                  reads=[b_misc], writes=[b_misc])


        def load_cast(dst, src, n, scale_col=None, cast_en="pool", dst_buf=None):
            i = wst_i[0] % 4
            wst_i[0] += 1
            st, stb, std = wst[i], wst_b[i], wst_d[i]
            kb.dma(st[:, 0:n], src, std, writes=[stb])
            wr = [dst_buf] if dst_buf is not None else []
            if scale_col is None:
                kb.op(cast_en, lambda e: e.tensor_copy(out=dst, in_=st[:, 0:n]), reads=[stb, b_const], writes=wr)
            else:
                kb.op(cast_en, lambda e: e.tensor_scalar(out=dst, in0=st[:, 0:n], scalar1=scale_col, scalar2=1.0,
                                                         op0=ALU.mult, op1=ALU.mult),
                      reads=[stb, b_const], writes=wr)

        pi = [0]

        def next_bank(lo=0, hi=4):
            i = lo + pi[0] % (hi - lo)
            pi[0] += 1
            return P[i], Pb[i]

        def rms_tile(stack_bufs, xt, xt_b, dstT, dst_b, t0, n=512):
            sq, sq_b, rt, rt_b = stack_bufs
            kb.op("act", lambda e: e.activation(out=sq[:, :, 0:n], in_=xt[:, :, 0:n], func=AF.Square),
                  reads=[xt_b], writes=[sq_b])
            pb, pbb = next_bank()
            for c in range(8):
                kb.op("pe", lambda e, c=c: e.matmul(pb[:, 0:n], lhsT=ones_bf[:], rhs=sq[:, c, 0:n],
                                                    start=(c == 0), stop=(c == 7)),
                      reads=[sq_b, b_misc], writes=[pbb], signal=(c == 7))
            if dstT is None:
                kb.op("act", lambda e: e.activation(out=rt[:, 0:n], in_=pb[:, 0:n], func=AF.Sqrt,
                                                    bias=epsc[:, 0:1], scale=1.0 / D),
                      reads=[pbb, b_misc], writes=[rt_b])
                kb.op("dve", lambda e: e.reciprocal(out=rt[:, 0:n], in_=rt[:, 0:n]), reads=[rt_b], writes=[rt_b])
            else:
                kb.op("act", lambda e: e.activation(out=rt[:, 0:n], in_=pb[:, 0:n], func=AF.Ln,
                                                    bias=epsc[:, 0:1], scale=1.0 / D),
                      reads=[pbb, b_misc], writes=[rt_b])
                kb.op("act", lambda e: e.activation(out=pb[:, 0:n], in_=rt[:, 0:n], func=AF.Exp, scale=-0.5),
                      reads=[rt_b], writes=[pbb])
                for c in range(8):
                    kb.op("dve", lambda e, c=c: e.tensor_tensor(out=dstT[:, c, t0:t0 + n], in0=xt[:, c, 0:n],
                                                                in1=pb[:, 0:n], op=ALU.mult),
                          reads=[xt_b, pbb, b_misc], writes=[dst_b])

        with ExitStack() as s1:
            hT = sb("hT", [128, 8, T], BF, s1)
            hTo = sb("hTo", [128, 8, TO], BF, s1)
            hT_b = kb.buf("hT")
            hTo_b = kb.buf("hTo")
            wqk_b = [sb("wqk_b0", [128, 8, 384], BF, s1)] * 2
            wqk_bb = [kb.buf("wqk")] * 2

            def load_wqk(h):
                for c in range(8):
                    load_cast(wqk_b[h % 2][:, c, :], wqkv[h].rearrange("(c p) n -> p c n", p=128)[:, c, :], 384,
                              pcol(PC_GMIX + c), dst_buf=wqk_bb[h % 2])
            x1T = hT[:].bitcast(F32)
            hmT = hTo
            hm_b = hTo_b
            x1_b = [kb.buf("x1") for _ in range(4)]
            x1_d = [kb.dsem(f"x1_{i}") for i in range(4)]

            def prefetch_x1():
                for tt in range(4):
                    kb.dma(x1T[:, :, tt * 512:(tt + 1) * 512], xo_v[:, :, tt * 512:(tt + 1) * 512], x1_d[tt],
                           writes=[x1_b[tt], hT_b])

            with ExitStack() as sA:
                NXS = 3
                xst = [sb(f"xst{i}", [128, 8, 512], F32, sA) for i in range(NXS)]
                xst_b = [kb.buf("xst") for _ in range(NXS)]
                xst_d = [kb.dsem(f"xst{i}") for i in range(NXS)]
                sq = sb("sqA", [128, 8, 512], BF, sA)
                sq_b = kb.buf("sq")
                rt = [sb(f"rtA{i}", [128, 512], F32, sA) for i in range(2)]
                rt_b = [kb.buf("rt") for _ in range(2)]
                selm = [cb_sb[:, 128:256], cb_sb[:, 256:384]]
                hT_t = [kb.buf("hTt") for _ in range(8)]

                def select_own(it):
                    for b in range(2):
                        ev = it * 512 + b * 256
                        ob = 2 * it + b
                        for ch in range(2):
                            pb, pbb = next_bank(4, 8)
                            pv3 = pb[:].rearrange("p (c t) -> p c t", t=128)
                            kb.op("pe", lambda e: e.matmul(pv3, lhsT=selm[0], rhs=hT[:, 4 * ch:4 * ch + 4, ev:ev + 128],
                                                           start=True, stop=False),
                                  reads=[hT_t[it], b_sel], writes=[pbb], signal=False)
                            kb.op("pe", lambda e: e.matmul(pv3, lhsT=selm[1],
                                                           rhs=hT[:, 4 * ch:4 * ch + 4, ev + 128:ev + 256],
                                                           start=False, stop=True),
                                  reads=[hT_t[it], b_sel], writes=[pbb])
                            kb.op("act", lambda e: e.activation(out=hTo[:, 4 * ch:4 * ch + 4, ob * 128:(ob + 1) * 128],
                                                                in_=pv3, func=AF.Copy),
                                  reads=[pbb], writes=[hTo_b])

                for it in range(8):
                    s = it % NXS
                    t0 = it * 512
                    kb.dma(xst[s][:], xf_v[:, :, t0:t0 + 512], xst_d[s], writes=[xst_b[s]])
                    rms_tile((sq, sq_b, rt[it % 2], rt_b[it % 2]), xst[s][:], xst_b[s], hT, hT_t[it], t0)
                    if it >= 2:
                        select_own(it - 2)
                select_own(6)
                select_own(7)
                kb.dma(cb_sb[:, 128:384], cbf[:, 128:384], d_sel, writes=[b_sel])
                if debug:
                    kb.dma(dbg["d_hT"], hT[:, :, 0:512], d_const, reads=[hT_b])
                    kb.dma(dbg["d_hTo"], hTo[:, :, 0:512], d_const, reads=[hTo_b])
                kb.barrier()

            with ExitStack() as sB:
                wu_b = sb("wu_b", [128, 8, 512], BF, sB)
                wg_b = sb("wg_b", [128, 8, 512], BF, sB)
                wu_bb, wg_bb = kb.buf("wu"), kb.buf("wg")
                for c in range(8):
                    load_cast(wu_b[:, c, :], wu_v[:, c, :], 512, pcol(PC_GMIX + c), dst_buf=wu_bb,
                              cast_en=("pool" if c % 2 else "dve"))
                for c in range(8):
                    load_cast(wg_b[:, c, :], wg_v[:, c, :], 512, pcol(PC_GMIX + c), dst_buf=wg_bb,
                              cast_en=("pool" if c % 2 else "dve"))
                load_wqk(0)
                NCH = 4

                def mkset(i):
                    S = {}
                    for nm, shp, dt in (("ub", [128, 516], F32), ("uc", [128, 512], F32), ("ucbf", [128, 256], F32),
                                        ("rr", [128, 512], F32), ("ii", [128, 512], F32), ("a2", [128, 512], F32)):
                        S[nm] = sb(f"{nm}_{i}", shp, dt, sB)
                        S[nm + "_b"] = kb.buf(nm)
                    S["ucb"], S["ucb_b"] = S["ucbf"][:].bitcast(BF), S["ucbf_b"]
                    S["g2"], S["g2_b"] = S["ucbf"][:], S["ucbf_b"]
                    S["hs"], S["hs_b"] = S["uc"][:], S["uc_b"]
                    S["ls"], S["ls_b"] = S["a2"][:, 0:256], S["a2_b"]
                    S["gp"], S["gp_b"] = S["ub"][:, 4:260], S["ub_b"]
                    for k in ("ub", "uc", "rr", "ii", "a2"):
                        S[k] = S[k][:]
                    return S

                sets = [mkset(i) for i in range(NCH)]
                halo = [sb(f"halo{i}", [128, 4], F32, sB) for i in range(4)]
                halo_b = [kb.buf("halo") for _ in range(4)]
                cyc = [halo[i][:, 3:4] for i in range(4)]
                cyc_b = [kb.buf("cyc") for _ in range(4)]

                def lru_iter(cg, tc, S):
                    cwc = lambda k: pcol(PC_CW + cg * 4 + k)
                    ub, uc, ucb, rr, ii, a2, hs, ls, g2, gp = (S[k] for k in
                        ("ub", "uc", "ucb", "rr", "ii", "a2", "hs", "ls", "g2", "gp"))
                    ub_b, uc_b, ucb_b, rr_b, ii_b, a2_b, hs_b, ls_b, g2_b, gp_b = (S[k + "_b"] for k in
                        ("ub", "uc", "ucb", "rr", "ii", "a2", "hs", "ls", "g2", "gp"))
                    cy, cy_b = cyc[cg], cyc_b[cg]
                    t0 = tc * 512
                    pu, pub = next_bank(0, 8)
                    for c in range(8):
                        kb.op("pe", lambda e, c=c: e.matmul(pu[:], lhsT=wu_b[:, c, cg * 128:(cg + 1) * 128],
                                                            rhs=hT[:, c, t0:t0 + 512], start=(c == 0), stop=(c == 7)),
                              reads=[wu_bb, hT_b], writes=[pub], signal=(c == 7))
                    if tc == 0:
                        kb.op("pool", lambda e: e.memset(halo[cg][:, 0:3], 0.0), writes=[halo_b[cg]])
                    yield
                    cbc = pcol(PC_CB + cg)
                    kb.op("act", lambda e: e.activation(out=uc[:, 3:512], in_=pu[:, 0:509], func=AF.Identity,
                                                        bias=cbc, scale=cwc(0)),
                          reads=[pub, b_const], writes=[uc_b])
                    kb.op("act", lambda e: e.activation(out=uc[:, 0:3], in_=halo[cg][:, 0:3], func=AF.Identity,
                                                        bias=cbc, scale=cwc(0)),
                          reads=[halo_b[cg], b_const], writes=[uc_b])
                    yield
                    for k in range(1, 4):
                        kb.op("dve", lambda e, k=k: e.scalar_tensor_tensor(
                            out=uc[:, 3 - k:512], in0=pu[:, 0:509 + k], scalar=cwc(k), in1=uc[:, 3 - k:512],
                            op0=ALU.mult, op1=ALU.add),
                              reads=[pub, uc_b, b_const], writes=[uc_b])
                        if k < 3:
                            kb.op("dve", lambda e, k=k: e.scalar_tensor_tensor(
                                out=uc[:, 0:3 - k], in0=halo[cg][:, k:3], scalar=cwc(k), in1=uc[:, 0:3 - k],
                                op0=ALU.mult, op1=ALU.add),
                                  reads=[halo_b[cg], uc_b, b_const], writes=[uc_b])
                    kb.op("dve", lambda e: e.tensor_copy(out=halo[cg][:, 0:3], in_=pu[:, 509:512]),
                          reads=[pub, halo_b[cg]], writes=[halo_b[cg]])
                    yield
                    kb.op("act", lambda e: e.activation(out=ucb[:], in_=uc[:], func=AF.Copy), reads=[uc_b], writes=[ucb_b])
                    yield
                    pr, prb = next_bank(0, 8)
                    kb.op("pe", lambda e: e.matmul(pr[:], lhsT=wgate_b[:, cg * 2, :], rhs=ucb[:], start=True, stop=True),
                          reads=[ucb_b, b_misc], writes=[prb])
                    pg, pgb = next_bank(0, 8)
                    kb.op("pe", lambda e: e.matmul(pg[:], lhsT=wgate_b[:, cg * 2 + 1, :], rhs=ucb[:], start=True, stop=True),
                          reads=[ucb_b, b_misc], writes=[pgb])
                    yield
                    kb.op("act", lambda e: e.activation(out=rr[:], in_=pr[:], func=AF.Tanh,
                                                        bias=hbias[:, cg:cg + 1], scale=0.5),
                          reads=[prb, b_misc], writes=[rr_b])
                    kb.op("act", lambda e: e.activation(out=ii[:], in_=pg[:], func=AF.Tanh,
                                                        bias=hbias[:, 4 + cg:5 + cg], scale=0.5),
                          reads=[pgb, b_misc], writes=[ii_b])
                    kb.op("act", lambda e: e.activation(out=a2[:], in_=rr[:], func=AF.Exp, bias=c1[:, cg:cg + 1],
                                                        scale=c1[:, cg:cg + 1]),
                          reads=[rr_b, b_misc], writes=[a2_b])
                    kb.op("act", lambda e: e.activation(out=rr[:], in_=rr[:], func=AF.Exp, bias=c1x2[:, cg:cg + 1],
                                                        scale=c1x2[:, cg:cg + 1]),
                          reads=[rr_b, b_misc], writes=[rr_b])
                    yield
                    kb.op("pool", lambda e: e.tensor_scalar(out=ii[:], in0=ii[:], scalar1=1.0, scalar2=1.0,
                                                            op0=ALU.add, op1=ALU.mult),
                          reads=[ii_b], writes=[ii_b])
                    kb.op("pool", lambda e: e.tensor_tensor(out=ii[:], in0=ii[:], in1=uc[:], op=ALU.mult),
                          reads=[ii_b, uc_b], writes=[ii_b])
                    yield
                    kb.op("act", lambda e: e.activation(out=a2[:], in_=a2[:], func=AF.Sqrt, bias=qc[:, 0:1], scale=-0.25),
                          reads=[a2_b, b_misc], writes=[a2_b])
                    yield
                    kb.op("pool", lambda e: e.tensor_tensor(out=ii[:], in0=ii[:], in1=a2[:], op=ALU.mult),
                          reads=[ii_b, a2_b], writes=[ii_b])
                    init = 0.0 if tc == 0 else cy[:, 0:1]
                    kb.op("dve", lambda e: e.tensor_tensor_scan(out=hs[:], data0=rr[:], data1=ii[:], initial=init,
                                                                op0=ALU.mult, op1=ALU.add),
                          reads=[rr_b, ii_b] + ([cy_b] if tc else []), writes=[hs_b])
                    yield
                    kb.op("pool", lambda e: e.tensor_copy(out=cy[:], in_=hs[:, 511:512]), reads=[hs_b], writes=[cy_b])
                    hv = hs[:].rearrange("p (b two t) -> p b two t", two=2, t=128)
                    lv3 = ls[:].rearrange("p (b t) -> p b t", t=128)
                    kb.op("pool", lambda e: e.tensor_scalar(out=lv3, in0=hv[:, :, 0, :], scalar1=pcol(PC_SEL),
                                                            scalar2=1.0, op0=ALU.mult, op1=ALU.mult),
                          reads=[hs_b, b_const], writes=[ls_b])
                    yield
                    kb.op("dve", lambda e: e.scalar_tensor_tensor(out=lv3, in0=hv[:, :, 1, :], scalar=pcol(PC_SEL + 1),
                                                                  in1=lv3, op0=ALU.mult, op1=ALU.add),
                          reads=[hs_b, ls_b, b_const], writes=[ls_b])
                    o0 = tc * 256
                    pq, pqb = next_bank(0, 8)
                    for c in range(8):
                        kb.op("pe", lambda e, c=c: e.matmul(pq[:, 0:256], lhsT=wg_b[:, c, cg * 128:(cg + 1) * 128],
                                                            rhs=hTo[:, c, o0:o0 + 256], start=(c == 0), stop=(c == 7)),
                              reads=[wg_bb, hTo_b], writes=[pqb], signal=(c == 7))
                    yield
                    kb.op("act", lambda e: e.activation(out=g2[:], in_=pq[:, 0:256], func=AF.Square),
                          reads=[pqb], writes=[g2_b])
                    yield
                    kb.op("pool", lambda e: e.tensor_scalar(out=g2[:], in0=g2[:], scalar1=0.044715, scalar2=1.0,
                                                            op0=ALU.mult, op1=ALU.add),
                          reads=[g2_b], writes=[g2_b])
                    kb.op("dve", lambda e: e.tensor_tensor(out=g2[:], in0=g2[:], in1=pq[:, 0:256], op=ALU.mult),
                          reads=[g2_b, pqb], writes=[g2_b])
                    yield
                    kb.op("act", lambda e: e.activation(out=gp[:], in_=g2[:], func=AF.Tanh, scale=0.5 * GELU_C),
                          reads=[g2_b], writes=[gp_b])
                    yield
                    kb.op("dve", lambda e: e.scalar_tensor_tensor(out=gp[:], in0=gp[:], scalar=1.0, in1=pq[:, 0:256],
                                                                  op0=ALU.add, op1=ALU.mult),
                          reads=[gp_b, pqb], writes=[gp_b])
                    kb.op("dve", lambda e: e.scalar_tensor_tensor(out=lru_mixT[:, cg, o0:o0 + 256], in0=gp[:], scalar=0.5,
                                                                  in1=ls[:], op0=ALU.mult, op1=ALU.mult),
                          reads=[gp_b, ls_b], writes=[b_lru])
                    yield

                items = [(cg, tc) for tc in range(8) for cg in range(4)]
                for r0 in range(0, len(items), NCH):
                    live = [lru_iter(cg, tc, sets[(r0 + i) % NCH]) for i, (cg, tc) in enumerate(items[r0:r0 + NCH])]
                    while live:
                        nxt = []
                        for g in live:
                            try:
                                next(g)
                                nxt.append(g)
                            except StopIteration:
                                pass
                        live = nxt
                if debug:
                    kb.dma(dbg["d_lru"], lru_mixT[:], d_const, reads=[b_lru])
                kb.barrier()

            with ExitStack() as sD:
                KT = [sb("KT0", [128, 2, T], BF, sD)] * 2
                KT_b = [kb.buf("KT")] * 2
                QT = [sb("QT0", [128, 2, TO], BF, sD)] * 2
                QT_b = [kb.buf("QT")] * 2
                aug_d = [kb.dsem(f"aug{i}") for i in range(3)]
                Vh2 = [sb(f"Vh{i}", [128, 32, 129], BF, sD) for i in range(2)]
                V_b2 = [kb.buf("V") for _ in range(2)]
                for i in range(2):
                    kb.op("pool", lambda e, i=i: e.memset(Vh2[i][:, :, 128:129], 1.0), writes=[V_b2[i]])
                pt = [sb(f"pt{i}", [128, 512], BF, sD) for i in range(3)]
                pt_b = [kb.buf("pt") for _ in range(3)]
                Osb = [sb(f"Osb{i}", [128, 4, 129], F32, sD) for i in range(2)]
                Osb_b = [kb.buf("Osb") for _ in range(2)]
                att = sb("att", [128, 4, 128], F32, sD)
                att_b = kb.buf("att")
                junk = sb("junk", [128, 128], F32, sD)
                junk_b = kb.buf("junk")
                sm = sb("sm", [128, 16], F32, sD)
                sm_b = kb.buf("sm")
                attb = sb("attb", [128, 4, 128], BF, sD)
                attb_b = kb.buf("attb")
                PO = [P[3], P[4], P[5], P[6]]
                PO_b = [Pb[3], Pb[4], Pb[5], Pb[6]]

                srot = [0]
                deferred = []

                def v_tasks(h, nb):
                    s = h % 2
                    w, wb = wqk_b[s], wqk_bb[s]
                    for blk in range(32):
                        if nb == 1:
                            pb, pbb = P[srot[0] % 3], Pb[srot[0] % 3]
                            srot[0] += 1
                        else:
                            pb, pbb = next_bank(0, nb)
                        for c in range(8):
                            kb.op("pe", lambda e, c=c: e.matmul(pb[:, 0:128], lhsT=hT[:, c, blk * 128:(blk + 1) * 128],
                                                                rhs=w[:, c, 256:384], start=(c == 0), stop=(c == 7)),
                                  reads=[wb, hT_b], writes=[pbb], signal=(c == 7))
                        if nb != 1 and blk % 2:
                            kb.op("act", lambda e: e.activation(out=Vh2[s][:, blk, 0:128], in_=pb[:, 0:128], func=AF.Copy),
                                  reads=[pbb], writes=[V_b2[s]])
                        else:
                            kb.op("dve", lambda e: e.tensor_copy(out=Vh2[s][:, blk, 0:128], in_=pb[:, 0:128]),
                                  reads=[pbb], writes=[V_b2[s]])
                        yield

                def k_proj(h, nb):
                    s = h % 2
                    w, wb = wqk_b[s], wqk_bb[s]
                    for m in range(2):
                        kb.dma(KT[s][64:68, m, :], kaug[h], aug_d[s], writes=[KT_b[s]])
                    for tc in range(8):
                        pb, pbb = next_bank(0, nb)
                        tsl = slice(tc * 512, (tc + 1) * 512)
                        for c in range(8):
                            kb.op("pe", lambda e, c=c: e.matmul(pb[:], lhsT=w[:, c, 128:256], rhs=hT[:, c, tsl],
                                                                start=(c == 0), stop=(c == 7)),
                                  reads=[wb, hT_b], writes=[pbb], signal=(c == 7))
                        kb.op("act", lambda e: e.activation(out=KT[s][0:64, 0, tsl], in_=pb[0:64, :], func=AF.Copy),
                              reads=[pbb], writes=[KT_b[s]])
                        kb.op("dve", lambda e: e.tensor_copy(out=KT[s][0:64, 1, tsl], in_=pb[64:128, :]),
                              reads=[pbb], writes=[KT_b[s]])

                def q_proj(h, nb):
                    s = h % 2
                    w, wb = wqk_b[s], wqk_bb[s]
                    for m in range(2):
                        kb.dma(QT[0][64:68, m, :], qaug[h], aug_d[2], writes=[QT_b[0]])
                    for tc in range(4):
                        pb, pbb = next_bank(0, nb)
                        tsl = slice(tc * 512, (tc + 1) * 512)
                        for c in range(8):
                            kb.op("pe", lambda e, c=c: e.matmul(pb[:], lhsT=w[:, c, 0:128], rhs=hTo[:, c, tsl],
                                                                start=(c == 0), stop=(c == 7)),
                                  reads=[wb, hTo_b], writes=[pbb], signal=(c == 7))
                        kb.op("act", lambda e: e.activation(out=QT[0][0:64, 0, tsl], in_=pb[0:64, :], func=AF.Copy,
                                                            scale=0.125),
                              reads=[pbb], writes=[QT_b[0]])
                        kb.op("dve", lambda e: e.tensor_scalar(out=QT[0][0:64, 1, tsl], in0=pb[64:128, :], scalar1=0.125,
                                                               scalar2=None, op0=ALU.mult),
                              reads=[pbb], writes=[QT_b[0]])

                for _ in v_tasks(0, 3):
                    pass
                k_proj(0, 3)
                q_proj(0, 3)
                for h in range(NH):
                    hs_ = h % 2
                    Vh, V_b = Vh2[hs_], V_b2[hs_]
                    nxt_tasks = None
                    if h + 1 < NH:
                        load_wqk(h + 1)
                        nxt_tasks = v_tasks(h + 1, 1)
                    tiles = [(I, m, kbk) for I in range(4) for m in range(2) for kbk in range(8 * I + 8)]

                    def geom(I, kbk):
                        i_min = max(4 * I, kbk // 2)
                        return i_min, (i_min - 4 * I) * 128, (kbk // 2) >= 4 * I

                    def emit_qk(t):
                        I, m, kbk = tiles[t]
                        i_min, col0, masked = geom(I, kbk)
                        sbank[t] = srot[0] % 3
                        srot[0] += 1
                        sps, spsb = P[sbank[t]], Pb[sbank[t]]
                        kT = KT[hs_][0:68, m, kbk * 128:(kbk + 1) * 128]
                        q0 = 4 * I * 128
                        rd = [KT_b[hs_], QT_b[hs_]]
                        if masked:
                            kb.op("pe", lambda e: e.matmul(sps[:, col0:col0 + 128], lhsT=kT,
                                                           rhs=QT[hs_][0:68, m, q0 + col0:q0 + col0 + 128],
                                                           start=True, stop=False),
                                  reads=rd, writes=[spsb], signal=False)
                            last = (col0 + 128 == 512)
                            kb.op("pe", lambda e: e.matmul(sps[:, col0:col0 + 128], lhsT=ident, rhs=maskAB[kbk % 2],
                                                           start=False, stop=True),
                                  reads=[b_const, b_sel], writes=[spsb], signal=last)
                            if not last:
                                kb.op("pe", lambda e: e.matmul(sps[:, col0 + 128:512], lhsT=kT,
                                                               rhs=QT[hs_][0:68, m, q0 + col0 + 128:q0 + 512],
                                                               start=True, stop=True),
                                      reads=rd, writes=[spsb])
                        else:
                            kb.op("pe", lambda e: e.matmul(sps[:, col0:512], lhsT=kT,
                                                           rhs=QT[hs_][0:68, m, q0 + col0:q0 + 512],
                                                           start=True, stop=True),
                                  reads=rd, writes=[spsb])

                    def emit_exp(t):
                        I, m, kbk = tiles[t]
                        i_min, col0, masked = geom(I, kbk)
                        sps, spsb = P[sbank[t]], Pb[sbank[t]]
                        ptt, pttb = pt[t % 3], pt_b[t % 3]
                        kb.op("act", lambda e: e.activation(out=ptt[:, col0:512], in_=sps[:, col0:512], func=AF.Exp),
                              reads=[spsb], writes=[pttb])

                    def emit_pv(t):
                        I, m, kbk = tiles[t]
                        i_min, col0, masked = geom(I, kbk)
                        ptt, pttb = pt[t % 3], pt_b[t % 3]
                        for i in range(i_min, 4 * I + 4):
                            qi = i - 4 * I
                            kb.op("pe", lambda e, qi=qi, i=i: e.matmul(
                                PO[qi][:, 0:129], lhsT=ptt[:, qi * 128:(qi + 1) * 128],
                                rhs=Vh[:, kbk, :], start=(kbk == 0), stop=(kbk == 2 * i + 1)),
                                  reads=[pttb, V_b], writes=[PO_b[qi]], signal=(i == 4 * I + 3))

                    sbank = {}
                    LA = 2
                    for d in deferred:
                        d[1]()
                    del deferred[:]
                    for t in range(min(LA, len(tiles))):
                        emit_qk(t)
                    for t in range(len(tiles)):
                        I, m, kbk = tiles[t]
                        emit_exp(t)
                        if t + LA < len(tiles):
                            emit_qk(t + LA)
                        emit_pv(t)
                        for d in list(deferred):
                            d[0] -= 1
                            if d[0] <= 0:
                                deferred.remove(d)
                                d[1]()
                        if nxt_tasks is not None and t % 3 == 2:
                            next(nxt_tasks, None)
                        if kbk != 8 * I + 7:
                            continue
                        for qi in range(4):
                            kb.op("dve", lambda e, qi=qi: e.tensor_copy(out=Osb[m][:, qi, :], in_=PO[qi][:, 0:129]),
                                  reads=[PO_b[qi]], writes=[Osb_b[m]])
                        if m != 1:
                            continue
                        kb.op("dve", lambda e: e.reciprocal(out=sm[:, 0:4], in_=Osb[0][:, :, 128]),
                              reads=[Osb_b[0]], writes=[sm_b])
                        kb.op("dve", lambda e: e.reciprocal(out=sm[:, 4:8], in_=Osb[1][:, :, 128]),
                              reads=[Osb_b[1], sm_b], writes=[sm_b])
                        kb.op("dve", lambda e: e.tensor_scalar(out=sm[:, 4:8], in0=sm[:, 4:8], scalar1=nlam[:, 0:1],
                                                               scalar2=None, op0=ALU.mult),
                              reads=[sm_b, b_misc], writes=[sm_b])
                        for qi in range(4):
                            kb.op("dve", lambda e, qi=qi: e.tensor_scalar(out=att[:, qi, :], in0=Osb[0][:, qi, 0:128],
                                                                          scalar1=sm[:, qi:qi + 1], scalar2=None,
                                                                          op0=ALU.mult),
                                  reads=[Osb_b[0], sm_b], writes=[att_b])
                            kb.op("dve", lambda e, qi=qi: e.scalar_tensor_tensor(
                                out=att[:, qi, :], in0=Osb[1][:, qi, 0:128], scalar=sm[:, 4 + qi:5 + qi],
                                in1=att[:, qi, :], op0=ALU.mult, op1=ALU.add),
                                  reads=[Osb_b[1], sm_b, att_b], writes=[att_b])
                            kb.op("dve", lambda e, qi=qi: e.scalar_tensor_tensor(
                                out=junk[:], in0=att[:, qi, :], scalar=1.0, in1=att[:, qi, :], op0=ALU.mult, op1=ALU.mult,
                                accum_out=sm[:, 8 + qi:9 + qi]),
                                  reads=[att_b, sm_b], writes=[junk_b, sm_b])
                        kb.op("dve", lambda e: e.tensor_scalar(out=sm[:, 12:16], in0=sm[:, 8:12],
                                                               scalar1=1.0 / (128 * (1.0 - LAMBDA_INIT) ** 2),
                                                               scalar2=EPS / (1.0 - LAMBDA_INIT) ** 2,
                                                               op0=ALU.mult, op1=ALU.add),
                              reads=[sm_b], writes=[sm_b])
                        kb.op("pool", lambda e: e.tensor_tensor(out=sm[:, 12:16], in0=sm[:, 12:16], in1=mhalf[:],
                                                                op=ALU.pow),
                              reads=[sm_b, b_misc], writes=[sm_b])
                        for qi in range(4):
                            kb.op("dve", lambda e, qi=qi: e.scalar_tensor_tensor(
                                out=attb[:, qi, :], in0=att[:, qi, :], scalar=sm[:, 12 + qi:13 + qi],
                                in1=pv[:, PC_SUBGR:PC_SUBGR + 128], op0=ALU.mult, op1=ALU.mult),
                                  reads=[att_b, sm_b, b_const], writes=[attb_b])
                        def finish(h=h, I=I):
                            for qi in range(4):
                                kb.op("pe", lambda e, qi=qi: e.transpose(out=PTbf[:, qi * 128:(qi + 1) * 128],
                                                                         in_=attb[:, qi, :], identity=ident),
                                      reads=[attb_b, b_const], writes=[PTb], signal=(qi == 3))
                            kb.op("dve", lambda e: e.tensor_copy(out=attn_mixT[:, h, I * 512:(I + 1) * 512],
                                                                 in_=PTbf[:, 0:512]),
                                  reads=[PTb], writes=[b_attn])
                        deferred.append([10, finish])
                        if t == len(tiles) - 1 and h + 1 < NH:
                            for _ in nxt_tasks:
                                pass
                            k_proj(h + 1, 3)
                            q_proj(h + 1, 3)
                            if h + 1 == NH - 1:
                                prefetch_x1()
                        if debug and h == 0 and I == 0:
                            kb.dma(dbg["d_kt"], KT[0][:, :, 0:512], d_const, reads=[KT_b[0]])
                            kb.dma(dbg["d_qt"], QT[0][:, :, 0:512], d_const, reads=[QT_b[0]])
                            kb.dma(dbg["d_v"], Vh[:, 0:4, :], d_const, reads=[V_b])
                            kb.dma(dbg["d_sm"], sm[:], d_const, reads=[sm_b])
                            kb.dma(dbg["d_osb"][:, 0], Osb[0][:], d_const, reads=[Osb_b[0]])
                            kb.dma(dbg["d_osb"][:, 1], Osb[1][:], d_const, reads=[Osb_b[1]])
                for d in deferred:
                    d[1]()
                del deferred[:]
                if debug:
                    kb.dma(dbg["d_attn"], attn_mixT[:], d_const, reads=[b_attn])
                kb.barrier()

            s2 = s1
            sq = sb("sqE", [128, 8, 512], BF, s2)
            sq_b = kb.buf("sq")
            rt = [sb(f"rtE{i}", [128, 512], F32, s2) for i in range(2)]
            rt_b = [kb.buf("rt") for _ in range(2)]
            wup_g = [sb(f"wup_g{i}", [128, 8, 512], BF, s2) for i in range(2)]
            wup_gb = [kb.buf("wup") for _ in range(2)]
            wdn_g = [sb(f"wdn_g{i}", [128, 4, D], BF, s2) for i in range(2)]
            wdn_gb = [kb.buf("wdn") for _ in range(2)]
            wqk_f32 = wqk_b[0][:].rearrange("p c n -> p (c n)").bitcast(F32)
            sqm = [wqk_f32[:, i * 512:(i + 1) * 512] for i in range(2)]
            sqm_b = [kb.buf("sqm") for _ in range(2)]

            def load_group(fg):
                s = fg % 2
                for c in range(8):
                    load_cast(wup_g[s][:, c, :], wup_v[:, c, fg * 512:(fg + 1) * 512], 512, pcol(PC_GMLP + c),
                              dst_buf=wup_gb[s])
                for j in range(4):
                    for hf in range(2):
                        load_cast(wdn_g[s][:, j, hf * 512:(hf + 1) * 512], wdn_v[:, fg, j, hf * 512:(hf + 1) * 512],
                                  512, None, dst_buf=wdn_gb[s])

            with ExitStack() as sE:
                wo_b = sb("wo_b", [128, 8, D], BF, sE)
                wo_bb = [kb.buf("wo") for _ in range(2)]
                for hf in range(2):
                    for c in range(8):
                        load_cast(wo_b[:, c, hf * 512:(hf + 1) * 512], wout_v[:, c, hf * 512:(hf + 1) * 512], 512,
                                  None, dst_buf=wo_bb[hf], cast_en=("pool" if c % 2 else "dve"))
                load_group(0)

                def e_part1(tt):
                    tsl = slice(tt * 512, (tt + 1) * 512)
                    kb.op("act", lambda e: e.activation(out=sq[:], in_=x1T[:, :, tsl], func=AF.Square),
                          reads=[x1_b[tt]], writes=[sq_b])

                def e_part2(tt):
                    s = tt % 2
                    tsl = slice(tt * 512, (tt + 1) * 512)
                    pb, pbb = next_bank(4, 7)
                    for c in range(8):
                        kb.op("pe", lambda e, c=c: e.matmul(pb[:], lhsT=ones_bf[:], rhs=sq[:, c, :],
                                                            start=(c == 0), stop=(c == 7)),
                              reads=[sq_b, b_misc], writes=[pbb], signal=(c == 7))
                    kb.op("act", lambda e: e.activation(out=rt[s][:], in_=pb[:], func=AF.Ln, bias=epsc[:, 0:1],
                                                        scale=1.0 / D),
                          reads=[pbb, b_misc], writes=[rt_b[s]])
                    kb.op("act", lambda e: e.activation(out=rt[s][:], in_=rt[s][:], func=AF.Exp, scale=-0.5),
                          reads=[rt_b[s]], writes=[rt_b[s]])
                    for c in range(8):
                        en = "dve" if c % 2 == 0 else "pool"
                        kb.op(en, lambda e, c=c: e.tensor_tensor(out=hmT[:, c, tsl], in0=x1T[:, c, tsl], in1=rt[s][:],
                                                                 op=ALU.mult),
                              reads=[x1_b[tt], rt_b[s]], writes=[hm_b])

                for tt in range(4):
                    tsl = slice(tt * 512, (tt + 1) * 512)
                    for fc in range(8):
                        pb, pbb = next_bank(0, 4)
                        for kc in range(8):
                            rhs = attn_mixT[:, kc, tsl] if kc < 4 else lru_mixT[:, kc - 4, tsl]
                            kb.op("pe", lambda e, kc=kc, rhs=rhs: e.matmul(
                                pb[:], lhsT=wo_b[:, kc, fc * 128:(fc + 1) * 128], rhs=rhs,
                                start=(kc == 0), stop=(kc == 7)),
                                  reads=[wo_bb[fc // 4], b_attn, b_lru], writes=[pbb], signal=(kc == 7))
                        kb.op("dve", lambda e: e.tensor_tensor(out=x1T[:, fc, tsl], in0=pb[:], in1=x1T[:, fc, tsl],
                                                               op=ALU.add),
                              reads=[pbb, x1_b[tt]], writes=[x1_b[tt]])
                    if tt > 0:
                        e_part2(tt - 1)
                    e_part1(tt)
                e_part2(3)
                if debug:
                    kb.dma(dbg["d_x1"], x1T[:], d_const, reads=x1_b)

            with ExitStack() as sF:
                yb = [sb(f"yb{i}", [128, 8, 256], F32, sF) for i in range(2)]
                yb_b = [kb.buf("yb") for _ in range(2)]
                y_d = [kb.dsem(f"y{i}") for i in range(2)]

                def final_part1(tt):
                    tsl = slice(tt * 512, (tt + 1) * 512)
                    kb.op("act", lambda e: e.activation(out=sq[:], in_=x1T[:, :, tsl], func=AF.Square),
                          reads=[x1_b[tt]], writes=[sq_b])

                def final_part2(tt):
                    s = tt % 2
                    tsl = slice(tt * 512, (tt + 1) * 512)
                    pb, pbb = next_bank(0, 4)
                    for c in range(8):
                        kb.op("pe", lambda e, c=c: e.matmul(pb[:], lhsT=ones_bf[:], rhs=sq[:, c, :],
                                                            start=(c == 0), stop=(c == 7)),
                              reads=[sq_b, b_misc], writes=[pbb], signal=(c == 7))
                    kb.op("act", lambda e: e.activation(out=rt[s][:], in_=pb[:], func=AF.Ln, bias=epsc[:, 0:1],
                                                        scale=1.0 / D),
                          reads=[pbb, b_misc], writes=[rt_b[s]])
                    kb.op("act", lambda e: e.activation(out=rt[s][:], in_=rt[s][:], func=AF.Exp, scale=-0.5),
                          reads=[rt_b[s]], writes=[rt_b[s]])
                    for hf in range(2):
                        hsl = slice(tt * 512 + hf * 256, tt * 512 + (hf + 1) * 256)
                        for c in range(8):
                            kb.op("dve", lambda e, c=c: e.scalar_tensor_tensor(
                                out=yb[hf][:, c, :], in0=x1T[:, c, hsl], scalar=pcol(PC_GFIN + c),
                                in1=rt[s][:, hf * 256:(hf + 1) * 256], op0=ALU.mult, op1=ALU.mult),
                                  reads=[x1_b[tt], rt_b[s], b_const], writes=[yb_b[hf]])
                        kb.dma(yT_v[:, :, hsl], yb[hf][:], y_d[hf], reads=[yb_b[hf]])

                actT = [attn_mixT, lru_mixT]
                act_b = [b_attn, b_lru]
                ui = 0
                di = 0
                for fg in range(8):
                    s = fg % 2
                    if fg + 1 < 8:
                        load_group(fg + 1)
                    for tt in range(4):
                        tsl = slice(tt * 512, (tt + 1) * 512)
                        for j in range(4):
                            pb, pbb = P[ui % 4], Pb[ui % 4]
                            q, qb_ = sqm[ui % 2], sqm_b[ui % 2]
                            ui += 1
                            for c in range(8):
                                kb.op("pe", lambda e, c=c: e.matmul(pb[:], lhsT=wup_g[s][:, c, j * 128:(j + 1) * 128],
                                                                    rhs=hmT[:, c, tsl], start=(c == 0), stop=(c == 7)),
                                      reads=[wup_gb[s], hm_b], writes=[pbb], signal=(c == 7))
                            kb.op("act", lambda e: e.activation(out=q[:], in_=pb[:], func=AF.Square),
                                  reads=[pbb], writes=[qb_])
                            kb.op("dve", lambda e: e.scalar_tensor_tensor(out=actT[s][:, j, tsl], in0=pb[:], scalar=0.0,
                                                                          in1=q[:], op0=ALU.is_gt, op1=ALU.mult),
                                  reads=[pbb, qb_], writes=[act_b[s]])
                    for tt in range(4):
                        tsl = slice(tt * 512, (tt + 1) * 512)
                        for fc in range(8):
                            pb, pbb = P[4 + di % 3], Pb[4 + di % 3]
                            di += 1
                            for j in range(4):
                                kb.op("pe", lambda e, j=j: e.matmul(pb[:], lhsT=wdn_g[s][:, j, fc * 128:(fc + 1) * 128],
                                                                    rhs=actT[s][:, j, tsl], start=(j == 0), stop=(j == 3)),
                                      reads=[wdn_gb[s], act_b[s]], writes=[pbb], signal=(j == 3))
                            kb.op("dve", lambda e: e.tensor_tensor(out=x1T[:, fc, tsl], in0=pb[:], in1=x1T[:, fc, tsl],
                                                                   op=ALU.add),
                                  reads=[pbb, x1_b[tt]], writes=[x1_b[tt]])
                            if fg == 7 and fc == 2 and tt > 0:
                                final_part2(tt - 1)
                        if fg == 7:
                            final_part1(tt)
                if True:
                    final_part2(3)
                if debug:
                    kb.dma(dbg["d_x2"], x1T[:], d_const, reads=x1_b)

            kb.barrier()
            spE = kb.E["sp"]
            for d in y_d:
                spE.eng.wait_ge(d.sem, d.count)
    return nc


_NC_CACHE = {}


def _slopes():
    return np.array([2.0 ** (-8.0 * (i + 1) / NH) for i in range(NH)], dtype=np.float32)


def _const_tables(j):
    bf = ml_dtypes.bfloat16
    ident = np.eye(128, dtype=np.float32)
    kk = np.arange(128)[:, None]
    qq = np.arange(128)[None, :]
    tri = np.where(kk > qq, NEG, 0.0).astype(np.float32)
    full = np.full((128, 128), NEG, np.float32)
    zero = np.zeros((128, 128), np.float32)
    mA, mB = (tri, full) if j == 0 else (zero, tri)
    s0 = ident * (1.0 if j == 0 else 0.0)
    s1 = ident * (0.0 if j == 0 else 1.0)
    cbf = np.concatenate([ident, mA, mB, s0, s1], axis=1).astype(bf)
    sl = _slopes()
    kpos = np.arange(T)
    kaug = np.zeros((NH, 4, T), np.float32)
    qaug = np.zeros((NH, 4, TO), np.float32)
    own = (np.arange(TO) // 128) * 256 + j * 128 + (np.arange(TO) % 128)
    for h in range(NH):
        kaug[h, 0] = sl[h] * (kpos % 128)
        kaug[h, 1] = sl[h] * (kpos - kpos % 128)
        kaug[h, 2] = 1.0
        kaug[h, 3] = 1.0
        qaug[h, 0] = 1.0
        qaug[h, 1] = 1.0
        qaug[h, 2] = -sl[h] * (own % 128)
        qaug[h, 3] = -sl[h] * (own - own % 128)
    return cbf, kaug.astype(bf), qaug.astype(bf), own


def _chunked(v):
    return np.ascontiguousarray(np.asarray(v, np.float32).reshape(8, 128).T)


def _prep_shared(inp):
    f = lambda k: np.asarray(inp[k], np.float32)
    w_in = f("w_in")[0]
    wq, wk, wvv = w_in[:, 0:512], w_in[:, 512:1024], w_in[:, 1024:1536]
    wqkv = np.stack([np.concatenate([wq[:, h * 128:(h + 1) * 128], wk[:, h * 128:(h + 1) * 128],
                                     wvv[:, h * 128:(h + 1) * 128]], axis=1) for h in range(NH)], 0)
    sh = {
        "wqkv": np.ascontiguousarray(wqkv),
        "wu": np.ascontiguousarray(w_in[:, 1536:2048]),
        "wg": np.ascontiguousarray(w_in[:, 2048:2560]),
        "wout": np.ascontiguousarray(f("w_out")[0]),
        "wup": np.ascontiguousarray(f("w_up")[0]),
        "wdn": np.ascontiguousarray(f("w_down")[0]),
    }
    pv = np.zeros((128, NPC), np.float32)
    pv[:, PC_GMIX:PC_GMIX + 8] = _chunked(f("norm_mix_g")[0])
    pv[:, PC_GMLP:PC_GMLP + 8] = _chunked(f("norm_mlp_g")[0])
    pv[:, PC_GFIN:PC_GFIN + 8] = _chunked(f("final_g"))
    cw = f("conv_w")[0]
    for cg in range(4):
        for k in range(4):
            pv[:, PC_CW + cg * 4 + k] = cw[k, cg * 128:(cg + 1) * 128]
        pv[:, PC_CB + cg] = f("conv_b")[0][cg * 128:(cg + 1) * 128]
        pv[:, PC_BRG + cg] = f("b_rg")[0].reshape(512)[cg * 128:(cg + 1) * 128]
        pv[:, PC_BIG + cg] = f("b_ig")[0].reshape(512)[cg * 128:(cg + 1) * 128]
        pv[:, PC_L + cg] = f("lru_L")[0][cg * 128:(cg + 1) * 128]
    pv[:, PC_SUBG] = f("subln_g")[0]
    pv[:, PC_SUBGR:PC_SUBGR + 128] = f("subln_g")[0][None, :]
    lam = np.stack([f("lambda_q1")[0], f("lambda_k1")[0], f("lambda_q2")[0], f("lambda_k2")[0]], 0).reshape(256)
    pv[:, PC_LAM:PC_LAM + 256] = lam[None, :]
    wgate = np.zeros((128, 8, 128), np.float32)
    wrg, wig = f("w_rg")[0], f("w_ig")[0]
    for cg in range(4):
        for nl in range(2):
            n = cg * 2 + nl
            wgate[nl * 64:(nl + 1) * 64, cg * 2 + 0, nl * 64:(nl + 1) * 64] = wrg[n]
            wgate[nl * 64:(nl + 1) * 64, cg * 2 + 1, nl * 64:(nl + 1) * 64] = wig[n]
    sh["wgate"] = wgate
    return sh, pv


def _in_maps(inp):
    x = np.asarray(inp["x"], np.float32)
    sh, pv = _prep_shared(inp)
    maps, owns = [], []
    for core in range(8):
        b, j = core // 2, core % 2
        cbf, kaug, qaug, own = _const_tables(j)
        p = pv.copy()
        p[:, PC_SEL] = 1.0 if j == 0 else 0.0
        p[:, PC_SEL + 1] = 0.0 if j == 0 else 1.0
        xT = np.ascontiguousarray(x[b].T)
        m = dict(sh)
        m.update({"xf": xT, "xo": np.ascontiguousarray(xT[:, own]), "pvec": p, "cbf": cbf, "kaug": kaug,
                  "qaug": qaug})
        maps.append(m)
        owns.append(own)
    return maps, owns


def kernel(**inputs):
    if "nc" not in _NC_CACHE:
        _NC_CACHE["nc"] = build_program()
    nc = _NC_CACHE["nc"]
    maps, owns = _in_maps(inputs)
    res = run_bass_kernel_spmd(nc, maps, core_ids=list(range(8)))
    B, S = inputs["x"].shape[0], inputs["x"].shape[1]
    out = np.empty((B, S, D), np.float32)
    for core in range(8):
        b = core // 2
        out[b, owns[core], :] = np.asarray(res.results[core]["yT"], np.float32).T
    return out
```

```python
import math
from contextlib import ExitStack

import numpy as np
import ml_dtypes
import concourse.bass as bass
import concourse.mybir as mybir
from concourse.bass_utils import run_bass_kernel_spmd

F32 = mybir.dt.float32
BF = mybir.dt.bfloat16
AF = mybir.ActivationFunctionType
ALU = mybir.AluOpType

D = 1024
T = 4096
TO = 2048
NH = 4
EPS = 1e-6
LAMBDA_INIT = 0.8 - 0.6 * math.exp(0.0)
GELU_C = 2.0 * math.sqrt(2.0 / math.pi)
NEG = -30000.0

PC_GMIX, PC_GMLP, PC_GFIN = 0, 8, 16
PC_CW, PC_CB, PC_BRG, PC_BIG, PC_L = 24, 40, 44, 48, 52
PC_SUBG, PC_SEL, PC_LAM = 56, 57, 59
PC_SUBGR = 59 + 256
NPC = 59 + 256 + 128


class Buf:
    __slots__ = ("name", "w", "r", "pr")

    def __init__(self, name):
        self.name = name
        self.w = {}
        self.r = {}
        self.pr = {}


class Eng:
    def __init__(self, name, eng, sem):
        self.name, self.eng, self.sem = name, eng, sem
        self.count = 0
        self.seen = {}


class DSem:
    def __init__(self, name, sem):
        self.name, self.sem = name, sem
        self.count = 0


class KB:
    def __init__(self, nc, es):
        self.nc = nc
        self.es = es
        self.E = {}
        for name, eng in (("pe", nc.tensor), ("act", nc.scalar), ("dve", nc.vector),
                          ("pool", nc.gpsimd), ("sp", nc.sync)):
            self.E[name] = Eng(name, eng, es.enter_context(nc.semaphore("s_" + name)))
        self.dsems = []
        self.nbuf = 0

    def dsem(self, name):
        d = DSem(name, self.es.enter_context(self.nc.semaphore("d_" + name)))
        self.dsems.append(d)
        return d

    def buf(self, name="b"):
        self.nbuf += 1
        return Buf(f"{name}{self.nbuf}")

    def _wait(self, E, st, raw):
        src, val = st
        if src is E and (E.name in ("pe", "sp") or not raw):
            return
        if isinstance(src, DSem):
            val = src.count
        key = id(src)
        if E.seen.get(key, 0) >= val:
            return
        E.eng.wait_ge(src.sem, val)
        E.seen[key] = val

    def _deps(self, E, reads, writes):
        for b in reads:
            for st in b.w.values():
                self._wait(E, st, True)
        for b in writes:
            for st in b.r.values():
                self._wait(E, st, False)
            for st in b.pr.values():
                self._wait(E, st, False)

    def _post(self, src, st, reads, writes):
        for b in reads:
            b.r[id(src)] = st
        for b in writes:
            if b.r:
                b.pr = b.r
                b.w = {}
                b.r = {}
            b.w[id(src)] = st

    def op(self, en, fn, reads=(), writes=(), signal=True):
        E = self.E[en]
        self._deps(E, reads, writes)
        ins = fn(E.eng)
        if signal:
            ins.then_inc(E.sem, 1)
            E.count += 1
            st = (E, E.count)
        else:
            st = (E, E.count + 1)
        self._post(E, st, reads, writes)
        return ins

    def dma(self, out, in_, dsem, reads=(), writes=(), en="sp"):
        E = self.E[en]
        self._deps(E, reads, writes)
        ins = E.eng.dma_start(out=out, in_=in_)
        ins.then_inc(dsem.sem, 16)
        dsem.count += 16
        self._post(dsem, (dsem, dsem.count), reads, writes)

    def barrier(self):
        for E in self.E.values():
            for F in self.E.values():
                if F is not E and F.count > 0:
                    self._wait(E, (F, F.count), True)
            for d in self.dsems:
                if d.count > 0:
                    self._wait(E, (d, d.count), True)


def build_program(debug=False):
    nc = bass.Bass("TRN2", target_bir_lowering=False)
    dbg = {}
    if debug:
        for nm, shp, dt in (("d_hT", [128, 8, 512], BF), ("d_hTo", [128, 8, 512], BF), ("d_lru", [128, 4, TO], BF),
                            ("d_attn", [128, 4, TO], BF), ("d_x1", [128, 8, TO], F32), ("d_x2", [128, 8, TO], F32),
                            ("d_kt", [128, 2, 512], BF), ("d_qt", [128, 2, 512], BF), ("d_v", [128, 4, 129], BF),
                            ("d_sm", [128, 16], F32), ("d_osb", [128, 2, 4, 129], F32)):
            dbg[nm] = nc.dram_tensor(nm, shp, dt, kind="ExternalOutput").ap()

    def din(name, shape, dt=F32):
        return nc.dram_tensor(name, list(shape), dt, kind="ExternalInput").ap()

    xf = din("xf", [D, T])
    xo = din("xo", [D, TO])
    wqkv = din("wqkv", [NH, D, 384])
    wu = din("wu", [D, 512])
    wg = din("wg", [D, 512])
    wout = din("wout", [D, D])
    wup = din("wup", [D, 4 * D])
    wdn = din("wdn", [4 * D, D])
    pvec = din("pvec", [128, NPC])
    wgate = din("wgate", [128, 8, 128])
    cbf = din("cbf", [128, 640], BF)
    kaug = din("kaug", [NH, 4, T], BF)
    qaug = din("qaug", [NH, 4, TO], BF)
    yT = nc.dram_tensor("yT", [D, TO], F32, kind="ExternalOutput").ap()

    xf_v = xf.rearrange("(c p) t -> p c t", p=128)
    xo_v = xo.rearrange("(c p) t -> p c t", p=128)
    yT_v = yT.rearrange("(c p) t -> p c t", p=128)
    wu_v = wu.rearrange("(c p) n -> p c n", p=128)
    wg_v = wg.rearrange("(c p) n -> p c n", p=128)
    wout_v = wout.rearrange("(c p) n -> p c n", p=128)
    wup_v = wup.rearrange("(c p) n -> p c n", p=128)
    wdn_v = wdn.rearrange("(g j p) n -> p g j n", j=4, p=128)

    with ExitStack() as es:
        kb = KB(nc, es)

        def sb(name, shape, dt, stack=es):
            return stack.enter_context(nc.sbuf_tensor(name, list(shape), dt))

        def ps(name, shape, dt=F32, stack=es):
            return stack.enter_context(nc.psum_tensor(name, list(shape), dt))

        pv = sb("pv", [128, NPC], F32)
        cb_sb = sb("cb_sb", [128, 384], BF)
        ones_bf = sb("ones_bf", [128, 128], BF)
        epsc = sb("epsc", [128, 1], F32)
        onec = sb("onec", [128, 1], F32)
        c1 = sb("c1", [128, 4], F32)
        c1x2 = sb("c1x2", [128, 4], F32)
        hbias = sb("hbias", [128, 8], F32)
        qc = sb("qc", [128, 1], F32)
        mhalf = sb("mhalf", [128, 4], F32)
        nlam = sb("nlam", [128, 1], F32)
        lsum = sb("lsum", [128, 4], F32)
        wgate_b = sb("wgate_b", [128, 8, 128], BF)
        attn_mixT = sb("attn_mixT", [128, 4, TO], BF)
        lru_mixT = sb("lru_mixT", [128, 4, TO], BF)
        wst = [sb(f"wst{i}", [128, 512], F32) for i in range(4)]
        wst_b = [kb.buf("wst") for _ in range(4)]
        wst_d = [kb.dsem(f"wst{i}") for i in range(4)]
        wst_i = [0]

        b_lru = kb.buf("lru")
        b_attn = kb.buf("attn")
        ident = cb_sb[:, 0:128]
        maskAB = [cb_sb[:, 128:256], cb_sb[:, 256:384]]

        P = [ps(f"P{i}", [128, 512]) for i in range(8)]
        Pb = [kb.buf("P") for _ in range(8)]
        PT, PTb = P[7], Pb[7]
        PTbf = P[7][:].bitcast(BF)
        ident_f = sb("ident_f", [128, 128], F32)

        s0 = ExitStack()
        with s0:
            ltmp = sb("ltmp", [128, 4, 64], F32, s0)
            wgate_f = sb("wgate_f", [128, 8, 128], F32, s0)
            b_const = kb.buf("const")
            d_const = kb.dsem("const")
            kb.dma(pv[:], pvec[:, :], d_const, writes=[b_const])
            b_sel = kb.buf("sel")
            d_sel = kb.dsem("sel")
            kb.dma(cb_sb[:, 0:128], cbf[:, 0:128], d_const, writes=[kb.buf()])
            kb.dma(cb_sb[:, 128:384], cbf[:, 384:640], d_sel, writes=[b_sel])
            kb.dma(wgate_f[:], wgate[:, :, :], d_const, writes=[kb.buf()])
            b_const.w = {id(d_const): (d_const, d_const.count)}

            def pcol(c, n=1):
                return pv[:, c:c + n]

            b_misc = kb.buf("misc")
            kb.op("pool", lambda e: e.memset(ones_bf[:], 1.0), writes=[b_misc])
            kb.op("pool", lambda e: e.tensor_copy(out=ident_f[:], in_=cb_sb[:, 0:128]), reads=[b_const], writes=[b_misc])
            kb.op("pool", lambda e: e.memset(epsc[:], EPS), writes=[b_misc])
            kb.op("pool", lambda e: e.memset(onec[:], 1.0), writes=[b_misc])
            kb.op("pool", lambda e: e.tensor_copy(out=wgate_b[:], in_=wgate_f[:]), reads=[b_const], writes=[b_misc])
            kb.op("act", lambda e: e.activation(out=c1[:], in_=pcol(PC_L, 4), func=AF.Exp, scale=-1.0),
                  reads=[b_const], writes=[b_misc])
            kb.op("act", lambda e: e.activation(out=c1[:], in_=c1[:], func=AF.Ln, bias=onec[:, 0:1], scale=1.0),
                  reads=[b_misc], writes=[b_misc])
            kb.op("dve", lambda e: e.tensor_scalar(out=c1x2[:], in0=c1[:], scalar1=-4.0, scalar2=None, op0=ALU.mult),
                  reads=[b_misc], writes=[b_misc])
            kb.op("dve", lambda e: e.tensor_scalar(out=c1[:], in0=c1[:], scalar1=-8.0, scalar2=None, op0=ALU.mult),
                  reads=[b_misc], writes=[b_misc])
            kb.op("dve", lambda e: e.tensor_scalar(out=hbias[:], in0=pcol(PC_BRG, 8), scalar1=0.5, scalar2=None,
                                                   op0=ALU.mult),
                  reads=[b_const], writes=[b_misc])
            kb.op("pool", lambda e: e.memset(qc[:], 0.25), writes=[b_misc])
            kb.op("pool", lambda e: e.memset(mhalf[:], -0.5), writes=[b_misc])
            lv = pv[:, PC_LAM:PC_LAM + 256].rearrange("p (a d) -> p a d", d=64)
            kb.op("dve", lambda e: e.tensor_tensor(out=ltmp[:, 0, :], in0=lv[:, 0, :], in1=lv[:, 1, :], op=ALU.mult),
                  reads=[b_const], writes=[b_misc])
            kb.op("dve", lambda e: e.tensor_tensor(out=ltmp[:, 1, :], in0=lv[:, 2, :], in1=lv[:, 3, :], op=ALU.mult),
                  reads=[b_const], writes=[b_misc])
            kb.op("dve", lambda e: e.reduce_sum(out=lsum[:, 0:2], in_=ltmp[:, 0:2, :], axis=mybir.AxisListType.X),
                  reads=[b_misc], writes=[b_misc])
            kb.op("act", lambda e: e.activation(out=lsum[:, 2:4], in_=lsum[:, 0:2], func=AF.Exp),
                  reads=[b_misc], writes=[b_misc])
            kb.op("dve", lambda e: e.scalar_tensor_tensor(out=nlam[:], in0=lsum[:, 3:4], scalar=-LAMBDA_INIT,
                                                          in1=lsum[:, 2:3], op0=ALU.add, op1=ALU.subtract),
                  reads=[b_misc], writes=[b_misc])


        def load_cast(dst, src, n, scale_col=None, cast_en="pool", dst_buf=None):
            i = wst_i[0] % 4
            wst_i[0] += 1
            st, stb, std = wst[i], wst_b[i], wst_d[i]
            kb.dma(st[:, 0:n], src, std, writes=[stb])
            wr = [dst_buf] if dst_buf is not None else []
            if scale_col is None:
                if cast_en == "pool":
                    kb.op("pool", lambda e: e.tensor_scalar(out=dst, in0=st[:, 0:n], scalar1=1.0, scalar2=1.0,
                                                            op0=ALU.mult, op1=ALU.mult),
                          reads=[stb, b_const], writes=wr)
                else:
                    kb.op(cast_en, lambda e: e.tensor_copy(out=dst, in_=st[:, 0:n]), reads=[stb, b_const], writes=wr)
            else:
                kb.op(cast_en, lambda e: e.tensor_scalar(out=dst, in0=st[:, 0:n], scalar1=scale_col, scalar2=1.0,
                                                         op0=ALU.mult, op1=ALU.mult),
                      reads=[stb, b_const], writes=wr)

        pi = [0]

        def next_bank(lo=0, hi=4):
            i = lo + pi[0] % (hi - lo)
            pi[0] += 1
            return P[i], Pb[i]

        def rms_tile(stack_bufs, xt, xt_b, dstT, dst_b, t0, n=512):
            sq, sq_b, rt, rt_b = stack_bufs
            kb.op("act", lambda e: e.activation(out=sq[:, :, 0:n], in_=xt[:, :, 0:n], func=AF.Square),
                  reads=[xt_b], writes=[sq_b])
            pb, pbb = next_bank()
            for c in range(8):
                kb.op("pe", lambda e, c=c: e.matmul(pb[:, 0:n], lhsT=ones_bf[:], rhs=sq[:, c, 0:n],
                                                    start=(c == 0), stop=(c == 7)),
                      reads=[sq_b, b_misc], writes=[pbb], signal=(c == 7))
            if dstT is None:
                kb.op("act", lambda e: e.activation(out=rt[:, 0:n], in_=pb[:, 0:n], func=AF.Sqrt,
                                                    bias=epsc[:, 0:1], scale=1.0 / D),
                      reads=[pbb, b_misc], writes=[rt_b])
                kb.op("dve", lambda e: e.reciprocal(out=rt[:, 0:n], in_=rt[:, 0:n]), reads=[rt_b], writes=[rt_b])
            else:
                kb.op("act", lambda e: e.activation(out=rt[:, 0:n], in_=pb[:, 0:n], func=AF.Ln,
                                                    bias=epsc[:, 0:1], scale=1.0 / D),
                      reads=[pbb, b_misc], writes=[rt_b])
                kb.op("act", lambda e: e.activation(out=pb[:, 0:n], in_=rt[:, 0:n], func=AF.Exp, scale=-0.5),
                      reads=[rt_b], writes=[pbb])
                for c in range(8):
                    kb.op("dve", lambda e, c=c: e.tensor_tensor(out=dstT[:, c, t0:t0 + n], in0=xt[:, c, 0:n],
                                                                in1=pb[:, 0:n], op=ALU.mult),
                          reads=[xt_b, pbb, b_misc], writes=[dst_b])

        with ExitStack() as s1:
            hT = sb("hT", [128, 8, T], BF, s1)
            hTo = sb("hTo", [128, 8, TO], BF, s1)
            hT_b = kb.buf("hT")
            hTo_b = kb.buf("hTo")
            wqk_b = [sb("wqk_b0", [128, 8, 384], BF, s1)] * 2
            wqk_bb = [kb.buf("wqk")] * 2

            def load_wqk(h):
                for c in range(8):
                    load_cast(wqk_b[h % 2][:, c, :], wqkv[h].rearrange("(c p) n -> p c n", p=128)[:, c, :], 384,
                              pcol(PC_GMIX + c), dst_buf=wqk_bb[h % 2])
            x1T = hT[:].bitcast(F32)
            hmT = hTo
            hm_b = hTo_b
            x1_b = [kb.buf("x1") for _ in range(4)]
            x1_d = [kb.dsem(f"x1_{i}") for i in range(4)]

            def prefetch_x1():
                for tt in range(4):
                    kb.dma(x1T[:, :, tt * 512:(tt + 1) * 512], xo_v[:, :, tt * 512:(tt + 1) * 512], x1_d[tt],
                           writes=[x1_b[tt], hT_b])

            with ExitStack() as sA:
                NXS = 3
                xst = [sb(f"xst{i}", [128, 8, 512], F32, sA) for i in range(NXS)]
                xst_b = [kb.buf("xst") for _ in range(NXS)]
                xst_d = [kb.dsem(f"xst{i}") for i in range(NXS)]
                sq = sb("sqA", [128, 8, 512], BF, sA)
                sq_b = kb.buf("sq")
                rt = [sb(f"rtA{i}", [128, 512], F32, sA) for i in range(2)]
                rt_b = [kb.buf("rt") for _ in range(2)]
                selm = [cb_sb[:, 128:256], cb_sb[:, 256:384]]
                hT_t = [kb.buf("hTt") for _ in range(8)]

                def select_own(it):
                    for b in range(2):
                        ev = it * 512 + b * 256
                        ob = 2 * it + b
                        for ch in range(2):
                            pb, pbb = next_bank(4, 8)
                            pv3 = pb[:].rearrange("p (c t) -> p c t", t=128)
                            kb.op("pe", lambda e: e.matmul(pv3, lhsT=selm[0], rhs=hT[:, 4 * ch:4 * ch + 4, ev:ev + 128],
                                                           start=True, stop=False),
                                  reads=[hT_t[it], b_sel], writes=[pbb], signal=False)
                            kb.op("pe", lambda e: e.matmul(pv3, lhsT=selm[1],
                                                           rhs=hT[:, 4 * ch:4 * ch + 4, ev + 128:ev + 256],
                                                           start=False, stop=True),
                                  reads=[hT_t[it], b_sel], writes=[pbb])
                            kb.op("act", lambda e: e.activation(out=hTo[:, 4 * ch:4 * ch + 4, ob * 128:(ob + 1) * 128],
                                                                in_=pv3, func=AF.Copy),
                                  reads=[pbb], writes=[hTo_b])

                for it in range(8):
                    s = it % NXS
                    t0 = it * 512
                    kb.dma(xst[s][:], xf_v[:, :, t0:t0 + 512], xst_d[s], writes=[xst_b[s]])
                    rms_tile((sq, sq_b, rt[it % 2], rt_b[it % 2]), xst[s][:], xst_b[s], hT, hT_t[it], t0)
                    if it >= 2:
                        select_own(it - 2)
                select_own(6)
                select_own(7)
                kb.dma(cb_sb[:, 128:384], cbf[:, 128:384], d_sel, writes=[b_sel])
                if debug:
                    kb.dma(dbg["d_hT"], hT[:, :, 0:512], d_const, reads=[hT_b])
                    kb.dma(dbg["d_hTo"], hTo[:, :, 0:512], d_const, reads=[hTo_b])
                kb.barrier()

            with ExitStack() as sB:
                wu_b = sb("wu_b", [128, 8, 512], BF, sB)
                wg_b = sb("wg_b", [128, 8, 512], BF, sB)
                wu_bb, wg_bb = kb.buf("wu"), kb.buf("wg")
                for c in range(8):
                    load_cast(wu_b[:, c, :], wu_v[:, c, :], 512, pcol(PC_GMIX + c), dst_buf=wu_bb,
                              cast_en=("pool" if c % 2 else "dve"))
                for c in range(8):
                    load_cast(wg_b[:, c, :], wg_v[:, c, :], 512, pcol(PC_GMIX + c), dst_buf=wg_bb,
                              cast_en=("pool" if c % 2 else "dve"))
                load_wqk(0)
                NCH = 4

                def mkset(i):
                    S = {}
                    for nm, shp, dt in (("ub", [128, 516], F32), ("uc", [128, 512], F32), ("ucbf", [128, 256], F32),
                                        ("rr", [128, 512], F32), ("ii", [128, 512], F32), ("a2", [128, 512], F32)):
                        S[nm] = sb(f"{nm}_{i}", shp, dt, sB)
                        S[nm + "_b"] = kb.buf(nm)
                    S["ucb"], S["ucb_b"] = S["ucbf"][:].bitcast(BF), S["ucbf_b"]
                    S["g2"], S["g2_b"] = S["ucbf"][:], S["ucbf_b"]
                    S["hs"], S["hs_b"] = S["uc"][:], S["uc_b"]
                    S["ls"], S["ls_b"] = S["a2"][:, 0:256], S["a2_b"]
                    S["gp"], S["gp_b"] = S["ub"][:, 4:260], S["ub_b"]
                    for k in ("ub", "uc", "rr", "ii", "a2"):
                        S[k] = S[k][:]
                    return S

                sets = [mkset(i) for i in range(NCH)]
                halo = [sb(f"halo{i}", [128, 4], F32, sB) for i in range(4)]
                halo_b = [kb.buf("halo") for _ in range(4)]
                cyc = [halo[i][:, 3:4] for i in range(4)]
                cyc_b = [kb.buf("cyc") for _ in range(4)]

                def lru_iter(cg, tc, S):
                    cwc = lambda k: pcol(PC_CW + cg * 4 + k)
                    ub, uc, ucb, rr, ii, a2, hs, ls, g2, gp = (S[k] for k in
                        ("ub", "uc", "ucb", "rr", "ii", "a2", "hs", "ls", "g2", "gp"))
                    ub_b, uc_b, ucb_b, rr_b, ii_b, a2_b, hs_b, ls_b, g2_b, gp_b = (S[k + "_b"] for k in
                        ("ub", "uc", "ucb", "rr", "ii", "a2", "hs", "ls", "g2", "gp"))
                    cy, cy_b = cyc[cg], cyc_b[cg]
                    t0 = tc * 512
                    pu, pub = next_bank(0, 8)
                    for c in range(8):
                        kb.op("pe", lambda e, c=c: e.matmul(pu[:], lhsT=wu_b[:, c, cg * 128:(cg + 1) * 128],
                                                            rhs=hT[:, c, t0:t0 + 512], start=(c == 0), stop=(c == 7)),
                              reads=[wu_bb, hT_b], writes=[pub], signal=(c == 7))
                    if tc == 0:
                        kb.op("pool", lambda e: e.memset(halo[cg][:, 0:3], 0.0), writes=[halo_b[cg]])
                    yield
                    cbc = pcol(PC_CB + cg)
                    kb.op("act", lambda e: e.activation(out=uc[:, 3:512], in_=pu[:, 0:509], func=AF.Identity,
                                                        bias=cbc, scale=cwc(0)),
                          reads=[pub, b_const], writes=[uc_b])
                    kb.op("act", lambda e: e.activation(out=uc[:, 0:3], in_=halo[cg][:, 0:3], func=AF.Identity,
                                                        bias=cbc, scale=cwc(0)),
                          reads=[halo_b[cg], b_const], writes=[uc_b])
                    yield
                    for k in range(1, 4):
                        kb.op("dve", lambda e, k=k: e.scalar_tensor_tensor(
                            out=uc[:, 3 - k:512], in0=pu[:, 0:509 + k], scalar=cwc(k), in1=uc[:, 3 - k:512],
                            op0=ALU.mult, op1=ALU.add),
                              reads=[pub, uc_b, b_const], writes=[uc_b])
                        if k < 3:
                            kb.op("dve", lambda e, k=k: e.scalar_tensor_tensor(
                                out=uc[:, 0:3 - k], in0=halo[cg][:, k:3], scalar=cwc(k), in1=uc[:, 0:3 - k],
                                op0=ALU.mult, op1=ALU.add),
                                  reads=[halo_b[cg], uc_b, b_const], writes=[uc_b])
                    kb.op("dve", lambda e: e.tensor_copy(out=halo[cg][:, 0:3], in_=pu[:, 509:512]),
                          reads=[pub, halo_b[cg]], writes=[halo_b[cg]])
                    yield
                    kb.op("pool", lambda e: e.tensor_scalar(out=ucb[:], in0=uc[:], scalar1=1.0, scalar2=1.0,
                                                            op0=ALU.mult, op1=ALU.mult),
                          reads=[uc_b], writes=[ucb_b])
                    yield
                    pr, prb = next_bank(0, 8)
                    kb.op("pe", lambda e: e.matmul(pr[:], lhsT=wgate_b[:, cg * 2, :], rhs=ucb[:], start=True, stop=True),
                          reads=[ucb_b, b_misc], writes=[prb])
                    pg, pgb = next_bank(0, 8)
                    kb.op("pe", lambda e: e.matmul(pg[:], lhsT=wgate_b[:, cg * 2 + 1, :], rhs=ucb[:], start=True, stop=True),
                          reads=[ucb_b, b_misc], writes=[pgb])
                    yield
                    kb.op("act", lambda e: e.activation(out=rr[:], in_=pr[:], func=AF.Tanh,
                                                        bias=hbias[:, cg:cg + 1], scale=0.5),
                          reads=[prb, b_misc], writes=[rr_b])
                    kb.op("act", lambda e: e.activation(out=ii[:], in_=pg[:], func=AF.Tanh,
                                                        bias=hbias[:, 4 + cg:5 + cg], scale=0.5),
                          reads=[pgb, b_misc], writes=[ii_b])
                    kb.op("act", lambda e: e.activation(out=a2[:], in_=rr[:], func=AF.Exp, bias=c1[:, cg:cg + 1],
                                                        scale=c1[:, cg:cg + 1]),
                          reads=[rr_b, b_misc], writes=[a2_b])
                    kb.op("act", lambda e: e.activation(out=rr[:], in_=rr[:], func=AF.Exp, bias=c1x2[:, cg:cg + 1],
                                                        scale=c1x2[:, cg:cg + 1]),
                          reads=[rr_b, b_misc], writes=[rr_b])
                    yield
                    kb.op("pool", lambda e: e.tensor_scalar(out=ii[:], in0=ii[:], scalar1=1.0, scalar2=1.0,
                                                            op0=ALU.add, op1=ALU.mult),
                          reads=[ii_b], writes=[ii_b])
                    kb.op("pool", lambda e: e.tensor_tensor(out=ii[:], in0=ii[:], in1=uc[:], op=ALU.mult),
                          reads=[ii_b, uc_b], writes=[ii_b])
                    yield
                    kb.op("act", lambda e: e.activation(out=a2[:], in_=a2[:], func=AF.Sqrt, bias=qc[:, 0:1], scale=-0.25),
                          reads=[a2_b, b_misc], writes=[a2_b])
                    yield
                    kb.op("dve", lambda e: e.tensor_tensor(out=ii[:], in0=ii[:], in1=a2[:], op=ALU.mult),
                          reads=[ii_b, a2_b], writes=[ii_b])
                    init = 0.0 if tc == 0 else cy[:, 0:1]
                    kb.op("dve", lambda e: e.tensor_tensor_scan(out=hs[:], data0=rr[:], data1=ii[:], initial=init,
                                                                op0=ALU.mult, op1=ALU.add),
                          reads=[rr_b, ii_b] + ([cy_b] if tc else []), writes=[hs_b])
                    yield
                    kb.op("pool", lambda e: e.tensor_copy(out=cy[:], in_=hs[:, 511:512]), reads=[hs_b], writes=[cy_b])
                    hv = hs[:].rearrange("p (b two t) -> p b two t", two=2, t=128)
                    lv3 = ls[:].rearrange("p (b t) -> p b t", t=128)
                    kb.op("pool", lambda e: e.tensor_scalar(out=lv3, in0=hv[:, :, 0, :], scalar1=pcol(PC_SEL),
                                                            scalar2=1.0, op0=ALU.mult, op1=ALU.mult),
                          reads=[hs_b, b_const], writes=[ls_b])
                    yield
                    kb.op("dve", lambda e: e.scalar_tensor_tensor(out=lv3, in0=hv[:, :, 1, :], scalar=pcol(PC_SEL + 1),
                                                                  in1=lv3, op0=ALU.mult, op1=ALU.add),
                          reads=[hs_b, ls_b, b_const], writes=[ls_b])
                    o0 = tc * 256
                    pq, pqb = next_bank(0, 8)
                    for c in range(8):
                        kb.op("pe", lambda e, c=c: e.matmul(pq[:, 0:256], lhsT=wg_b[:, c, cg * 128:(cg + 1) * 128],
                                                            rhs=hTo[:, c, o0:o0 + 256], start=(c == 0), stop=(c == 7)),
                              reads=[wg_bb, hTo_b], writes=[pqb], signal=(c == 7))
                    yield
                    kb.op("act", lambda e: e.activation(out=g2[:], in_=pq[:, 0:256], func=AF.Square),
                          reads=[pqb], writes=[g2_b])
                    yield
                    kb.op("pool", lambda e: e.tensor_scalar(out=g2[:], in0=g2[:], scalar1=0.044715, scalar2=1.0,
                                                            op0=ALU.mult, op1=ALU.add),
                          reads=[g2_b], writes=[g2_b])
                    kb.op("dve", lambda e: e.tensor_tensor(out=g2[:], in0=g2[:], in1=pq[:, 0:256], op=ALU.mult),
                          reads=[g2_b, pqb], writes=[g2_b])
                    yield
                    kb.op("act", lambda e: e.activation(out=gp[:], in_=g2[:], func=AF.Tanh, scale=0.5 * GELU_C),
                          reads=[g2_b], writes=[gp_b])
                    yield
                    kb.op("dve", lambda e: e.scalar_tensor_tensor(out=gp[:], in0=gp[:], scalar=1.0, in1=pq[:, 0:256],
                                                                  op0=ALU.add, op1=ALU.mult),
                          reads=[gp_b, pqb], writes=[gp_b])
                    kb.op("dve", lambda e: e.scalar_tensor_tensor(out=lru_mixT[:, cg, o0:o0 + 256], in0=gp[:], scalar=0.5,
                                                                  in1=ls[:], op0=ALU.mult, op1=ALU.mult),
                          reads=[gp_b, ls_b], writes=[b_lru])
                    yield

                items = [(cg, tc) for tc in range(8) for cg in range(4)]
                for r0 in range(0, len(items), NCH):
                    live = [lru_iter(cg, tc, sets[(r0 + i) % NCH]) for i, (cg, tc) in enumerate(items[r0:r0 + NCH])]
                    while live:
                        nxt = []
                        for g in live:
                            try:
                                next(g)
                                nxt.append(g)
                            except StopIteration:
                                pass
                        live = nxt
                if debug:
                    kb.dma(dbg["d_lru"], lru_mixT[:], d_const, reads=[b_lru])
                kb.barrier()

            with ExitStack() as sD:
                KT = [sb("KT0", [128, 2, T], BF, sD)] * 2
                KT_b = [kb.buf("KT")] * 2
                QT = [sb("QT0", [128, 2, TO], BF, sD)] * 2
                QT_b = [kb.buf("QT")] * 2
                aug_d = [kb.dsem(f"aug{i}") for i in range(3)]
                Vh2 = [sb(f"Vh{i}", [128, 32, 129], BF, sD) for i in range(2)]
                V_b2 = [kb.buf("V") for _ in range(2)]
                for i in range(2):
                    kb.op("pool", lambda e, i=i: e.memset(Vh2[i][:, :, 128:129], 1.0), writes=[V_b2[i]])
                pt = [sb(f"pt{i}", [128, 512], BF, sD) for i in range(3)]
                pt_b = [kb.buf("pt") for _ in range(3)]
                Osb = [sb(f"Osb{i}", [128, 4, 129], F32, sD) for i in range(2)]
                Osb_b = [kb.buf("Osb") for _ in range(2)]
                att = sb("att", [128, 4, 128], F32, sD)
                att_b = kb.buf("att")
                junk = sb("junk", [128, 128], F32, sD)
                junk_b = kb.buf("junk")
                sm = sb("sm", [128, 16], F32, sD)
                sm_b = kb.buf("sm")
                attb = sb("attb", [128, 4, 128], BF, sD)
                attb_b = kb.buf("attb")
                PO = [P[3], P[4], P[5], P[6]]
                PO_b = [Pb[3], Pb[4], Pb[5], Pb[6]]

                srot = [0]
                deferred = []

                def v_tasks(h, nb):
                    s = h % 2
                    w, wb = wqk_b[s], wqk_bb[s]
                    for blk in range(32):
                        if nb == 1:
                            pb, pbb = P[srot[0] % 3], Pb[srot[0] % 3]
                            srot[0] += 1
                        else:
                            pb, pbb = next_bank(0, nb)
                        for c in range(8):
                            kb.op("pe", lambda e, c=c: e.matmul(pb[:, 0:128], lhsT=hT[:, c, blk * 128:(blk + 1) * 128],
                                                                rhs=w[:, c, 256:384], start=(c == 0), stop=(c == 7)),
                                  reads=[wb, hT_b], writes=[pbb], signal=(c == 7))
                        if nb != 1 and blk % 2:
                            kb.op("act", lambda e: e.activation(out=Vh2[s][:, blk, 0:128], in_=pb[:, 0:128], func=AF.Copy),
                                  reads=[pbb], writes=[V_b2[s]])
                        else:
                            kb.op("dve", lambda e: e.tensor_copy(out=Vh2[s][:, blk, 0:128], in_=pb[:, 0:128]),
                                  reads=[pbb], writes=[V_b2[s]])
                        yield

                def k_proj(h, nb):
                    s = h % 2
                    w, wb = wqk_b[s], wqk_bb[s]
                    for m in range(2):
                        kb.dma(KT[s][64:68, m, :], kaug[h], aug_d[s], writes=[KT_b[s]])
                    for tc in range(8):
                        pb, pbb = next_bank(0, nb)
                        tsl = slice(tc * 512, (tc + 1) * 512)
                        for c in range(8):
                            kb.op("pe", lambda e, c=c: e.matmul(pb[:], lhsT=w[:, c, 128:256], rhs=hT[:, c, tsl],
                                                                start=(c == 0), stop=(c == 7)),
                                  reads=[wb, hT_b], writes=[pbb], signal=(c == 7))
                        kb.op("act", lambda e: e.activation(out=KT[s][0:64, 0, tsl], in_=pb[0:64, :], func=AF.Copy),
                              reads=[pbb], writes=[KT_b[s]])
                        kb.op("dve", lambda e: e.tensor_copy(out=KT[s][0:64, 1, tsl], in_=pb[64:128, :]),
                              reads=[pbb], writes=[KT_b[s]])

                def q_proj(h, nb):
                    s = h % 2
                    w, wb = wqk_b[s], wqk_bb[s]
                    for m in range(2):
                        kb.dma(QT[0][64:68, m, :], qaug[h], aug_d[2], writes=[QT_b[0]])
                    for tc in range(4):
                        pb, pbb = next_bank(0, nb)
                        tsl = slice(tc * 512, (tc + 1) * 512)
                        for c in range(8):
                            kb.op("pe", lambda e, c=c: e.matmul(pb[:], lhsT=w[:, c, 0:128], rhs=hTo[:, c, tsl],
                                                                start=(c == 0), stop=(c == 7)),
                                  reads=[wb, hTo_b], writes=[pbb], signal=(c == 7))
                        kb.op("act", lambda e: e.activation(out=QT[0][0:64, 0, tsl], in_=pb[0:64, :], func=AF.Copy,
                                                            scale=0.125),
                              reads=[pbb], writes=[QT_b[0]])
                        kb.op("dve", lambda e: e.tensor_scalar(out=QT[0][0:64, 1, tsl], in0=pb[64:128, :], scalar1=0.125,
                                                               scalar2=None, op0=ALU.mult),
                              reads=[pbb], writes=[QT_b[0]])

                for _ in v_tasks(0, 3):
                    pass
                k_proj(0, 3)
                q_proj(0, 3)
                for h in range(NH):
                    hs_ = h % 2
                    Vh, V_b = Vh2[hs_], V_b2[hs_]
                    nxt_tasks = None
                    if h + 1 < NH:
                        load_wqk(h + 1)
                        nxt_tasks = v_tasks(h + 1, 1)
                    tiles = [(I, m, kbk) for I in range(4) for m in range(2) for kbk in range(8 * I + 8)]

                    def geom(I, kbk):
                        i_min = max(4 * I, kbk // 2)
                        return i_min, (i_min - 4 * I) * 128, (kbk // 2) >= 4 * I

                    def emit_qk(t):
                        I, m, kbk = tiles[t]
                        i_min, col0, masked = geom(I, kbk)
                        sbank[t] = srot[0] % 3
                        srot[0] += 1
                        sps, spsb = P[sbank[t]], Pb[sbank[t]]
                        kT = KT[hs_][0:68, m, kbk * 128:(kbk + 1) * 128]
                        q0 = 4 * I * 128
                        rd = [KT_b[hs_], QT_b[hs_]]
                        if masked:
                            kb.op("pe", lambda e: e.matmul(sps[:, col0:col0 + 128], lhsT=kT,
                                                           rhs=QT[hs_][0:68, m, q0 + col0:q0 + col0 + 128],
                                                           start=True, stop=False),
                                  reads=rd, writes=[spsb], signal=False)
                            last = (col0 + 128 == 512)
                            kb.op("pe", lambda e: e.matmul(sps[:, col0:col0 + 128], lhsT=ident, rhs=maskAB[kbk % 2],
                                                           start=False, stop=True),
                                  reads=[b_const, b_sel], writes=[spsb], signal=last)
                            if not last:
                                kb.op("pe", lambda e: e.matmul(sps[:, col0 + 128:512], lhsT=kT,
                                                               rhs=QT[hs_][0:68, m, q0 + col0 + 128:q0 + 512],
                                                               start=True, stop=True),
                                      reads=rd, writes=[spsb])
                        else:
                            kb.op("pe", lambda e: e.matmul(sps[:, col0:512], lhsT=kT,
                                                           rhs=QT[hs_][0:68, m, q0 + col0:q0 + 512],
                                                           start=True, stop=True),
                                  reads=rd, writes=[spsb])

                    def emit_exp(t):
                        I, m, kbk = tiles[t]
                        i_min, col0, masked = geom(I, kbk)
                        sps, spsb = P[sbank[t]], Pb[sbank[t]]
                        ptt, pttb = pt[t % 3], pt_b[t % 3]
                        kb.op("act", lambda e: e.activation(out=ptt[:, col0:512], in_=sps[:, col0:512], func=AF.Exp),
                              reads=[spsb], writes=[pttb])

                    def emit_pv(t):
                        I, m, kbk = tiles[t]
                        i_min, col0, masked = geom(I, kbk)
                        ptt, pttb = pt[t % 3], pt_b[t % 3]
                        for i in range(i_min, 4 * I + 4):
                            qi = i - 4 * I
                            kb.op("pe", lambda e, qi=qi, i=i: e.matmul(
                                PO[qi][:, 0:129], lhsT=ptt[:, qi * 128:(qi + 1) * 128],
                                rhs=Vh[:, kbk, :], start=(kbk == 0), stop=(kbk == 2 * i + 1)),
                                  reads=[pttb, V_b], writes=[PO_b[qi]], signal=(i == 4 * I + 3))

                    sbank = {}
                    LA = 2
                    for d in deferred:
                        d[1]()
                    del deferred[:]
                    for t in range(min(LA, len(tiles))):
                        emit_qk(t)
                    for t in range(len(tiles)):
                        I, m, kbk = tiles[t]
                        emit_exp(t)
                        if t + LA < len(tiles):
                            emit_qk(t + LA)
                        emit_pv(t)
                        for d in list(deferred):
                            d[0] -= 1
                            if d[0] <= 0:
                                deferred.remove(d)
                                d[1]()
                        if nxt_tasks is not None and t % 3 == 2:
                            next(nxt_tasks, None)
                        if kbk != 8 * I + 7:
                            continue
                        for qi in range(4):
                            kb.op("dve", lambda e, qi=qi: e.tensor_copy(out=Osb[m][:, qi, :], in_=PO[qi][:, 0:129]),
                                  reads=[PO_b[qi]], writes=[Osb_b[m]])
                        if m != 1:
                            continue
                        kb.op("dve", lambda e: e.reciprocal(out=sm[:, 0:4], in_=Osb[0][:, :, 128]),
                              reads=[Osb_b[0]], writes=[sm_b])
                        kb.op("dve", lambda e: e.reciprocal(out=sm[:, 4:8], in_=Osb[1][:, :, 128]),
                              reads=[Osb_b[1], sm_b], writes=[sm_b])
                        kb.op("dve", lambda e: e.tensor_scalar(out=sm[:, 4:8], in0=sm[:, 4:8], scalar1=nlam[:, 0:1],
                                                               scalar2=None, op0=ALU.mult),
                              reads=[sm_b, b_misc], writes=[sm_b])
                        for qi in range(4):
                            kb.op("dve", lambda e, qi=qi: e.tensor_scalar(out=att[:, qi, :], in0=Osb[0][:, qi, 0:128],
                                                                          scalar1=sm[:, qi:qi + 1], scalar2=None,
                                                                          op0=ALU.mult),
                                  reads=[Osb_b[0], sm_b], writes=[att_b])
                            kb.op("dve", lambda e, qi=qi: e.scalar_tensor_tensor(
                                out=att[:, qi, :], in0=Osb[1][:, qi, 0:128], scalar=sm[:, 4 + qi:5 + qi],
                                in1=att[:, qi, :], op0=ALU.mult, op1=ALU.add),
                                  reads=[Osb_b[1], sm_b, att_b], writes=[att_b])
                            kb.op("dve", lambda e, qi=qi: e.scalar_tensor_tensor(
                                out=junk[:], in0=att[:, qi, :], scalar=1.0, in1=att[:, qi, :], op0=ALU.mult, op1=ALU.mult,
                                accum_out=sm[:, 8 + qi:9 + qi]),
                                  reads=[att_b, sm_b], writes=[junk_b, sm_b])
                        kb.op("dve", lambda e: e.tensor_scalar(out=sm[:, 12:16], in0=sm[:, 8:12],
                                                               scalar1=1.0 / (128 * (1.0 - LAMBDA_INIT) ** 2),
                                                               scalar2=EPS / (1.0 - LAMBDA_INIT) ** 2,
                                                               op0=ALU.mult, op1=ALU.add),
                              reads=[sm_b], writes=[sm_b])
                        kb.op("pool", lambda e: e.tensor_tensor(out=sm[:, 12:16], in0=sm[:, 12:16], in1=mhalf[:],
                                                                op=ALU.pow),
                              reads=[sm_b, b_misc], writes=[sm_b])
                        for qi in range(4):
                            kb.op("dve", lambda e, qi=qi: e.scalar_tensor_tensor(
                                out=attb[:, qi, :], in0=att[:, qi, :], scalar=sm[:, 12 + qi:13 + qi],
                                in1=pv[:, PC_SUBGR:PC_SUBGR + 128], op0=ALU.mult, op1=ALU.mult),
                                  reads=[att_b, sm_b, b_const], writes=[attb_b])
                        def finish(h=h, I=I):
                            for qi in range(4):
                                kb.op("pe", lambda e, qi=qi: e.transpose(out=PTbf[:, qi * 128:(qi + 1) * 128],
                                                                         in_=attb[:, qi, :], identity=ident),
                                      reads=[attb_b, b_const], writes=[PTb], signal=(qi == 3))
                            kb.op("dve", lambda e: e.tensor_copy(out=attn_mixT[:, h, I * 512:(I + 1) * 512],
                                                                 in_=PTbf[:, 0:512]),
                                  reads=[PTb], writes=[b_attn])
                        deferred.append([10, finish])
                        if t == len(tiles) - 1 and h + 1 < NH:
                            for _ in nxt_tasks:
                                pass
                            k_proj(h + 1, 3)
                            q_proj(h + 1, 3)
                            if h + 1 == NH - 1:
                                prefetch_x1()
                        if debug and h == 0 and I == 0:
                            kb.dma(dbg["d_kt"], KT[0][:, :, 0:512], d_const, reads=[KT_b[0]])
                            kb.dma(dbg["d_qt"], QT[0][:, :, 0:512], d_const, reads=[QT_b[0]])
                            kb.dma(dbg["d_v"], Vh[:, 0:4, :], d_const, reads=[V_b])
                            kb.dma(dbg["d_sm"], sm[:], d_const, reads=[sm_b])
                            kb.dma(dbg["d_osb"][:, 0], Osb[0][:], d_const, reads=[Osb_b[0]])
                            kb.dma(dbg["d_osb"][:, 1], Osb[1][:], d_const, reads=[Osb_b[1]])
                for d in deferred:
                    d[1]()
                del deferred[:]
                if debug:
                    kb.dma(dbg["d_attn"], attn_mixT[:], d_const, reads=[b_attn])
                kb.barrier()

            s2 = s1
            sq = sb("sqE", [128, 8, 512], BF, s2)
            sq_b = kb.buf("sq")
            rt = [sb(f"rtE{i}", [128, 512], F32, s2) for i in range(2)]
            rt_b = [kb.buf("rt") for _ in range(2)]
            wup_g = [sb(f"wup_g{i}", [128, 8, 512], BF, s2) for i in range(2)]
            wup_gb = [kb.buf("wup") for _ in range(2)]
            wdn_g = [sb(f"wdn_g{i}", [128, 4, D], BF, s2) for i in range(2)]
            wdn_gb = [kb.buf("wdn") for _ in range(2)]
            wqk_f32 = wqk_b[0][:].rearrange("p c n -> p (c n)").bitcast(F32)
            sqm = [wqk_f32[:, i * 512:(i + 1) * 512] for i in range(2)]
            sqm_b = [kb.buf("sqm") for _ in range(2)]

            def load_group(fg):
                s = fg % 2
                for c in range(8):
                    load_cast(wup_g[s][:, c, :], wup_v[:, c, fg * 512:(fg + 1) * 512], 512, pcol(PC_GMLP + c),
                              dst_buf=wup_gb[s])
                for j in range(4):
                    for hf in range(2):
                        load_cast(wdn_g[s][:, j, hf * 512:(hf + 1) * 512], wdn_v[:, fg, j, hf * 512:(hf + 1) * 512],
                                  512, None, dst_buf=wdn_gb[s])

            with ExitStack() as sE:
                wo_b = sb("wo_b", [128, 8, D], BF, sE)
                wo_bb = [kb.buf("wo") for _ in range(2)]
                for hf in range(2):
                    for c in range(8):
                        load_cast(wo_b[:, c, hf * 512:(hf + 1) * 512], wout_v[:, c, hf * 512:(hf + 1) * 512], 512,
                                  None, dst_buf=wo_bb[hf], cast_en=("pool" if c % 2 else "dve"))
                load_group(0)

                def e_part1(tt):
                    tsl = slice(tt * 512, (tt + 1) * 512)
                    kb.op("act", lambda e: e.activation(out=sq[:], in_=x1T[:, :, tsl], func=AF.Square),
                          reads=[x1_b[tt]], writes=[sq_b])

                def e_part2(tt):
                    s = tt % 2
                    tsl = slice(tt * 512, (tt + 1) * 512)
                    pb, pbb = next_bank(4, 7)
                    for c in range(8):
                        kb.op("pe", lambda e, c=c: e.matmul(pb[:], lhsT=ones_bf[:], rhs=sq[:, c, :],
                                                            start=(c == 0), stop=(c == 7)),
                              reads=[sq_b, b_misc], writes=[pbb], signal=(c == 7))
                    kb.op("act", lambda e: e.activation(out=rt[s][:], in_=pb[:], func=AF.Ln, bias=epsc[:, 0:1],
                                                        scale=1.0 / D),
                          reads=[pbb, b_misc], writes=[rt_b[s]])
                    kb.op("act", lambda e: e.activation(out=rt[s][:], in_=rt[s][:], func=AF.Exp, scale=-0.5),
                          reads=[rt_b[s]], writes=[rt_b[s]])
                    for c in range(8):
                        en = "dve" if c % 2 == 0 else "pool"
                        kb.op(en, lambda e, c=c: e.tensor_tensor(out=hmT[:, c, tsl], in0=x1T[:, c, tsl], in1=rt[s][:],
                                                                 op=ALU.mult),
                              reads=[x1_b[tt], rt_b[s]], writes=[hm_b])

                for tt in range(4):
                    tsl = slice(tt * 512, (tt + 1) * 512)
                    for fc in range(8):
                        pb, pbb = next_bank(0, 4)
                        for kc in range(8):
                            rhs = attn_mixT[:, kc, tsl] if kc < 4 else lru_mixT[:, kc - 4, tsl]
                            kb.op("pe", lambda e, kc=kc, rhs=rhs: e.matmul(
                                pb[:], lhsT=wo_b[:, kc, fc * 128:(fc + 1) * 128], rhs=rhs,
                                start=(kc == 0), stop=(kc == 7)),
                                  reads=[wo_bb[fc // 4], b_attn, b_lru], writes=[pbb], signal=(kc == 7))
                        kb.op("dve", lambda e: e.tensor_tensor(out=x1T[:, fc, tsl], in0=pb[:], in1=x1T[:, fc, tsl],
                                                               op=ALU.add),
                              reads=[pbb, x1_b[tt]], writes=[x1_b[tt]])
                    if tt > 0:
                        e_part2(tt - 1)
                    e_part1(tt)
                e_part2(3)
                if debug:
                    kb.dma(dbg["d_x1"], x1T[:], d_const, reads=x1_b)

            with ExitStack() as sF:
                yb = [sb(f"yb{i}", [128, 8, 256], F32, sF) for i in range(2)]
                yb_b = [kb.buf("yb") for _ in range(2)]
                y_d = [kb.dsem(f"y{i}") for i in range(2)]

                def final_part1(tt):
                    tsl = slice(tt * 512, (tt + 1) * 512)
                    kb.op("act", lambda e: e.activation(out=sq[:], in_=x1T[:, :, tsl], func=AF.Square),
                          reads=[x1_b[tt]], writes=[sq_b])

                def final_part2(tt):
                    s = tt % 2
                    tsl = slice(tt * 512, (tt + 1) * 512)
                    pb, pbb = next_bank(0, 4)
                    for c in range(8):
                        kb.op("pe", lambda e, c=c: e.matmul(pb[:], lhsT=ones_bf[:], rhs=sq[:, c, :],
                                                            start=(c == 0), stop=(c == 7)),
                              reads=[sq_b, b_misc], writes=[pbb], signal=(c == 7))
                    kb.op("act", lambda e: e.activation(out=rt[s][:], in_=pb[:], func=AF.Ln, bias=epsc[:, 0:1],
                                                        scale=1.0 / D),
                          reads=[pbb, b_misc], writes=[rt_b[s]])
                    kb.op("act", lambda e: e.activation(out=rt[s][:], in_=rt[s][:], func=AF.Exp, scale=-0.5),
                          reads=[rt_b[s]], writes=[rt_b[s]])
                    for hf in range(2):
                        hsl = slice(tt * 512 + hf * 256, tt * 512 + (hf + 1) * 256)
                        for c in range(8):
                            kb.op("dve", lambda e, c=c: e.scalar_tensor_tensor(
                                out=yb[hf][:, c, :], in0=x1T[:, c, hsl], scalar=pcol(PC_GFIN + c),
                                in1=rt[s][:, hf * 256:(hf + 1) * 256], op0=ALU.mult, op1=ALU.mult),
                                  reads=[x1_b[tt], rt_b[s], b_const], writes=[yb_b[hf]])
                        kb.dma(yT_v[:, :, hsl], yb[hf][:], y_d[hf], reads=[yb_b[hf]])

                actT = [attn_mixT, lru_mixT]
                act_b = [b_attn, b_lru]
                ui = 0
                di = 0
                for fg in range(8):
                    s = fg % 2
                    if fg + 1 < 8:
                        load_group(fg + 1)
                    for tt in range(4):
                        tsl = slice(tt * 512, (tt + 1) * 512)
                        for j in range(4):
                            pb, pbb = P[ui % 4], Pb[ui % 4]
                            q, qb_ = sqm[ui % 2], sqm_b[ui % 2]
                            ui += 1
                            for c in range(8):
                                kb.op("pe", lambda e, c=c: e.matmul(pb[:], lhsT=wup_g[s][:, c, j * 128:(j + 1) * 128],
                                                                    rhs=hmT[:, c, tsl], start=(c == 0), stop=(c == 7)),
                                      reads=[wup_gb[s], hm_b], writes=[pbb], signal=(c == 7))
                            kb.op("act", lambda e: e.activation(out=q[:], in_=pb[:], func=AF.Square),
                                  reads=[pbb], writes=[qb_])
                            kb.op("dve", lambda e: e.scalar_tensor_tensor(out=actT[s][:, j, tsl], in0=pb[:], scalar=0.0,
                                                                          in1=q[:], op0=ALU.is_gt, op1=ALU.mult),
                                  reads=[pbb, qb_], writes=[act_b[s]])
                    for tt in range(4):
                        tsl = slice(tt * 512, (tt + 1) * 512)
                        for fc in range(8):
                            pb, pbb = P[4 + di % 3], Pb[4 + di % 3]
                            di += 1
                            for j in range(4):
                                kb.op("pe", lambda e, j=j: e.matmul(pb[:], lhsT=wdn_g[s][:, j, fc * 128:(fc + 1) * 128],
                                                                    rhs=actT[s][:, j, tsl], start=(j == 0), stop=(j == 3)),
                                      reads=[wdn_gb[s], act_b[s]], writes=[pbb], signal=(j == 3))
                            kb.op("dve", lambda e: e.tensor_tensor(out=x1T[:, fc, tsl], in0=pb[:], in1=x1T[:, fc, tsl],
                                                                   op=ALU.add),
                                  reads=[pbb, x1_b[tt]], writes=[x1_b[tt]])
                            if fg == 7 and fc == 2 and tt > 0:
                                final_part2(tt - 1)
                        if fg == 7:
                            final_part1(tt)
                if True:
                    final_part2(3)
                if debug:
                    kb.dma(dbg["d_x2"], x1T[:], d_const, reads=x1_b)

            kb.barrier()
            spE = kb.E["sp"]
            for d in y_d:
                spE.eng.wait_ge(d.sem, d.count)
    return nc


_NC_CACHE = {}


def _slopes():
    return np.array([2.0 ** (-8.0 * (i + 1) / NH) for i in range(NH)], dtype=np.float32)


def _const_tables(j):
    bf = ml_dtypes.bfloat16
    ident = np.eye(128, dtype=np.float32)
    kk = np.arange(128)[:, None]
    qq = np.arange(128)[None, :]
    tri = np.where(kk > qq, NEG, 0.0).astype(np.float32)
    full = np.full((128, 128), NEG, np.float32)
    zero = np.zeros((128, 128), np.float32)
    mA, mB = (tri, full) if j == 0 else (zero, tri)
    s0 = ident * (1.0 if j == 0 else 0.0)
    s1 = ident * (0.0 if j == 0 else 1.0)
    cbf = np.concatenate([ident, mA, mB, s0, s1], axis=1).astype(bf)
    sl = _slopes()
    kpos = np.arange(T)
    kaug = np.zeros((NH, 4, T), np.float32)
    qaug = np.zeros((NH, 4, TO), np.float32)
    own = (np.arange(TO) // 128) * 256 + j * 128 + (np.arange(TO) % 128)
    for h in range(NH):
        kaug[h, 0] = sl[h] * (kpos % 128)
        kaug[h, 1] = sl[h] * (kpos - kpos % 128)
        kaug[h, 2] = 1.0
        kaug[h, 3] = 1.0
        qaug[h, 0] = 1.0
        qaug[h, 1] = 1.0
        qaug[h, 2] = -sl[h] * (own % 128)
        qaug[h, 3] = -sl[h] * (own - own % 128)
    return cbf, kaug.astype(bf), qaug.astype(bf), own


def _chunked(v):
    return np.ascontiguousarray(np.asarray(v, np.float32).reshape(8, 128).T)


def _prep_shared(inp):
    f = lambda k: np.asarray(inp[k], np.float32)
    w_in = f("w_in")[0]
    wq, wk, wvv = w_in[:, 0:512], w_in[:, 512:1024], w_in[:, 1024:1536]
    wqkv = np.stack([np.concatenate([wq[:, h * 128:(h + 1) * 128], wk[:, h * 128:(h + 1) * 128],
                                     wvv[:, h * 128:(h + 1) * 128]], axis=1) for h in range(NH)], 0)
    sh = {
        "wqkv": np.ascontiguousarray(wqkv),
        "wu": np.ascontiguousarray(w_in[:, 1536:2048]),
        "wg": np.ascontiguousarray(w_in[:, 2048:2560]),
        "wout": np.ascontiguousarray(f("w_out")[0]),
        "wup": np.ascontiguousarray(f("w_up")[0]),
        "wdn": np.ascontiguousarray(f("w_down")[0]),
    }
    pv = np.zeros((128, NPC), np.float32)
    pv[:, PC_GMIX:PC_GMIX + 8] = _chunked(f("norm_mix_g")[0])
    pv[:, PC_GMLP:PC_GMLP + 8] = _chunked(f("norm_mlp_g")[0])
    pv[:, PC_GFIN:PC_GFIN + 8] = _chunked(f("final_g"))
    cw = f("conv_w")[0]
    for cg in range(4):
        for k in range(4):
            pv[:, PC_CW + cg * 4 + k] = cw[k, cg * 128:(cg + 1) * 128]
        pv[:, PC_CB + cg] = f("conv_b")[0][cg * 128:(cg + 1) * 128]
        pv[:, PC_BRG + cg] = f("b_rg")[0].reshape(512)[cg * 128:(cg + 1) * 128]
        pv[:, PC_BIG + cg] = f("b_ig")[0].reshape(512)[cg * 128:(cg + 1) * 128]
        pv[:, PC_L + cg] = f("lru_L")[0][cg * 128:(cg + 1) * 128]
    pv[:, PC_SUBG] = f("subln_g")[0]
    pv[:, PC_SUBGR:PC_SUBGR + 128] = f("subln_g")[0][None, :]
    lam = np.stack([f("lambda_q1")[0], f("lambda_k1")[0], f("lambda_q2")[0], f("lambda_k2")[0]], 0).reshape(256)
    pv[:, PC_LAM:PC_LAM + 256] = lam[None, :]
    wgate = np.zeros((128, 8, 128), np.float32)
    wrg, wig = f("w_rg")[0], f("w_ig")[0]
    for cg in range(4):
        for nl in range(2):
            n = cg * 2 + nl
            wgate[nl * 64:(nl + 1) * 64, cg * 2 + 0, nl * 64:(nl + 1) * 64] = wrg[n]
            wgate[nl * 64:(nl + 1) * 64, cg * 2 + 1, nl * 64:(nl + 1) * 64] = wig[n]
    sh["wgate"] = wgate
    return sh, pv


def _in_maps(inp):
    x = np.asarray(inp["x"], np.float32)
    sh, pv = _prep_shared(inp)
    maps, owns = [], []
    for core in range(8):
        b, j = core // 2, core % 2
        cbf, kaug, qaug, own = _const_tables(j)
        p = pv.copy()
        p[:, PC_SEL] = 1.0 if j == 0 else 0.0
        p[:, PC_SEL + 1] = 0.0 if j == 0 else 1.0
        xT = np.ascontiguousarray(x[b].T)
        m = dict(sh)
        m.update({"xf": xT, "xo": np.ascontiguousarray(xT[:, own]), "pvec": p, "cbf": cbf, "kaug": kaug,
                  "qaug": qaug})
        maps.append(m)
        owns.append(own)
    return maps, owns


def kernel(**inputs):
    if "nc" not in _NC_CACHE:
        _NC_CACHE["nc"] = build_program()
    nc = _NC_CACHE["nc"]
    maps, owns = _in_maps(inputs)
    res = run_bass_kernel_spmd(nc, maps, core_ids=list(range(8)))
    B, S = inputs["x"].shape[0], inputs["x"].shape[1]
    out = np.empty((B, S, D), np.float32)
    for core in range(8):
        b = core // 2
        out[b, owns[core], :] = np.asarray(res.results[core]["yT"], np.float32).T
    return out
```

```python
import math
from contextlib import ExitStack

import numpy as np
import ml_dtypes
import concourse.bass as bass
import concourse.mybir as mybir
from concourse.bass_utils import run_bass_kernel_spmd

F32 = mybir.dt.float32
BF = mybir.dt.bfloat16
AF = mybir.ActivationFunctionType
ALU = mybir.AluOpType

D = 1024
T = 4096
TO = 2048
NH = 4
EPS = 1e-6
LAMBDA_INIT = 0.8 - 0.6 * math.exp(0.0)
GELU_C = 2.0 * math.sqrt(2.0 / math.pi)
NEG = -30000.0

PC_GMIX, PC_GMLP, PC_GFIN = 0, 8, 16
PC_CW, PC_CB, PC_BRG, PC_BIG, PC_L = 24, 40, 44, 48, 52
PC_SUBG, PC_SEL, PC_LAM = 56, 57, 59
PC_SUBGR = 59 + 256
NPC = 59 + 256 + 128


class Buf:
    __slots__ = ("name", "w", "r", "pr")

    def __init__(self, name):
        self.name = name
        self.w = {}
        self.r = {}
        self.pr = {}


class Eng:
    def __init__(self, name, eng, sem):
        self.name, self.eng, self.sem = name, eng, sem
        self.count = 0
        self.seen = {}


class DSem:
    def __init__(self, name, sem):
        self.name, self.sem = name, sem
        self.count = 0


class KB:
    def __init__(self, nc, es):
        self.nc = nc
        self.es = es
        self.E = {}
        for name, eng in (("pe", nc.tensor), ("act", nc.scalar), ("dve", nc.vector),
                          ("pool", nc.gpsimd), ("sp", nc.sync)):
            self.E[name] = Eng(name, eng, es.enter_context(nc.semaphore("s_" + name)))
        self.dsems = []
        self.nbuf = 0

    def dsem(self, name):
        d = DSem(name, self.es.enter_context(self.nc.semaphore("d_" + name)))
        self.dsems.append(d)
        return d

    def buf(self, name="b"):
        self.nbuf += 1
        return Buf(f"{name}{self.nbuf}")

    def _wait(self, E, st, raw):
        src, val = st
        if src is E and (E.name in ("pe", "sp") or not raw):
            return
        if isinstance(src, DSem):
            val = src.count
        key = id(src)
        if E.seen.get(key, 0) >= val:
            return
        E.eng.wait_ge(src.sem, val)
        E.seen[key] = val

    def _deps(self, E, reads, writes):
        for b in reads:
            for st in b.w.values():
                self._wait(E, st, True)
        for b in writes:
            for st in b.r.values():
                self._wait(E, st, False)
            for st in b.pr.values():
                self._wait(E, st, False)

    def _post(self, src, st, reads, writes):
        for b in reads:
            b.r[id(src)] = st
        for b in writes:
            if b.r:
                b.pr = b.r
                b.w = {}
                b.r = {}
            b.w[id(src)] = st

    def op(self, en, fn, reads=(), writes=(), signal=True):
        E = self.E[en]
        self._deps(E, reads, writes)
        ins = fn(E.eng)
        if signal:
            ins.then_inc(E.sem, 1)
            E.count += 1
            st = (E, E.count)
        else:
            st = (E, E.count + 1)
        self._post(E, st, reads, writes)
        return ins

    def dma(self, out, in_, dsem, reads=(), writes=(), en="sp"):
        E = self.E[en]
        self._deps(E, reads, writes)
        ins = E.eng.dma_start(out=out, in_=in_)
        ins.then_inc(dsem.sem, 16)
        dsem.count += 16
        self._post(dsem, (dsem, dsem.count), reads, writes)

    def barrier(self):
        for E in self.E.values():
            for F in self.E.values():
                if F is not E and F.count > 0:
                    self._wait(E, (F, F.count), True)
            for d in self.dsems:
                if d.count > 0:
                    self._wait(E, (d, d.count), True)


def build_program(debug=False):
    nc = bass.Bass("TRN2", target_bir_lowering=False)
    dbg = {}
    if debug:
        for nm, shp, dt in (("d_hT", [128, 8, 512], BF), ("d_hTo", [128, 8, 512], BF), ("d_lru", [128, 4, TO], BF),
                            ("d_attn", [128, 4, TO], BF), ("d_x1", [128, 8, TO], F32), ("d_x2", [128, 8, TO], F32),
                            ("d_kt", [128, 2, 512], BF), ("d_qt", [128, 2, 512], BF), ("d_v", [128, 4, 129], BF),
                            ("d_sm", [128, 16], F32), ("d_osb", [128, 2, 4, 129], F32)):
            dbg[nm] = nc.dram_tensor(nm, shp, dt, kind="ExternalOutput").ap()

    def din(name, shape, dt=F32):
        return nc.dram_tensor(name, list(shape), dt, kind="ExternalInput").ap()

    xf = din("xf", [D, T])
    xo = din("xo", [D, TO])
    wqkv = din("wqkv", [NH, D, 384])
    wu = din("wu", [D, 512])
    wg = din("wg", [D, 512])
    wout = din("wout", [D, D])
    wup = din("wup", [D, 4 * D])
    wdn = din("wdn", [4 * D, D])
    pvec = din("pvec", [128, NPC])
    wgate = din("wgate", [128, 8, 128])
    cbf = din("cbf", [128, 640], BF)
    kaug = din("kaug", [NH, 4, T], BF)
    qaug = din("qaug", [NH, 4, TO], BF)
    yT = nc.dram_tensor("yT", [D, TO], F32, kind="ExternalOutput").ap()

    xf_v = xf.rearrange("(c p) t -> p c t", p=128)
    xo_v = xo.rearrange("(c p) t -> p c t", p=128)
    yT_v = yT.rearrange("(c p) t -> p c t", p=128)
    wu_v = wu.rearrange("(c p) n -> p c n", p=128)
    wg_v = wg.rearrange("(c p) n -> p c n", p=128)
    wout_v = wout.rearrange("(c p) n -> p c n", p=128)
    wup_v = wup.rearrange("(c p) n -> p c n", p=128)
    wdn_v = wdn.rearrange("(g j p) n -> p g j n", j=4, p=128)

    with ExitStack() as es:
        kb = KB(nc, es)

        def sb(name, shape, dt, stack=es):
            return stack.enter_context(nc.sbuf_tensor(name, list(shape), dt))

        def ps(name, shape, dt=F32, stack=es):
            return stack.enter_context(nc.psum_tensor(name, list(shape), dt))

        pv = sb("pv", [128, NPC], F32)
        cb_sb = sb("cb_sb", [128, 384], BF)
        ones_bf = sb("ones_bf", [128, 128], BF)
        epsc = sb("epsc", [128, 1], F32)
        onec = sb("onec", [128, 1], F32)
        c1 = sb("c1", [128, 4], F32)
        c1x2 = sb("c1x2", [128, 4], F32)
        hbias = sb("hbias", [128, 8], F32)
        qc = sb("qc", [128, 1], F32)
        mhalf = sb("mhalf", [128, 4], F32)
        nlam = sb("nlam", [128, 1], F32)
        lsum = sb("lsum", [128, 4], F32)
        wgate_b = sb("wgate_b", [128, 8, 128], BF)
        attn_mixT = sb("attn_mixT", [128, 4, TO], BF)
        lru_mixT = sb("lru_mixT", [128, 4, TO], BF)
        wst = [sb(f"wst{i}", [128, 512], F32) for i in range(4)]
        wst_b = [kb.buf("wst") for _ in range(4)]
        wst_d = [kb.dsem(f"wst{i}") for i in range(4)]
        wst_i = [0]

        b_lru = kb.buf("lru")
        b_attn = kb.buf("attn")
        ident = cb_sb[:, 0:128]
        maskAB = [cb_sb[:, 128:256], cb_sb[:, 256:384]]

        P = [ps(f"P{i}", [128, 512]) for i in range(8)]
        Pb = [kb.buf("P") for _ in range(8)]
        PT, PTb = P[7], Pb[7]
        PTbf = P[7][:].bitcast(BF)
        ident_f = sb("ident_f", [128, 128], F32)

        s0 = ExitStack()
        with s0:
            ltmp = sb("ltmp", [128, 4, 64], F32, s0)
            wgate_f = sb("wgate_f", [128, 8, 128], F32, s0)
            b_const = kb.buf("const")
            d_const = kb.dsem("const")
            kb.dma(pv[:], pvec[:, :], d_const, writes=[b_const])
            b_sel = kb.buf("sel")
            d_sel = kb.dsem("sel")
            kb.dma(cb_sb[:, 0:128], cbf[:, 0:128], d_const, writes=[kb.buf()])
            kb.dma(cb_sb[:, 128:384], cbf[:, 384:640], d_sel, writes=[b_sel])
            kb.dma(wgate_f[:], wgate[:, :, :], d_const, writes=[kb.buf()])
            b_const.w = {id(d_const): (d_const, d_const.count)}

            def pcol(c, n=1):
                return pv[:, c:c + n]

            b_misc = kb.buf("misc")
            kb.op("pool", lambda e: e.memset(ones_bf[:], 1.0), writes=[b_misc])
            kb.op("pool", lambda e: e.tensor_copy(out=ident_f[:], in_=cb_sb[:, 0:128]), reads=[b_const], writes=[b_misc])
            kb.op("pool", lambda e: e.memset(epsc[:], EPS), writes=[b_misc])
            kb.op("pool", lambda e: e.memset(onec[:], 1.0), writes=[b_misc])
            kb.op("pool", lambda e: e.tensor_copy(out=wgate_b[:], in_=wgate_f[:]), reads=[b_const], writes=[b_misc])
            kb.op("act", lambda e: e.activation(out=c1[:], in_=pcol(PC_L, 4), func=AF.Exp, scale=-1.0),
                  reads=[b_const], writes=[b_misc])
            kb.op("act", lambda e: e.activation(out=c1[:], in_=c1[:], func=AF.Ln, bias=onec[:, 0:1], scale=1.0),
                  reads=[b_misc], writes=[b_misc])
            kb.op("dve", lambda e: e.tensor_scalar(out=c1x2[:], in0=c1[:], scalar1=-4.0, scalar2=None, op0=ALU.mult),
                  reads=[b_misc], writes=[b_misc])
            kb.op("dve", lambda e: e.tensor_scalar(out=c1[:], in0=c1[:], scalar1=-8.0, scalar2=None, op0=ALU.mult),
                  reads=[b_misc], writes=[b_misc])
            kb.op("dve", lambda e: e.tensor_scalar(out=hbias[:], in0=pcol(PC_BRG, 8), scalar1=0.5, scalar2=None,
                                                   op0=ALU.mult),
                  reads=[b_const], writes=[b_misc])
            kb.op("pool", lambda e: e.memset(qc[:], 0.25), writes=[b_misc])
            kb.op("pool", lambda e: e.memset(mhalf[:], -0.5), writes=[b_misc])
            lv = pv[:, PC_LAM:PC_LAM + 256].rearrange("p (a d) -> p a d", d=64)
            kb.op("dve", lambda e: e.tensor_tensor(out=ltmp[:, 0, :], in0=lv[:, 0, :], in1=lv[:, 1, :], op=ALU.mult),
                  reads=[b_const], writes=[b_misc])
            kb.op("dve", lambda e: e.tensor_tensor(out=ltmp[:, 1, :], in0=lv[:, 2, :], in1=lv[:, 3, :], op=ALU.mult),
                  reads=[b_const], writes=[b_misc])
            kb.op("dve", lambda e: e.reduce_sum(out=lsum[:, 0:2], in_=ltmp[:, 0:2, :], axis=mybir.AxisListType.X),
                  reads=[b_misc], writes=[b_misc])
            kb.op("act", lambda e: e.activation(out=lsum[:, 2:4], in_=lsum[:, 0:2], func=AF.Exp),
                  reads=[b_misc], writes=[b_misc])
            kb.op("dve", lambda e: e.scalar_tensor_tensor(out=nlam[:], in0=lsum[:, 3:4], scalar=-LAMBDA_INIT,
                                                          in1=lsum[:, 2:3], op0=ALU.add, op1=ALU.subtract),
                  reads=[b_misc], writes=[b_misc])


        def load_cast(dst, src, n, scale_col=None, cast_en="pool", dst_buf=None):
            i = wst_i[0] % 4
            wst_i[0] += 1
            st, stb, std = wst[i], wst_b[i], wst_d[i]
            kb.dma(st[:, 0:n], src, std, writes=[stb])
            wr = [dst_buf] if dst_buf is not None else []
            if scale_col is None:
                if cast_en == "pool":
                    kb.op("pool", lambda e: e.tensor_scalar(out=dst, in0=st[:, 0:n], scalar1=1.0, scalar2=1.0,
                                                            op0=ALU.mult, op1=ALU.mult),
                          reads=[stb, b_const], writes=wr)
                else:
                    kb.op(cast_en, lambda e: e.tensor_copy(out=dst, in_=st[:, 0:n]), reads=[stb, b_const], writes=wr)
            else:
                kb.op(cast_en, lambda e: e.tensor_scalar(out=dst, in0=st[:, 0:n], scalar1=scale_col, scalar2=1.0,
                                                         op0=ALU.mult, op1=ALU.mult),
                      reads=[stb, b_const], writes=wr)

        pi = [0]

        def next_bank(lo=0, hi=4):
            i = lo + pi[0] % (hi - lo)
            pi[0] += 1
            return P[i], Pb[i]

        def rms_tile(stack_bufs, xt, xt_b, dstT, dst_b, t0, n=512):
            sq, sq_b, rt, rt_b = stack_bufs
            kb.op("act", lambda e: e.activation(out=sq[:, :, 0:n], in_=xt[:, :, 0:n], func=AF.Square),
                  reads=[xt_b], writes=[sq_b])
            pb, pbb = next_bank()
            for c in range(8):
                kb.op("pe", lambda e, c=c: e.matmul(pb[:, 0:n], lhsT=ones_bf[:], rhs=sq[:, c, 0:n],
                                                    start=(c == 0), stop=(c == 7)),
                      reads=[sq_b, b_misc], writes=[pbb], signal=(c == 7))
            if dstT is None:
                kb.op("act", lambda e: e.activation(out=rt[:, 0:n], in_=pb[:, 0:n], func=AF.Sqrt,
                                                    bias=epsc[:, 0:1], scale=1.0 / D),
                      reads=[pbb, b_misc], writes=[rt_b])
                kb.op("dve", lambda e: e.reciprocal(out=rt[:, 0:n], in_=rt[:, 0:n]), reads=[rt_b], writes=[rt_b])
            else:
                kb.op("act", lambda e: e.activation(out=rt[:, 0:n], in_=pb[:, 0:n], func=AF.Ln,
                                                    bias=epsc[:, 0:1], scale=1.0 / D),
                      reads=[pbb, b_misc], writes=[rt_b])
                kb.op("act", lambda e: e.activation(out=pb[:, 0:n], in_=rt[:, 0:n], func=AF.Exp, scale=-0.5),
                      reads=[rt_b], writes=[pbb])
                for c in range(8):
                    kb.op("dve", lambda e, c=c: e.tensor_tensor(out=dstT[:, c, t0:t0 + n], in0=xt[:, c, 0:n],
                                                                in1=pb[:, 0:n], op=ALU.mult),
                          reads=[xt_b, pbb, b_misc], writes=[dst_b])

        with ExitStack() as s1:
            hT = sb("hT", [128, 8, T], BF, s1)
            hTo = sb("hTo", [128, 8, TO], BF, s1)
            hT_b = kb.buf("hT")
            hTo_b = kb.buf("hTo")
            wqk_b = [sb("wqk_b0", [128, 8, 384], BF, s1)] * 2
            wqk_bb = [kb.buf("wqk")] * 2

            def load_wqk(h):
                for c in range(8):
                    load_cast(wqk_b[h % 2][:, c, :], wqkv[h].rearrange("(c p) n -> p c n", p=128)[:, c, :], 384,
                              pcol(PC_GMIX + c), dst_buf=wqk_bb[h % 2])
            x1T = hT[:].bitcast(F32)
            hmT = hTo
            hm_b = hTo_b
            x1_b = [kb.buf("x1") for _ in range(4)]
            x1_d = [kb.dsem(f"x1_{i}") for i in range(4)]

            def prefetch_x1():
                for tt in range(4):
                    kb.dma(x1T[:, :, tt * 512:(tt + 1) * 512], xo_v[:, :, tt * 512:(tt + 1) * 512], x1_d[tt],
                           writes=[x1_b[tt], hT_b])

            with ExitStack() as sA:
                NXS = 3
                xst = [sb(f"xst{i}", [128, 8, 512], F32, sA) for i in range(NXS)]
                xst_b = [kb.buf("xst") for _ in range(NXS)]
                xst_d = [kb.dsem(f"xst{i}") for i in range(NXS)]
                sq = sb("sqA", [128, 8, 512], BF, sA)
                sq_b = kb.buf("sq")
                rt = [sb(f"rtA{i}", [128, 512], F32, sA) for i in range(2)]
                rt_b = [kb.buf("rt") for _ in range(2)]
                selm = [cb_sb[:, 128:256], cb_sb[:, 256:384]]
                hT_t = [kb.buf("hTt") for _ in range(8)]

                def select_own(it):
                    for b in range(2):
                        ev = it * 512 + b * 256
                        ob = 2 * it + b
                        for ch in range(2):
                            pb, pbb = next_bank(4, 8)
                            pv3 = pb[:].rearrange("p (c t) -> p c t", t=128)
                            kb.op("pe", lambda e: e.matmul(pv3, lhsT=selm[0], rhs=hT[:, 4 * ch:4 * ch + 4, ev:ev + 128],
                                                           start=True, stop=False),
                                  reads=[hT_t[it], b_sel], writes=[pbb], signal=False)
                            kb.op("pe", lambda e: e.matmul(pv3, lhsT=selm[1],
                                                           rhs=hT[:, 4 * ch:4 * ch + 4, ev + 128:ev + 256],
                                                           start=False, stop=True),
                                  reads=[hT_t[it], b_sel], writes=[pbb])
                            kb.op("act", lambda e: e.activation(out=hTo[:, 4 * ch:4 * ch + 4, ob * 128:(ob + 1) * 128],
                                                                in_=pv3, func=AF.Copy),
                                  reads=[pbb], writes=[hTo_b])

                for it in range(8):
                    s = it % NXS
                    t0 = it * 512
                    kb.dma(xst[s][:], xf_v[:, :, t0:t0 + 512], xst_d[s], writes=[xst_b[s]])
                    rms_tile((sq, sq_b, rt[it % 2], rt_b[it % 2]), xst[s][:], xst_b[s], hT, hT_t[it], t0)
                    if it >= 2:
                        select_own(it - 2)
                select_own(6)
                select_own(7)
                kb.dma(cb_sb[:, 128:384], cbf[:, 128:384], d_sel, writes=[b_sel])
                if debug:
                    kb.dma(dbg["d_hT"], hT[:, :, 0:512], d_const, reads=[hT_b])
                    kb.dma(dbg["d_hTo"], hTo[:, :, 0:512], d_const, reads=[hTo_b])
                kb.barrier()

            with ExitStack() as sB:
                wu_b = sb("wu_b", [128, 8, 512], BF, sB)
                wg_b = sb("wg_b", [128, 8, 512], BF, sB)
                wu_bb, wg_bb = kb.buf("wu"), kb.buf("wg")
                for c in range(8):
                    load_cast(wu_b[:, c, :], wu_v[:, c, :], 512, pcol(PC_GMIX + c), dst_buf=wu_bb,
                              cast_en=("pool" if c % 2 else "dve"))
                for c in range(8):
                    load_cast(wg_b[:, c, :], wg_v[:, c, :], 512, pcol(PC_GMIX + c), dst_buf=wg_bb,
                              cast_en=("pool" if c % 2 else "dve"))
                load_wqk(0)
                NCH = 4

                def mkset(i):
                    S = {}
                    for nm, shp, dt in (("ub", [128, 516], F32), ("uc", [128, 512], F32), ("ucbf", [128, 256], F32),
                                        ("rr", [128, 512], F32), ("ii", [128, 512], F32), ("a2", [128, 512], F32)):
                        S[nm] = sb(f"{nm}_{i}", shp, dt, sB)
                        S[nm + "_b"] = kb.buf(nm)
                    S["ucb"], S["ucb_b"] = S["ucbf"][:].bitcast(BF), S["ucbf_b"]
                    S["g2"], S["g2_b"] = S["ucbf"][:], S["ucbf_b"]
                    S["hs"], S["hs_b"] = S["uc"][:], S["uc_b"]
                    S["ls"], S["ls_b"] = S["a2"][:, 0:256], S["a2_b"]
                    S["gp"], S["gp_b"] = S["ub"][:, 4:260], S["ub_b"]
                    for k in ("ub", "uc", "rr", "ii", "a2"):
                        S[k] = S[k][:]
                    return S

                sets = [mkset(i) for i in range(NCH)]
                halo = [sb(f"halo{i}", [128, 4], F32, sB) for i in range(4)]
                halo_b = [kb.buf("halo") for _ in range(4)]
                cyc = [halo[i][:, 3:4] for i in range(4)]
                cyc_b = [kb.buf("cyc") for _ in range(4)]

                def lru_iter(cg, tc, S):
                    cwc = lambda k: pcol(PC_CW + cg * 4 + k)
                    ub, uc, ucb, rr, ii, a2, hs, ls, g2, gp = (S[k] for k in
                        ("ub", "uc", "ucb", "rr", "ii", "a2", "hs", "ls", "g2", "gp"))
                    ub_b, uc_b, ucb_b, rr_b, ii_b, a2_b, hs_b, ls_b, g2_b, gp_b = (S[k + "_b"] for k in
                        ("ub", "uc", "ucb", "rr", "ii", "a2", "hs", "ls", "g2", "gp"))
                    cy, cy_b = cyc[cg], cyc_b[cg]
                    t0 = tc * 512
                    pu, pub = next_bank(0, 8)
                    for c in range(8):
                        kb.op("pe", lambda e, c=c: e.matmul(pu[:], lhsT=wu_b[:, c, cg * 128:(cg + 1) * 128],
                                                            rhs=hT[:, c, t0:t0 + 512], start=(c == 0), stop=(c == 7)),
                              reads=[wu_bb, hT_b], writes=[pub], signal=(c == 7))
                    if tc == 0:
                        kb.op("pool", lambda e: e.memset(halo[cg][:, 0:3], 0.0), writes=[halo_b[cg]])
                    yield
                    cbc = pcol(PC_CB + cg)
                    kb.op("act", lambda e: e.activation(out=uc[:, 3:512], in_=pu[:, 0:509], func=AF.Identity,
                                                        bias=cbc, scale=cwc(0)),
                          reads=[pub, b_const], writes=[uc_b])
                    kb.op("act", lambda e: e.activation(out=uc[:, 0:3], in_=halo[cg][:, 0:3], func=AF.Identity,
                                                        bias=cbc, scale=cwc(0)),
                          reads=[halo_b[cg], b_const], writes=[uc_b])
                    yield
                    for k in range(1, 4):
                        kb.op("dve", lambda e, k=k: e.scalar_tensor_tensor(
                            out=uc[:, 3 - k:512], in0=pu[:, 0:509 + k], scalar=cwc(k), in1=uc[:, 3 - k:512],
                            op0=ALU.mult, op1=ALU.add),
                              reads=[pub, uc_b, b_const], writes=[uc_b])
                        if k < 3:
                            kb.op("dve", lambda e, k=k: e.scalar_tensor_tensor(
                                out=uc[:, 0:3 - k], in0=halo[cg][:, k:3], scalar=cwc(k), in1=uc[:, 0:3 - k],
                                op0=ALU.mult, op1=ALU.add),
                                  reads=[halo_b[cg], uc_b, b_const], writes=[uc_b])
                    kb.op("dve", lambda e: e.tensor_copy(out=halo[cg][:, 0:3], in_=pu[:, 509:512]),
                          reads=[pub, halo_b[cg]], writes=[halo_b[cg]])
                    yield
                    kb.op("pool", lambda e: e.tensor_scalar(out=ucb[:], in0=uc[:], scalar1=1.0, scalar2=1.0,
                                                            op0=ALU.mult, op1=ALU.mult),
                          reads=[uc_b], writes=[ucb_b])
                    yield
                    pr, prb = next_bank(0, 8)
                    kb.op("pe", lambda e: e.matmul(pr[:], lhsT=wgate_b[:, cg * 2, :], rhs=ucb[:], start=True, stop=True),
                          reads=[ucb_b, b_misc], writes=[prb])
                    pg, pgb = next_bank(0, 8)
                    kb.op("pe", lambda e: e.matmul(pg[:], lhsT=wgate_b[:, cg * 2 + 1, :], rhs=ucb[:], start=True, stop=True),
                          reads=[ucb_b, b_misc], writes=[pgb])
                    yield
                    kb.op("act", lambda e: e.activation(out=rr[:], in_=pr[:], func=AF.Tanh,
                                                        bias=hbias[:, cg:cg + 1], scale=0.5),
                          reads=[prb, b_misc], writes=[rr_b])
                    kb.op("act", lambda e: e.activation(out=ii[:], in_=pg[:], func=AF.Tanh,
                                                        bias=hbias[:, 4 + cg:5 + cg], scale=0.5),
                          reads=[pgb, b_misc], writes=[ii_b])
                    kb.op("act", lambda e: e.activation(out=a2[:], in_=rr[:], func=AF.Exp, bias=c1[:, cg:cg + 1],
                                                        scale=c1[:, cg:cg + 1]),
                          reads=[rr_b, b_misc], writes=[a2_b])
                    kb.op("act", lambda e: e.activation(out=rr[:], in_=rr[:], func=AF.Exp, bias=c1x2[:, cg:cg + 1],
                                                        scale=c1x2[:, cg:cg + 1]),
                          reads=[rr_b, b_misc], writes=[rr_b])
                    yield
                    kb.op("pool", lambda e: e.tensor_scalar(out=ii[:], in0=ii[:], scalar1=1.0, scalar2=1.0,
                                                            op0=ALU.add, op1=ALU.mult),
                          reads=[ii_b], writes=[ii_b])
                    kb.op("pool", lambda e: e.tensor_tensor(out=ii[:], in0=ii[:], in1=uc[:], op=ALU.mult),
                          reads=[ii_b, uc_b], writes=[ii_b])
                    yield
                    kb.op("act", lambda e: e.activation(out=a2[:], in_=a2[:], func=AF.Sqrt, bias=qc[:, 0:1], scale=-0.25),
                          reads=[a2_b, b_misc], writes=[a2_b])
                    yield
                    kb.op("dve", lambda e: e.tensor_tensor(out=ii[:], in0=ii[:], in1=a2[:], op=ALU.mult),
                          reads=[ii_b, a2_b], writes=[ii_b])
                    init = 0.0 if tc == 0 else cy[:, 0:1]
                    kb.op("dve", lambda e: e.tensor_tensor_scan(out=hs[:], data0=rr[:], data1=ii[:], initial=init,
                                                                op0=ALU.mult, op1=ALU.add),
                          reads=[rr_b, ii_b] + ([cy_b] if tc else []), writes=[hs_b])
                    yield
                    kb.op("pool", lambda e: e.tensor_copy(out=cy[:], in_=hs[:, 511:512]), reads=[hs_b], writes=[cy_b])
                    hv = hs[:].rearrange("p (b two t) -> p b two t", two=2, t=128)
                    lv3 = ls[:].rearrange("p (b t) -> p b t", t=128)
                    kb.op("pool", lambda e: e.tensor_scalar(out=lv3, in0=hv[:, :, 0, :], scalar1=pcol(PC_SEL),
                                                            scalar2=1.0, op0=ALU.mult, op1=ALU.mult),
                          reads=[hs_b, b_const], writes=[ls_b])
                    yield
                    kb.op("dve", lambda e: e.scalar_tensor_tensor(out=lv3, in0=hv[:, :, 1, :], scalar=pcol(PC_SEL + 1),
                                                                  in1=lv3, op0=ALU.mult, op1=ALU.add),
                          reads=[hs_b, ls_b, b_const], writes=[ls_b])
                    o0 = tc * 256
                    pq, pqb = next_bank(0, 8)
                    for c in range(8):
                        kb.op("pe", lambda e, c=c: e.matmul(pq[:, 0:256], lhsT=wg_b[:, c, cg * 128:(cg + 1) * 128],
                                                            rhs=hTo[:, c, o0:o0 + 256], start=(c == 0), stop=(c == 7)),
                              reads=[wg_bb, hTo_b], writes=[pqb], signal=(c == 7))
                    yield
                    kb.op("act", lambda e: e.activation(out=g2[:], in_=pq[:, 0:256], func=AF.Square),
                          reads=[pqb], writes=[g2_b])
                    yield
                    kb.op("pool", lambda e: e.tensor_scalar(out=g2[:], in0=g2[:], scalar1=0.044715, scalar2=1.0,
                                                            op0=ALU.mult, op1=ALU.add),
                          reads=[g2_b], writes=[g2_b])
                    kb.op("dve", lambda e: e.tensor_tensor(out=g2[:], in0=g2[:], in1=pq[:, 0:256], op=ALU.mult),
                          reads=[g2_b, pqb], writes=[g2_b])
                    yield
                    kb.op("act", lambda e: e.activation(out=gp[:], in_=g2[:], func=AF.Tanh, scale=0.5 * GELU_C),
                          reads=[g2_b], writes=[gp_b])
                    yield
                    kb.op("dve", lambda e: e.scalar_tensor_tensor(out=gp[:], in0=gp[:], scalar=1.0, in1=pq[:, 0:256],
                                                                  op0=ALU.add, op1=ALU.mult),
                          reads=[gp_b, pqb], writes=[gp_b])
                    kb.op("dve", lambda e: e.scalar_tensor_tensor(out=lru_mixT[:, cg, o0:o0 + 256], in0=gp[:], scalar=0.5,
                                                                  in1=ls[:], op0=ALU.mult, op1=ALU.mult),
                          reads=[gp_b, ls_b], writes=[b_lru])
                    yield

                items = [(cg, tc) for tc in range(8) for cg in range(4)]
                for r0 in range(0, len(items), NCH):
                    live = [lru_iter(cg, tc, sets[(r0 + i) % NCH]) for i, (cg, tc) in enumerate(items[r0:r0 + NCH])]
                    while live:
                        nxt = []
                        for g in live:
                            try:
                                next(g)
                                nxt.append(g)
                            except StopIteration:
                                pass
                        live = nxt
                if debug:
                    kb.dma(dbg["d_lru"], lru_mixT[:], d_const, reads=[b_lru])
                kb.barrier()

            with ExitStack() as sD:
                KT = [sb("KT0", [128, 2, T], BF, sD)] * 2
                KT_b = [kb.buf("KT")] * 2
                QT = [sb("QT0", [128, 2, TO], BF, sD)] * 2
                QT_b = [kb.buf("QT")] * 2
                aug_d = [kb.dsem(f"aug{i}") for i in range(3)]
                Vh2 = [sb(f"Vh{i}", [128, 32, 129], BF, sD) for i in range(2)]
                V_b2 = [kb.buf("V") for _ in range(2)]
                for i in range(2):
                    kb.op("pool", lambda e, i=i: e.memset(Vh2[i][:, :, 128:129], 1.0), writes=[V_b2[i]])
                pt = [sb(f"pt{i}", [128, 512], BF, sD) for i in range(3)]
                pt_b = [kb.buf("pt") for _ in range(3)]
                Osb = [sb(f"Osb{i}", [128, 4, 129], F32, sD) for i in range(2)]
                Osb_b = [kb.buf("Osb") for _ in range(2)]
                att = sb("att", [128, 4, 128], F32, sD)
                att_b = kb.buf("att")
                junk = sb("junk", [128, 128], F32, sD)
                junk_b = kb.buf("junk")
                sm = sb("sm", [128, 16], F32, sD)
                sm_b = kb.buf("sm")
                attb = sb("attb", [128, 4, 128], BF, sD)
                attb_b = kb.buf("attb")
                PO = [P[3], P[4], P[5], P[6]]
                PO_b = [Pb[3], Pb[4], Pb[5], Pb[6]]

                srot = [0]
                deferred = []

                def v_tasks(h, nb):
                    s = h % 2
                    w, wb = wqk_b[s], wqk_bb[s]
                    for blk in range(32):
                        if nb == 1:
                            pb, pbb = P[srot[0] % 3], Pb[srot[0] % 3]
                            srot[0] += 1
                        else:
                            pb, pbb = next_bank(0, nb)
                        for c in range(8):
                            kb.op("pe", lambda e, c=c: e.matmul(pb[:, 0:128], lhsT=hT[:, c, blk * 128:(blk + 1) * 128],
                                                                rhs=w[:, c, 256:384], start=(c == 0), stop=(c == 7)),
                                  reads=[wb, hT_b], writes=[pbb], signal=(c == 7))
                        if nb != 1 and blk % 2:
                            kb.op("act", lambda e: e.activation(out=Vh2[s][:, blk, 0:128], in_=pb[:, 0:128], func=AF.Copy),
                                  reads=[pbb], writes=[V_b2[s]])
                        else:
                            kb.op("dve", lambda e: e.tensor_copy(out=Vh2[s][:, blk, 0:128], in_=pb[:, 0:128]),
                                  reads=[pbb], writes=[V_b2[s]])
                        yield

                def k_proj(h, nb):
                    s = h % 2
                    w, wb = wqk_b[s], wqk_bb[s]
                    for m in range(2):
                        kb.dma(KT[s][64:68, m, :], kaug[h], aug_d[s], writes=[KT_b[s]])
                    for tc in range(8):
                        pb, pbb = next_bank(0, nb)
                        tsl = slice(tc * 512, (tc + 1) * 512)
                        for c in range(8):
                            kb.op("pe", lambda e, c=c: e.matmul(pb[:], lhsT=w[:, c, 128:256], rhs=hT[:, c, tsl],
                                                                start=(c == 0), stop=(c == 7)),
                                  reads=[wb, hT_b], writes=[pbb], signal=(c == 7))
                        kb.op("act", lambda e: e.activation(out=KT[s][0:64, 0, tsl], in_=pb[0:64, :], func=AF.Copy),
                              reads=[pbb], writes=[KT_b[s]])
                        kb.op("dve", lambda e: e.tensor_copy(out=KT[s][0:64, 1, tsl], in_=pb[64:128, :]),
                              reads=[pbb], writes=[KT_b[s]])

                def q_proj(h, nb):
                    s = h % 2
                    w, wb = wqk_b[s], wqk_bb[s]
                    for m in range(2):
                        kb.dma(QT[0][64:68, m, :], qaug[h], aug_d[2], writes=[QT_b[0]])
                    for tc in range(4):
                        pb, pbb = next_bank(0, nb)
                        tsl = slice(tc * 512, (tc + 1) * 512)
                        for c in range(8):
                            kb.op("pe", lambda e, c=c: e.matmul(pb[:], lhsT=w[:, c, 0:128], rhs=hTo[:, c, tsl],
                                                                start=(c == 0), stop=(c == 7)),
                                  reads=[wb, hTo_b], writes=[pbb], signal=(c == 7))
                        kb.op("act", lambda e: e.activation(out=QT[0][0:64, 0, tsl], in_=pb[0:64, :], func=AF.Copy,
                                                            scale=0.125),
                              reads=[pbb], writes=[QT_b[0]])
                        kb.op("dve", lambda e: e.tensor_scalar(out=QT[0][0:64, 1, tsl], in0=pb[64:128, :], scalar1=0.125,
                                                               scalar2=None, op0=ALU.mult),
                              reads=[pbb], writes=[QT_b[0]])

                for _ in v_tasks(0, 3):
                    pass
                k_proj(0, 3)
                q_proj(0, 3)
                for h in range(NH):
                    hs_ = h % 2
                    Vh, V_b = Vh2[hs_], V_b2[hs_]
                    nxt_tasks = None
                    if h + 1 < NH:
                        load_wqk(h + 1)
                        nxt_tasks = v_tasks(h + 1, 1)
                    tiles = [(I, m, kbk) for I in range(4) for m in range(2) for kbk in range(8 * I + 8)]

                    def geom(I, kbk):
                        i_min = max(4 * I, kbk // 2)
                        return i_min, (i_min - 4 * I) * 128, (kbk // 2) >= 4 * I

                    def emit_qk(t):
                        I, m, kbk = tiles[t]
                        i_min, col0, masked = geom(I, kbk)
                        sbank[t] = srot[0] % 3
                        srot[0] += 1
                        sps, spsb = P[sbank[t]], Pb[sbank[t]]
                        kT = KT[hs_][0:68, m, kbk * 128:(kbk + 1) * 128]
                        q0 = 4 * I * 128
                        rd = [KT_b[hs_], QT_b[hs_]]
                        if masked:
                            kb.op("pe", lambda e: e.matmul(sps[:, col0:col0 + 128], lhsT=kT,
                                                           rhs=QT[hs_][0:68, m, q0 + col0:q0 + col0 + 128],
                                                           start=True, stop=False),
                                  reads=rd, writes=[spsb], signal=False)
                            last = (col0 + 128 == 512)
                            kb.op("pe", lambda e: e.matmul(sps[:, col0:col0 + 128], lhsT=ident, rhs=maskAB[kbk % 2],
                                                           start=False, stop=True),
                                  reads=[b_const, b_sel], writes=[spsb], signal=last)
                            if not last:
                                kb.op("pe", lambda e: e.matmul(sps[:, col0 + 128:512], lhsT=kT,
                                                               rhs=QT[hs_][0:68, m, q0 + col0 + 128:q0 + 512],
                                                               start=True, stop=True),
                                      reads=rd, writes=[spsb])
                        else:
                            kb.op("pe", lambda e: e.matmul(sps[:, col0:512], lhsT=kT,
                                                           rhs=QT[hs_][0:68, m, q0 + col0:q0 + 512],
                                                           start=True, stop=True),
                                  reads=rd, writes=[spsb])

                    def emit_exp(t):
                        I, m, kbk = tiles[t]
                        i_min, col0, masked = geom(I, kbk)
                        sps, spsb = P[sbank[t]], Pb[sbank[t]]
                        ptt, pttb = pt[t % 3], pt_b[t % 3]
                        kb.op("act", lambda e: e.activation(out=ptt[:, col0:512], in_=sps[:, col0:512], func=AF.Exp),
                              reads=[spsb], writes=[pttb])

                    def emit_pv(t):
                        I, m, kbk = tiles[t]
                        i_min, col0, masked = geom(I, kbk)
                        ptt, pttb = pt[t % 3], pt_b[t % 3]
                        for i in range(i_min, 4 * I + 4):
                            qi = i - 4 * I
                            kb.op("pe", lambda e, qi=qi, i=i: e.matmul(
                                PO[qi][:, 0:129], lhsT=ptt[:, qi * 128:(qi + 1) * 128],
                                rhs=Vh[:, kbk, :], start=(kbk == 0), stop=(kbk == 2 * i + 1)),
                                  reads=[pttb, V_b], writes=[PO_b[qi]], signal=(i == 4 * I + 3))

                    sbank = {}
                    LA = 2
                    for d in deferred:
                        d[1]()
                    del deferred[:]
                    for t in range(min(LA, len(tiles))):
                        emit_qk(t)
                    for t in range(len(tiles)):
                        I, m, kbk = tiles[t]
                        emit_exp(t)
                        if t + LA < len(tiles):
                            emit_qk(t + LA)
                        emit_pv(t)
                        for d in list(deferred):
                            d[0] -= 1
                            if d[0] <= 0:
                                deferred.remove(d)
                                d[1]()
                        if nxt_tasks is not None and t % 3 == 2:
                            next(nxt_tasks, None)
                        if kbk != 8 * I + 7:
                            continue
                        for qi in range(4):
                            kb.op("dve", lambda e, qi=qi: e.tensor_copy(out=Osb[m][:, qi, :], in_=PO[qi][:, 0:129]),
                                  reads=[PO_b[qi]], writes=[Osb_b[m]])
                        if m != 1:
                            continue
                        kb.op("dve", lambda e: e.reciprocal(out=sm[:, 0:4], in_=Osb[0][:, :, 128]),
                              reads=[Osb_b[0]], writes=[sm_b])
                        kb.op("dve", lambda e: e.reciprocal(out=sm[:, 4:8], in_=Osb[1][:, :, 128]),
                              reads=[Osb_b[1], sm_b], writes=[sm_b])
                        kb.op("dve", lambda e: e.tensor_scalar(out=sm[:, 4:8], in0=sm[:, 4:8], scalar1=nlam[:, 0:1],
                                                               scalar2=None, op0=ALU.mult),
                              reads=[sm_b, b_misc], writes=[sm_b])
                        for qi in range(4):
                            kb.op("dve", lambda e, qi=qi: e.tensor_scalar(out=att[:, qi, :], in0=Osb[0][:, qi, 0:128],
                                                                          scalar1=sm[:, qi:qi + 1], scalar2=None,
                                                                          op0=ALU.mult),
                                  reads=[Osb_b[0], sm_b], writes=[att_b])
                            kb.op("dve", lambda e, qi=qi: e.scalar_tensor_tensor(
                                out=att[:, qi, :], in0=Osb[1][:, qi, 0:128], scalar=sm[:, 4 + qi:5 + qi],
                                in1=att[:, qi, :], op0=ALU.mult, op1=ALU.add),
                                  reads=[Osb_b[1], sm_b, att_b], writes=[att_b])
                            kb.op("dve", lambda e, qi=qi: e.scalar_tensor_tensor(
                                out=junk[:], in0=att[:, qi, :], scalar=1.0, in1=att[:, qi, :], op0=ALU.mult, op1=ALU.mult,
                                accum_out=sm[:, 8 + qi:9 + qi]),
                                  reads=[att_b, sm_b], writes=[junk_b, sm_b])
                        kb.op("dve", lambda e: e.tensor_scalar(out=sm[:, 12:16], in0=sm[:, 8:12],
                                                               scalar1=1.0 / (128 * (1.0 - LAMBDA_INIT) ** 2),
                                                               scalar2=EPS / (1.0 - LAMBDA_INIT) ** 2,
                                                               op0=ALU.mult, op1=ALU.add),
                              reads=[sm_b], writes=[sm_b])
                        kb.op("pool", lambda e: e.tensor_tensor(out=sm[:, 12:16], in0=sm[:, 12:16], in1=mhalf[:],
                                                                op=ALU.pow),
                              reads=[sm_b, b_misc], writes=[sm_b])
                        for qi in range(4):
                            kb.op("dve", lambda e, qi=qi: e.scalar_tensor_tensor(
                                out=attb[:, qi, :], in0=att[:, qi, :], scalar=sm[:, 12 + qi:13 + qi],
                                in1=pv[:, PC_SUBGR:PC_SUBGR + 128], op0=ALU.mult, op1=ALU.mult),
                                  reads=[att_b, sm_b, b_const], writes=[attb_b])
                        def finish(h=h, I=I):
                            for qi in range(4):
                                kb.op("pe", lambda e, qi=qi: e.transpose(out=PTbf[:, qi * 128:(qi + 1) * 128],
                                                                         in_=attb[:, qi, :], identity=ident),
                                      reads=[attb_b, b_const], writes=[PTb], signal=(qi == 3))
                            kb.op("dve", lambda e: e.tensor_copy(out=attn_mixT[:, h, I * 512:(I + 1) * 512],
                                                                 in_=PTbf[:, 0:512]),
                                  reads=[PTb], writes=[b_attn])
                        deferred.append([10, finish])
                        if t == len(tiles) - 1 and h + 1 < NH:
                            for _ in nxt_tasks:
                                pass
                            k_proj(h + 1, 3)
                            q_proj(h + 1, 3)
                            if h + 1 == NH - 1:
                                prefetch_x1()
                        if debug and h == 0 and I == 0:
                            kb.dma(dbg["d_kt"], KT[0][:, :, 0:512], d_const, reads=[KT_b[0]])
                            kb.dma(dbg["d_qt"], QT[0][:, :, 0:512], d_const, reads=[QT_b[0]])
                            kb.dma(dbg["d_v"], Vh[:, 0:4, :], d_const, reads=[V_b])
                            kb.dma(dbg["d_sm"], sm[:], d_const, reads=[sm_b])
                            kb.dma(dbg["d_osb"][:, 0], Osb[0][:], d_const, reads=[Osb_b[0]])
                            kb.dma(dbg["d_osb"][:, 1], Osb[1][:], d_const, reads=[Osb_b[1]])
                for d in deferred:
                    d[1]()
                del deferred[:]
                if debug:
                    kb.dma(dbg["d_attn"], attn_mixT[:], d_const, reads=[b_attn])
                kb.barrier()

            s2 = s1
            sq = sb("sqE", [128, 8, 512], BF, s2)
            sq_b = kb.buf("sq")
            rt = [sb(f"rtE{i}", [128, 512], F32, s2) for i in range(2)]
            rt_b = [kb.buf("rt") for _ in range(2)]
            wup_g = [sb(f"wup_g{i}", [128, 8, 512], BF, s2) for i in range(2)]
            wup_gb = [kb.buf("wup") for _ in range(2)]
            wdn_g = [sb(f"wdn_g{i}", [128, 4, D], BF, s2) for i in range(2)]
            wdn_gb = [kb.buf("wdn") for _ in range(2)]
            wqk_f32 = wqk_b[0][:].rearrange("p c n -> p (c n)").bitcast(F32)
            sqm = [wqk_f32[:, i * 512:(i + 1) * 512] for i in range(2)]
            sqm_b = [kb.buf("sqm") for _ in range(2)]

            def load_group(fg):
                s = fg % 2
                for c in range(8):
                    load_cast(wup_g[s][:, c, :], wup_v[:, c, fg * 512:(fg + 1) * 512], 512, pcol(PC_GMLP + c),
                              dst_buf=wup_gb[s])
                for j in range(4):
                    for hf in range(2):
                        load_cast(wdn_g[s][:, j, hf * 512:(hf + 1) * 512], wdn_v[:, fg, j, hf * 512:(hf + 1) * 512],
                                  512, None, dst_buf=wdn_gb[s])

            with ExitStack() as sE:
                wo_b = sb("wo_b", [128, 8, D], BF, sE)
                wo_bb = [kb.buf("wo") for _ in range(2)]
                for hf in range(2):
                    for c in range(8):
                        load_cast(wo_b[:, c, hf * 512:(hf + 1) * 512], wout_v[:, c, hf * 512:(hf + 1) * 512], 512,
                                  None, dst_buf=wo_bb[hf], cast_en=("pool" if c % 2 else "dve"))
                load_group(0)

                def e_part1(tt):
                    tsl = slice(tt * 512, (tt + 1) * 512)
                    kb.op("act", lambda e: e.activation(out=sq[:], in_=x1T[:, :, tsl], func=AF.Square),
                          reads=[x1_b[tt]], writes=[sq_b])

                def e_part2(tt):
                    s = tt % 2
                    tsl = slice(tt * 512, (tt + 1) * 512)
                    pb, pbb = next_bank(4, 7)
                    for c in range(8):
                        kb.op("pe", lambda e, c=c: e.matmul(pb[:], lhsT=ones_bf[:], rhs=sq[:, c, :],
                                                            start=(c == 0), stop=(c == 7)),
                              reads=[sq_b, b_misc], writes=[pbb], signal=(c == 7))
                    kb.op("act", lambda e: e.activation(out=rt[s][:], in_=pb[:], func=AF.Ln, bias=epsc[:, 0:1],
                                                        scale=1.0 / D),
                          reads=[pbb, b_misc], writes=[rt_b[s]])
                    kb.op("act", lambda e: e.activation(out=pb[:], in_=rt[s][:], func=AF.Exp, scale=-0.5),
                          reads=[rt_b[s]], writes=[pbb])
                    for c in range(8):
                        kb.op("dve", lambda e, c=c: e.tensor_tensor(out=hmT[:, c, tsl], in0=x1T[:, c, tsl], in1=pb[:],
                                                                    op=ALU.mult),
                              reads=[x1_b[tt], pbb], writes=[hm_b])

                for tt in range(4):
                    tsl = slice(tt * 512, (tt + 1) * 512)
                    for fc in range(8):
                        pb, pbb = next_bank(0, 4)
                        for kc in range(8):
                            rhs = attn_mixT[:, kc, tsl] if kc < 4 else lru_mixT[:, kc - 4, tsl]
                            kb.op("pe", lambda e, kc=kc, rhs=rhs: e.matmul(
                                pb[:], lhsT=wo_b[:, kc, fc * 128:(fc + 1) * 128], rhs=rhs,
                                start=(kc == 0), stop=(kc == 7)),
                                  reads=[wo_bb[fc // 4], b_attn, b_lru], writes=[pbb], signal=(kc == 7))
                        kb.op("dve", lambda e: e.tensor_tensor(out=x1T[:, fc, tsl], in0=pb[:], in1=x1T[:, fc, tsl],
                                                               op=ALU.add),
                              reads=[pbb, x1_b[tt]], writes=[x1_b[tt]])
                    if tt > 0:
                        e_part2(tt - 1)
                    e_part1(tt)
                e_part2(3)
                if debug:
                    kb.dma(dbg["d_x1"], x1T[:], d_const, reads=x1_b)

            with ExitStack() as sF:
                yb = [sb(f"yb{i}", [128, 8, 256], F32, sF) for i in range(2)]
                yb_b = [kb.buf("yb") for _ in range(2)]
                y_d = [kb.dsem(f"y{i}") for i in range(2)]

                def final_part1(tt):
                    tsl = slice(tt * 512, (tt + 1) * 512)
                    kb.op("act", lambda e: e.activation(out=sq[:], in_=x1T[:, :, tsl], func=AF.Square),
                          reads=[x1_b[tt]], writes=[sq_b])

                def final_part2(tt):
                    s = tt % 2
                    tsl = slice(tt * 512, (tt + 1) * 512)
                    pb, pbb = next_bank(0, 4)
                    for c in range(8):
                        kb.op("pe", lambda e, c=c: e.matmul(pb[:], lhsT=ones_bf[:], rhs=sq[:, c, :],
                                                            start=(c == 0), stop=(c == 7)),
                              reads=[sq_b, b_misc], writes=[pbb], signal=(c == 7))
                    kb.op("act", lambda e: e.activation(out=rt[s][:], in_=pb[:], func=AF.Ln, bias=epsc[:, 0:1],
                                                        scale=1.0 / D),
                          reads=[pbb, b_misc], writes=[rt_b[s]])
                    kb.op("act", lambda e: e.activation(out=rt[s][:], in_=rt[s][:], func=AF.Exp, scale=-0.5),
                          reads=[rt_b[s]], writes=[rt_b[s]])
                    for hf in range(2):
                        hsl = slice(tt * 512 + hf * 256, tt * 512 + (hf + 1) * 256)
                        for c in range(8):
                            kb.op("dve", lambda e, c=c: e.scalar_tensor_tensor(
                                out=yb[hf][:, c, :], in0=x1T[:, c, hsl], scalar=pcol(PC_GFIN + c),
                                in1=rt[s][:, hf * 256:(hf + 1) * 256], op0=ALU.mult, op1=ALU.mult),
                                  reads=[x1_b[tt], rt_b[s], b_const], writes=[yb_b[hf]])
                        kb.dma(yT_v[:, :, hsl], yb[hf][:], y_d[hf], reads=[yb_b[hf]])

                actT = [attn_mixT, lru_mixT]
                act_b = [b_attn, b_lru]
                ui = 0
                di = 0
                for fg in range(8):
                    s = fg % 2
                    if fg + 1 < 8:
                        load_group(fg + 1)
                    for tt in range(4):
                        tsl = slice(tt * 512, (tt + 1) * 512)
                        for j in range(4):
                            pb, pbb = P[ui % 4], Pb[ui % 4]
                            q, qb_ = sqm[ui % 2], sqm_b[ui % 2]
                            ui += 1
                            for c in range(8):
                                kb.op("pe", lambda e, c=c: e.matmul(pb[:], lhsT=wup_g[s][:, c, j * 128:(j + 1) * 128],
                                                                    rhs=hmT[:, c, tsl], start=(c == 0), stop=(c == 7)),
                                      reads=[wup_gb[s], hm_b], writes=[pbb], signal=(c == 7))
                            kb.op("act", lambda e: e.activation(out=q[:], in_=pb[:], func=AF.Square),
                                  reads=[pbb], writes=[qb_])
                            kb.op("dve", lambda e: e.scalar_tensor_tensor(out=actT[s][:, j, tsl], in0=pb[:], scalar=0.0,
                                                                          in1=q[:], op0=ALU.is_gt, op1=ALU.mult),
                                  reads=[pbb, qb_], writes=[act_b[s]])
                    for tt in range(4):
                        tsl = slice(tt * 512, (tt + 1) * 512)
                        for fc in range(8):
                            pb, pbb = P[4 + di % 3], Pb[4 + di % 3]
                            di += 1
                            for j in range(4):
                                kb.op("pe", lambda e, j=j: e.matmul(pb[:], lhsT=wdn_g[s][:, j, fc * 128:(fc + 1) * 128],
                                                                    rhs=actT[s][:, j, tsl], start=(j == 0), stop=(j == 3)),
                                      reads=[wdn_gb[s], act_b[s]], writes=[pbb], signal=(j == 3))
                            kb.op("dve", lambda e: e.tensor_tensor(out=x1T[:, fc, tsl], in0=pb[:], in1=x1T[:, fc, tsl],
                                                                   op=ALU.add),
                                  reads=[pbb, x1_b[tt]], writes=[x1_b[tt]])
                            if fg == 7 and fc == 2 and tt > 0:
                                final_part2(tt - 1)
                        if fg == 7:
                            final_part1(tt)
                if True:
                    final_part2(3)
                if debug:
                    kb.dma(dbg["d_x2"], x1T[:], d_const, reads=x1_b)

            kb.barrier()
            spE = kb.E["sp"]
            for d in y_d:
                spE.eng.wait_ge(d.sem, d.count)
    return nc


_NC_CACHE = {}


def _slopes():
    return np.array([2.0 ** (-8.0 * (i + 1) / NH) for i in range(NH)], dtype=np.float32)


def _const_tables(j):
    bf = ml_dtypes.bfloat16
    ident = np.eye(128, dtype=np.float32)
    kk = np.arange(128)[:, None]
    qq = np.arange(128)[None, :]
    tri = np.where(kk > qq, NEG, 0.0).astype(np.float32)
    full = np.full((128, 128), NEG, np.float32)
    zero = np.zeros((128, 128), np.float32)
    mA, mB = (tri, full) if j == 0 else (zero, tri)
    s0 = ident * (1.0 if j == 0 else 0.0)
    s1 = ident * (0.0 if j == 0 else 1.0)
    cbf = np.concatenate([ident, mA, mB, s0, s1], axis=1).astype(bf)
    sl = _slopes()
    kpos = np.arange(T)
    kaug = np.zeros((NH, 4, T), np.float32)
    qaug = np.zeros((NH, 4, TO), np.float32)
    own = (np.arange(TO) // 128) * 256 + j * 128 + (np.arange(TO) % 128)
    for h in range(NH):
        kaug[h, 0] = sl[h] * (kpos % 128)
        kaug[h, 1] = sl[h] * (kpos - kpos % 128)
        kaug[h, 2] = 1.0
        kaug[h, 3] = 1.0
        qaug[h, 0] = 1.0
        qaug[h, 1] = 1.0
        qaug[h, 2] = -sl[h] * (own % 128)
        qaug[h, 3] = -sl[h] * (own - own % 128)
    return cbf, kaug.astype(bf), qaug.astype(bf), own


def _chunked(v):
    return np.ascontiguousarray(np.asarray(v, np.float32).reshape(8, 128).T)


def _prep_shared(inp):
    f = lambda k: np.asarray(inp[k], np.float32)
    w_in = f("w_in")[0]
    wq, wk, wvv = w_in[:, 0:512], w_in[:, 512:1024], w_in[:, 1024:1536]
    wqkv = np.stack([np.concatenate([wq[:, h * 128:(h + 1) * 128], wk[:, h * 128:(h + 1) * 128],
                                     wvv[:, h * 128:(h + 1) * 128]], axis=1) for h in range(NH)], 0)
    sh = {
        "wqkv": np.ascontiguousarray(wqkv),
        "wu": np.ascontiguousarray(w_in[:, 1536:2048]),
        "wg": np.ascontiguousarray(w_in[:, 2048:2560]),
        "wout": np.ascontiguousarray(f("w_out")[0]),
        "wup": np.ascontiguousarray(f("w_up")[0]),
        "wdn": np.ascontiguousarray(f("w_down")[0]),
    }
    pv = np.zeros((128, NPC), np.float32)
    pv[:, PC_GMIX:PC_GMIX + 8] = _chunked(f("norm_mix_g")[0])
    pv[:, PC_GMLP:PC_GMLP + 8] = _chunked(f("norm_mlp_g")[0])
    pv[:, PC_GFIN:PC_GFIN + 8] = _chunked(f("final_g"))
    cw = f("conv_w")[0]
    for cg in range(4):
        for k in range(4):
            pv[:, PC_CW + cg * 4 + k] = cw[k, cg * 128:(cg + 1) * 128]
        pv[:, PC_CB + cg] = f("conv_b")[0][cg * 128:(cg + 1) * 128]
        pv[:, PC_BRG + cg] = f("b_rg")[0].reshape(512)[cg * 128:(cg + 1) * 128]
        pv[:, PC_BIG + cg] = f("b_ig")[0].reshape(512)[cg * 128:(cg + 1) * 128]
        pv[:, PC_L + cg] = f("lru_L")[0][cg * 128:(cg + 1) * 128]
    pv[:, PC_SUBG] = f("subln_g")[0]
    pv[:, PC_SUBGR:PC_SUBGR + 128] = f("subln_g")[0][None, :]
    lam = np.stack([f("lambda_q1")[0], f("lambda_k1")[0], f("lambda_q2")[0], f("lambda_k2")[0]], 0).reshape(256)
    pv[:, PC_LAM:PC_LAM + 256] = lam[None, :]
    wgate = np.zeros((128, 8, 128), np.float32)
    wrg, wig = f("w_rg")[0], f("w_ig")[0]
    for cg in range(4):
        for nl in range(2):
            n = cg * 2 + nl
            wgate[nl * 64:(nl + 1) * 64, cg * 2 + 0, nl * 64:(nl + 1) * 64] = wrg[n]
            wgate[nl * 64:(nl + 1) * 64, cg * 2 + 1, nl * 64:(nl + 1) * 64] = wig[n]
    sh["wgate"] = wgate
    return sh, pv


def _in_maps(inp):
    x = np.asarray(inp["x"], np.float32)
    sh, pv = _prep_shared(inp)
    maps, owns = [], []
    for core in range(8):
        b, j = core // 2, core % 2
        cbf, kaug, qaug, own = _const_tables(j)
        p = pv.copy()
        p[:, PC_SEL] = 1.0 if j == 0 else 0.0
        p[:, PC_SEL + 1] = 0.0 if j == 0 else 1.0
        xT = np.ascontiguousarray(x[b].T)
        m = dict(sh)
        m.update({"xf": xT, "xo": np.ascontiguousarray(xT[:, own]), "pvec": p, "cbf": cbf, "kaug": kaug,
                  "qaug": qaug})
        maps.append(m)
        owns.append(own)
    return maps, owns


def kernel(**inputs):
    if "nc" not in _NC_CACHE:
        _NC_CACHE["nc"] = build_program()
    nc = _NC_CACHE["nc"]
    maps, owns = _in_maps(inputs)
    res = run_bass_kernel_spmd(nc, maps, core_ids=list(range(8)))
    B, S = inputs["x"].shape[0], inputs["x"].shape[1]
    out = np.empty((B, S, D), np.float32)
    for core in range(8):
        b = core // 2
        out[b, owns[core], :] = np.asarray(res.results[core]["yT"], np.float32).T
    return out
```

```python
import math
from contextlib import ExitStack

import numpy as np
import ml_dtypes
import concourse.bass as bass
import concourse.mybir as mybir
from concourse.bass_utils import run_bass_kernel_spmd

F32 = mybir.dt.float32
BF = mybir.dt.bfloat16
AF = mybir.ActivationFunctionType
ALU = mybir.AluOpType

D = 1024
T = 4096
TO = 2048
NH = 4
EPS = 1e-6
LAMBDA_INIT = 0.8 - 0.6 * math.exp(0.0)
GELU_C = 2.0 * math.sqrt(2.0 / math.pi)
NEG = -30000.0

PC_GMIX, PC_GMLP, PC_GFIN = 0, 8, 16
PC_CW, PC_CB, PC_BRG, PC_BIG, PC_L = 24, 40, 44, 48, 52
PC_SUBG, PC_SEL, PC_LAM = 56, 57, 59
PC_SUBGR = 59 + 256
NPC = 59 + 256 + 128


class Buf:
    __slots__ = ("name", "w", "r", "pr")

    def __init__(self, name):
        self.name = name
        self.w = {}
        self.r = {}
        self.pr = {}


class Eng:
    def __init__(self, name, eng, sem):
        self.name, self.eng, self.sem = name, eng, sem
        self.count = 0
        self.seen = {}


class DSem:
    def __init__(self, name, sem):
        self.name, self.sem = name, sem
        self.count = 0


class KB:
    def __init__(self, nc, es):
        self.nc = nc
        self.es = es
        self.E = {}
        for name, eng in (("pe", nc.tensor), ("act", nc.scalar), ("dve", nc.vector),
                          ("pool", nc.gpsimd), ("sp", nc.sync)):
            self.E[name] = Eng(name, eng, es.enter_context(nc.semaphore("s_" + name)))
        self.dsems = []
        self.nbuf = 0

    def dsem(self, name):
        d = DSem(name, self.es.enter_context(self.nc.semaphore("d_" + name)))
        self.dsems.append(d)
        return d

    def buf(self, name="b"):
        self.nbuf += 1
        return Buf(f"{name}{self.nbuf}")

    def _wait(self, E, st, raw):
        src, val = st
        if src is E and (E.name in ("pe", "sp") or not raw):
            return
        if isinstance(src, DSem):
            val = src.count
        key = id(src)
        if E.seen.get(key, 0) >= val:
            return
        E.eng.wait_ge(src.sem, val)
        E.seen[key] = val

    def _deps(self, E, reads, writes):
        for b in reads:
            for st in b.w.values():
                self._wait(E, st, True)
        for b in writes:
            for st in b.r.values():
                self._wait(E, st, False)
            for st in b.pr.values():
                self._wait(E, st, False)

    def _post(self, src, st, reads, writes):
        for b in reads:
            b.r[id(src)] = st
        for b in writes:
            if b.r:
                b.pr = b.r
                b.w = {}
                b.r = {}
            b.w[id(src)] = st

    def op(self, en, fn, reads=(), writes=(), signal=True):
        E = self.E[en]
        self._deps(E, reads, writes)
        ins = fn(E.eng)
        if signal:
            ins.then_inc(E.sem, 1)
            E.count += 1
            st = (E, E.count)
        else:
            st = (E, E.count + 1)
        self._post(E, st, reads, writes)
        return ins

    def dma(self, out, in_, dsem, reads=(), writes=(), en="sp"):
        E = self.E[en]
        self._deps(E, reads, writes)
        ins = E.eng.dma_start(out=out, in_=in_)
        ins.then_inc(dsem.sem, 16)
        dsem.count += 16
        self._post(dsem, (dsem, dsem.count), reads, writes)

    def barrier(self):
        for E in self.E.values():
            for F in self.E.values():
                if F is not E and F.count > 0:
                    self._wait(E, (F, F.count), True)
            for d in self.dsems:
                if d.count > 0:
                    self._wait(E, (d, d.count), True)


def build_program(debug=False):
    nc = bass.Bass("TRN2", target_bir_lowering=False)
    dbg = {}
    if debug:
        for nm, shp, dt in (("d_hT", [128, 8, 512], BF), ("d_hTo", [128, 8, 512], BF), ("d_lru", [128, 4, TO], BF),
                            ("d_attn", [128, 4, TO], BF), ("d_x1", [128, 8, TO], F32), ("d_x2", [128, 8, TO], F32),
                            ("d_kt", [128, 2, 512], BF), ("d_qt", [128, 2, 512], BF), ("d_v", [128, 4, 129], BF),
                            ("d_sm", [128, 16], F32), ("d_osb", [128, 2, 4, 129], F32)):
            dbg[nm] = nc.dram_tensor(nm, shp, dt, kind="ExternalOutput").ap()

    def din(name, shape, dt=F32):
        return nc.dram_tensor(name, list(shape), dt, kind="ExternalInput").ap()

    xf = din("xf", [D, T])
    xo = din("xo", [D, TO])
    wqkv = din("wqkv", [NH, D, 384])
    wu = din("wu", [D, 512])
    wg = din("wg", [D, 512])
    wout = din("wout", [D, D])
    wup = din("wup", [D, 4 * D])
    wdn = din("wdn", [4 * D, D])
    pvec = din("pvec", [128, NPC])
    wgate = din("wgate", [128, 8, 128])
    cbf = din("cbf", [128, 640], BF)
    kaug = din("kaug", [NH, 4, T], BF)
    qaug = din("qaug", [NH, 4, TO], BF)
    yT = nc.dram_tensor("yT", [D, TO], F32, kind="ExternalOutput").ap()

    xf_v = xf.rearrange("(c p) t -> p c t", p=128)
    xo_v = xo.rearrange("(c p) t -> p c t", p=128)
    yT_v = yT.rearrange("(c p) t -> p c t", p=128)
    wu_v = wu.rearrange("(c p) n -> p c n", p=128)
    wg_v = wg.rearrange("(c p) n -> p c n", p=128)
    wout_v = wout.rearrange("(c p) n -> p c n", p=128)
    wup_v = wup.rearrange("(c p) n -> p c n", p=128)
    wdn_v = wdn.rearrange("(g j p) n -> p g j n", j=4, p=128)

    with ExitStack() as es:
        kb = KB(nc, es)

        def sb(name, shape, dt, stack=es):
            return stack.enter_context(nc.sbuf_tensor(name, list(shape), dt))

        def ps(name, shape, dt=F32, stack=es):
            return stack.enter_context(nc.psum_tensor(name, list(shape), dt))

        pv = sb("pv", [128, NPC], F32)
        cb_sb = sb("cb_sb", [128, 384], BF)
        ones_bf = sb("ones_bf", [128, 128], BF)
        epsc = sb("epsc", [128, 1], F32)
        onec = sb("onec", [128, 1], F32)
        c1 = sb("c1", [128, 4], F32)
        c1x2 = sb("c1x2", [128, 4], F32)
        hbias = sb("hbias", [128, 8], F32)
        qc = sb("qc", [128, 1], F32)
        mhalf = sb("mhalf", [128, 4], F32)
        nlam = sb("nlam", [128, 1], F32)
        lsum = sb("lsum", [128, 4], F32)
        wgate_b = sb("wgate_b", [128, 8, 128], BF)
        attn_mixT = sb("attn_mixT", [128, 4, TO], BF)
        lru_mixT = sb("lru_mixT", [128, 4, TO], BF)
        wst = [sb(f"wst{i}", [128, 512], F32) for i in range(4)]
        wst_b = [kb.buf("wst") for _ in range(4)]
        wst_d = [kb.dsem(f"wst{i}") for i in range(4)]
        wst_i = [0]

        b_lru = kb.buf("lru")
        b_attn = kb.buf("attn")
        ident = cb_sb[:, 0:128]
        maskAB = [cb_sb[:, 128:256], cb_sb[:, 256:384]]

        P = [ps(f"P{i}", [128, 512]) for i in range(8)]
        Pb = [kb.buf("P") for _ in range(8)]
        PT, PTb = P[7], Pb[7]
        PTbf = P[7][:].bitcast(BF)
        ident_f = sb("ident_f", [128, 128], F32)

        s0 = ExitStack()
        with s0:
            ltmp = sb("ltmp", [128, 4, 64], F32, s0)
            wgate_f = sb("wgate_f", [128, 8, 128], F32, s0)
            b_const = kb.buf("const")
            d_const = kb.dsem("const")
            kb.dma(pv[:], pvec[:, :], d_const, writes=[b_const])
            b_sel = kb.buf("sel")
            d_sel = kb.dsem("sel")
            kb.dma(cb_sb[:, 0:128], cbf[:, 0:128], d_const, writes=[kb.buf()])
            kb.dma(cb_sb[:, 128:384], cbf[:, 384:640], d_sel, writes=[b_sel])
            kb.dma(wgate_f[:], wgate[:, :, :], d_const, writes=[kb.buf()])
            b_const.w = {id(d_const): (d_const, d_const.count)}

            def pcol(c, n=1):
                return pv[:, c:c + n]

            b_misc = kb.buf("misc")
            kb.op("pool", lambda e: e.memset(ones_bf[:], 1.0), writes=[b_misc])
            kb.op("pool", lambda e: e.tensor_copy(out=ident_f[:], in_=cb_sb[:, 0:128]), reads=[b_const], writes=[b_misc])
            kb.op("pool", lambda e: e.memset(epsc[:], EPS), writes=[b_misc])
            kb.op("pool", lambda e: e.memset(onec[:], 1.0), writes=[b_misc])
            kb.op("pool", lambda e: e.tensor_copy(out=wgate_b[:], in_=wgate_f[:]), reads=[b_const], writes=[b_misc])
            kb.op("act", lambda e: e.activation(out=c1[:], in_=pcol(PC_L, 4), func=AF.Exp, scale=-1.0),
                  reads=[b_const], writes=[b_misc])
            kb.op("act", lambda e: e.activation(out=c1[:], in_=c1[:], func=AF.Ln, bias=onec[:, 0:1], scale=1.0),
                  reads=[b_misc], writes=[b_misc])
            kb.op("dve", lambda e: e.tensor_scalar(out=c1x2[:], in0=c1[:], scalar1=-4.0, scalar2=None, op0=ALU.mult),
                  reads=[b_misc], writes=[b_misc])
            kb.op("dve", lambda e: e.tensor_scalar(out=c1[:], in0=c1[:], scalar1=-8.0, scalar2=None, op0=ALU.mult),
                  reads=[b_misc], writes=[b_misc])
            kb.op("dve", lambda e: e.tensor_scalar(out=hbias[:], in0=pcol(PC_BRG, 8), scalar1=0.5, scalar2=None,
                                                   op0=ALU.mult),
                  reads=[b_const], writes=[b_misc])
            kb.op("pool", lambda e: e.memset(qc[:], 0.25), writes=[b_misc])
            kb.op("pool", lambda e: e.memset(mhalf[:], -0.5), writes=[b_misc])
            lv = pv[:, PC_LAM:PC_LAM + 256].rearrange("p (a d) -> p a d", d=64)
            kb.op("dve", lambda e: e.tensor_tensor(out=ltmp[:, 0, :], in0=lv[:, 0, :], in1=lv[:, 1, :], op=ALU.mult),
                  reads=[b_const], writes=[b_misc])
            kb.op("dve", lambda e: e.tensor_tensor(out=ltmp[:, 1, :], in0=lv[:, 2, :], in1=lv[:, 3, :], op=ALU.mult),
                  reads=[b_const], writes=[b_misc])
            kb.op("dve", lambda e: e.reduce_sum(out=lsum[:, 0:2], in_=ltmp[:, 0:2, :], axis=mybir.AxisListType.X),
                  reads=[b_misc], writes=[b_misc])
            kb.op("act", lambda e: e.activation(out=lsum[:, 2:4], in_=lsum[:, 0:2], func=AF.Exp),
                  reads=[b_misc], writes=[b_misc])
            kb.op("dve", lambda e: e.scalar_tensor_tensor(out=nlam[:], in0=lsum[:, 3:4], scalar=-LAMBDA_INIT,
                                                          in1=lsum[:, 2:3], op0=ALU.add, op1=ALU.subtract),
                  reads=[b_misc], writes=[b_misc])


        def load_cast(dst, src, n, scale_col=None, cast_en="pool", dst_buf=None):
            i = wst_i[0] % 4
            wst_i[0] += 1
            st, stb, std = wst[i], wst_b[i], wst_d[i]
            kb.dma(st[:, 0:n], src, std, writes=[stb])
            wr = [dst_buf] if dst_buf is not None else []
            if scale_col is None:
                if cast_en == "pool":
                    kb.op("pool", lambda e: e.tensor_scalar(out=dst, in0=st[:, 0:n], scalar1=1.0, scalar2=1.0,
                                                            op0=ALU.mult, op1=ALU.mult),
                          reads=[stb, b_const], writes=wr)
                else:
                    kb.op(cast_en, lambda e: e.tensor_copy(out=dst, in_=st[:, 0:n]), reads=[stb, b_const], writes=wr)
            else:
                kb.op(cast_en, lambda e: e.tensor_scalar(out=dst, in0=st[:, 0:n], scalar1=scale_col, scalar2=1.0,
                                                         op0=ALU.mult, op1=ALU.mult),
                      reads=[stb, b_const], writes=wr)

        pi = [0]

        def next_bank(lo=0, hi=4):
            i = lo + pi[0] % (hi - lo)
            pi[0] += 1
            return P[i], Pb[i]

        def rms_tile(stack_bufs, xt, xt_b, dstT, dst_b, t0, n=512):
            sq, sq_b, rt, rt_b = stack_bufs
            kb.op("act", lambda e: e.activation(out=sq[:, :, 0:n], in_=xt[:, :, 0:n], func=AF.Square),
                  reads=[xt_b], writes=[sq_b])
            pb, pbb = next_bank()
            for c in range(8):
                kb.op("pe", lambda e, c=c: e.matmul(pb[:, 0:n], lhsT=ones_bf[:], rhs=sq[:, c, 0:n],
                                                    start=(c == 0), stop=(c == 7)),
                      reads=[sq_b, b_misc], writes=[pbb], signal=(c == 7))
            if dstT is None:
                kb.op("act", lambda e: e.activation(out=rt[:, 0:n], in_=pb[:, 0:n], func=AF.Sqrt,
                                                    bias=epsc[:, 0:1], scale=1.0 / D),
                      reads=[pbb, b_misc], writes=[rt_b])
                kb.op("dve", lambda e: e.reciprocal(out=rt[:, 0:n], in_=rt[:, 0:n]), reads=[rt_b], writes=[rt_b])
            else:
                kb.op("act", lambda e: e.activation(out=rt[:, 0:n], in_=pb[:, 0:n], func=AF.Ln,
                                                    bias=epsc[:, 0:1], scale=1.0 / D),
                      reads=[pbb, b_misc], writes=[rt_b])
                kb.op("act", lambda e: e.activation(out=pb[:, 0:n], in_=rt[:, 0:n], func=AF.Exp, scale=-0.5),
                      reads=[rt_b], writes=[pbb])
                for c in range(8):
                    kb.op("dve", lambda e, c=c: e.tensor_tensor(out=dstT[:, c, t0:t0 + n], in0=xt[:, c, 0:n],
                                                                in1=pb[:, 0:n], op=ALU.mult),
                          reads=[xt_b, pbb, b_misc], writes=[dst_b])

        with ExitStack() as s1:
            hT = sb("hT", [128, 8, T], BF, s1)
            hTo = sb("hTo", [128, 8, TO], BF, s1)
            hT_b = kb.buf("hT")
            hTo_b = kb.buf("hTo")
            wqk_b = [sb("wqk_b0", [128, 8, 384], BF, s1)] * 2
            wqk_bb = [kb.buf("wqk")] * 2

            def load_wqk(h):
                for c in range(8):
                    load_cast(wqk_b[h % 2][:, c, :], wqkv[h].rearrange("(c p) n -> p c n", p=128)[:, c, :], 384,
                              pcol(PC_GMIX + c), dst_buf=wqk_bb[h % 2])
            x1T = hT[:].bitcast(F32)
            hmT = hTo
            hm_b = hTo_b
            x1_b = [kb.buf("x1") for _ in range(4)]
            x1_d = [kb.dsem(f"x1_{i}") for i in range(4)]

            def prefetch_x1():
                for tt in range(4):
                    kb.dma(x1T[:, :, tt * 512:(tt + 1) * 512], xo_v[:, :, tt * 512:(tt + 1) * 512], x1_d[tt],
                           writes=[x1_b[tt], hT_b])

            with ExitStack() as sA:
                NXS = 3
                xst = [sb(f"xst{i}", [128, 8, 512], F32, sA) for i in range(NXS)]
                xst_b = [kb.buf("xst") for _ in range(NXS)]
                xst_d = [kb.dsem(f"xst{i}") for i in range(NXS)]
                sq = sb("sqA", [128, 8, 512], BF, sA)
                sq_b = kb.buf("sq")
                rt = [sb(f"rtA{i}", [128, 512], F32, sA) for i in range(2)]
                rt_b = [kb.buf("rt") for _ in range(2)]
                selm = [cb_sb[:, 128:256], cb_sb[:, 256:384]]
                hT_t = [kb.buf("hTt") for _ in range(8)]

                def select_own(it):
                    for b in range(2):
                        ev = it * 512 + b * 256
                        ob = 2 * it + b
                        for ch in range(2):
                            pb, pbb = next_bank(4, 8)
                            pv3 = pb[:].rearrange("p (c t) -> p c t", t=128)
                            kb.op("pe", lambda e: e.matmul(pv3, lhsT=selm[0], rhs=hT[:, 4 * ch:4 * ch + 4, ev:ev + 128],
                                                           start=True, stop=False),
                                  reads=[hT_t[it], b_sel], writes=[pbb], signal=False)
                            kb.op("pe", lambda e: e.matmul(pv3, lhsT=selm[1],
                                                           rhs=hT[:, 4 * ch:4 * ch + 4, ev + 128:ev + 256],
                                                           start=False, stop=True),
                                  reads=[hT_t[it], b_sel], writes=[pbb])
                            kb.op("act", lambda e: e.activation(out=hTo[:, 4 * ch:4 * ch + 4, ob * 128:(ob + 1) * 128],
                                                                in_=pv3, func=AF.Copy),
                                  reads=[pbb], writes=[hTo_b])

                for it in range(8):
                    s = it % NXS
                    t0 = it * 512
                    kb.dma(xst[s][:], xf_v[:, :, t0:t0 + 512], xst_d[s], writes=[xst_b[s]])
                    rms_tile((sq, sq_b, rt[it % 2], rt_b[it % 2]), xst[s][:], xst_b[s], hT, hT_t[it], t0)
                    if it >= 2:
                        select_own(it - 2)
                select_own(6)
                select_own(7)
                kb.dma(cb_sb[:, 128:384], cbf[:, 128:384], d_sel, writes=[b_sel])
                if debug:
                    kb.dma(dbg["d_hT"], hT[:, :, 0:512], d_const, reads=[hT_b])
                    kb.dma(dbg["d_hTo"], hTo[:, :, 0:512], d_const, reads=[hTo_b])
                kb.barrier()

            with ExitStack() as sB:
                wu_b = sb("wu_b", [128, 8, 512], BF, sB)
                wg_b = sb("wg_b", [128, 8, 512], BF, sB)
                wu_bb, wg_bb = kb.buf("wu"), kb.buf("wg")
                for c in range(8):
                    load_cast(wu_b[:, c, :], wu_v[:, c, :], 512, pcol(PC_GMIX + c), dst_buf=wu_bb,
                              cast_en=("pool" if c % 2 else "dve"))
                for c in range(8):
                    load_cast(wg_b[:, c, :], wg_v[:, c, :], 512, pcol(PC_GMIX + c), dst_buf=wg_bb,
                              cast_en=("pool" if c % 2 else "dve"))
                load_wqk(0)
                NCH = 4

                def mkset(i):
                    S = {}
                    for nm, shp, dt in (("ub", [128, 516], F32), ("uc", [128, 512], F32), ("ucbf", [128, 256], F32),
                                        ("rr", [128, 512], F32), ("ii", [128, 512], F32), ("a2", [128, 512], F32)):
                        S[nm] = sb(f"{nm}_{i}", shp, dt, sB)
                        S[nm + "_b"] = kb.buf(nm)
                    S["ucb"], S["ucb_b"] = S["ucbf"][:].bitcast(BF), S["ucbf_b"]
                    S["g2"], S["g2_b"] = S["ucbf"][:], S["ucbf_b"]
                    S["hs"], S["hs_b"] = S["uc"][:], S["uc_b"]
                    S["ls"], S["ls_b"] = S["a2"][:, 0:256], S["a2_b"]
                    S["gp"], S["gp_b"] = S["ub"][:, 4:260], S["ub_b"]
                    for k in ("ub", "uc", "rr", "ii", "a2"):
                        S[k] = S[k][:]
                    return S

                sets = [mkset(i) for i in range(NCH)]
                halo = [sb(f"halo{i}", [128, 4], F32, sB) for i in range(4)]
                halo_b = [kb.buf("halo") for _ in range(4)]
                cyc = [halo[i][:, 3:4] for i in range(4)]
                cyc_b = [kb.buf("cyc") for _ in range(4)]

                def lru_iter(cg, tc, S):
                    cwc = lambda k: pcol(PC_CW + cg * 4 + k)
                    ub, uc, ucb, rr, ii, a2, hs, ls, g2, gp = (S[k] for k in
                        ("ub", "uc", "ucb", "rr", "ii", "a2", "hs", "ls", "g2", "gp"))
                    ub_b, uc_b, ucb_b, rr_b, ii_b, a2_b, hs_b, ls_b, g2_b, gp_b = (S[k + "_b"] for k in
                        ("ub", "uc", "ucb", "rr", "ii", "a2", "hs", "ls", "g2", "gp"))
                    cy, cy_b = cyc[cg], cyc_b[cg]
                    t0 = tc * 512
                    pu, pub = next_bank(0, 8)
                    for c in range(8):
                        kb.op("pe", lambda e, c=c: e.matmul(pu[:], lhsT=wu_b[:, c, cg * 128:(cg + 1) * 128],
                                                            rhs=hT[:, c, t0:t0 + 512], start=(c == 0), stop=(c == 7)),
                              reads=[wu_bb, hT_b], writes=[pub], signal=(c == 7))
                    if tc == 0:
                        kb.op("pool", lambda e: e.memset(halo[cg][:, 0:3], 0.0), writes=[halo_b[cg]])
                    yield
                    cbc = pcol(PC_CB + cg)
                    kb.op("act", lambda e: e.activation(out=uc[:, 3:512], in_=pu[:, 0:509], func=AF.Identity,
                                                        bias=cbc, scale=cwc(0)),
                          reads=[pub, b_const], writes=[uc_b])
                    kb.op("act", lambda e: e.activation(out=uc[:, 0:3], in_=halo[cg][:, 0:3], func=AF.Identity,
                                                        bias=cbc, scale=cwc(0)),
                          reads=[halo_b[cg], b_const], writes=[uc_b])
                    yield
                    for k in range(1, 4):
                        kb.op("dve", lambda e, k=k: e.scalar_tensor_tensor(
                            out=uc[:, 3 - k:512], in0=pu[:, 0:509 + k], scalar=cwc(k), in1=uc[:, 3 - k:512],
                            op0=ALU.mult, op1=ALU.add),
                              reads=[pub, uc_b, b_const], writes=[uc_b])
                        if k < 3:
                            kb.op("dve", lambda e, k=k: e.scalar_tensor_tensor(
                                out=uc[:, 0:3 - k], in0=halo[cg][:, k:3], scalar=cwc(k), in1=uc[:, 0:3 - k],
                                op0=ALU.mult, op1=ALU.add),
                                  reads=[halo_b[cg], uc_b, b_const], writes=[uc_b])
                    kb.op("dve", lambda e: e.tensor_copy(out=halo[cg][:, 0:3], in_=pu[:, 509:512]),
                          reads=[pub, halo_b[cg]], writes=[halo_b[cg]])
                    yield
                    kb.op("pool", lambda e: e.tensor_scalar(out=ucb[:], in0=uc[:], scalar1=1.0, scalar2=1.0,
                                                            op0=ALU.mult, op1=ALU.mult),
                          reads=[uc_b], writes=[ucb_b])
                    yield
                    pr, prb = next_bank(0, 8)
                    kb.op("pe", lambda e: e.matmul(pr[:], lhsT=wgate_b[:, cg * 2, :], rhs=ucb[:], start=True, stop=True),
                          reads=[ucb_b, b_misc], writes=[prb])
                    pg, pgb = next_bank(0, 8)
                    kb.op("pe", lambda e: e.matmul(pg[:], lhsT=wgate_b[:, cg * 2 + 1, :], rhs=ucb[:], start=True, stop=True),
                          reads=[ucb_b, b_misc], writes=[pgb])
                    yield
                    kb.op("act", lambda e: e.activation(out=rr[:], in_=pr[:], func=AF.Tanh,
                                                        bias=hbias[:, cg:cg + 1], scale=0.5),
                          reads=[prb, b_misc], writes=[rr_b])
                    kb.op("act", lambda e: e.activation(out=ii[:], in_=pg[:], func=AF.Tanh,
                                                        bias=hbias[:, 4 + cg:5 + cg], scale=0.5),
                          reads=[pgb, b_misc], writes=[ii_b])
                    kb.op("act", lambda e: e.activation(out=a2[:], in_=rr[:], func=AF.Exp, bias=c1[:, cg:cg + 1],
                                                        scale=c1[:, cg:cg + 1]),
                          reads=[rr_b, b_misc], writes=[a2_b])
                    kb.op("act", lambda e: e.activation(out=rr[:], in_=rr[:], func=AF.Exp, bias=c1x2[:, cg:cg + 1],
                                                        scale=c1x2[:, cg:cg + 1]),
                          reads=[rr_b, b_misc], writes=[rr_b])
                    yield
                    kb.op("pool", lambda e: e.tensor_scalar(out=ii[:], in0=ii[:], scalar1=1.0, scalar2=1.0,
                                                            op0=ALU.add, op1=ALU.mult),
                          reads=[ii_b], writes=[ii_b])
                    kb.op("pool", lambda e: e.tensor_tensor(out=ii[:], in0=ii[:], in1=uc[:], op=ALU.mult),
                          reads=[ii_b, uc_b], writes=[ii_b])
                    yield
                    kb.op("act", lambda e: e.activation(out=a2[:], in_=a2[:], func=AF.Sqrt, bias=qc[:, 0:1], scale=-0.25),
                          reads=[a2_b, b_misc], writes=[a2_b])
                    yield
                    kb.op("dve", lambda e: e.tensor_tensor(out=ii[:], in0=ii[:], in1=a2[:], op=ALU.mult),
                          reads=[ii_b, a2_b], writes=[ii_b])
                    init = 0.0 if tc == 0 else cy[:, 0:1]
                    kb.op("dve", lambda e: e.tensor_tensor_scan(out=hs[:], data0=rr[:], data1=ii[:], initial=init,
                                                                op0=ALU.mult, op1=ALU.add),
                          reads=[rr_b, ii_b] + ([cy_b] if tc else []), writes=[hs_b])
                    yield
                    kb.op("pool", lambda e: e.tensor_copy(out=cy[:], in_=hs[:, 511:512]), reads=[hs_b], writes=[cy_b])
                    hv = hs[:].rearrange("p (b two t) -> p b two t", two=2, t=128)
                    lv3 = ls[:].rearrange("p (b t) -> p b t", t=128)
                    kb.op("pool", lambda e: e.tensor_scalar(out=lv3, in0=hv[:, :, 0, :], scalar1=pcol(PC_SEL),
                                                            scalar2=1.0, op0=ALU.mult, op1=ALU.mult),
                          reads=[hs_b, b_const], writes=[ls_b])
                    yield
                    kb.op("dve", lambda e: e.scalar_tensor_tensor(out=lv3, in0=hv[:, :, 1, :], scalar=pcol(PC_SEL + 1),
                                                                  in1=lv3, op0=ALU.mult, op1=ALU.add),
                          reads=[hs_b, ls_b, b_const], writes=[ls_b])
                    o0 = tc * 256
                    pq, pqb = next_bank(0, 8)
                    for c in range(8):
                        kb.op("pe", lambda e, c=c: e.matmul(pq[:, 0:256], lhsT=wg_b[:, c, cg * 128:(cg + 1) * 128],
                                                            rhs=hTo[:, c, o0:o0 + 256], start=(c == 0), stop=(c == 7)),
                              reads=[wg_bb, hTo_b], writes=[pqb], signal=(c == 7))
                    yield
                    kb.op("act", lambda e: e.activation(out=g2[:], in_=pq[:, 0:256], func=AF.Square),
                          reads=[pqb], writes=[g2_b])
                    yield
                    kb.op("pool", lambda e: e.tensor_scalar(out=g2[:], in0=g2[:], scalar1=0.044715, scalar2=1.0,
                                                            op0=ALU.mult, op1=ALU.add),
                          reads=[g2_b], writes=[g2_b])
                    kb.op("dve", lambda e: e.tensor_tensor(out=g2[:], in0=g2[:], in1=pq[:, 0:256], op=ALU.mult),
                          reads=[g2_b, pqb], writes=[g2_b])
                    yield
                    kb.op("act", lambda e: e.activation(out=gp[:], in_=g2[:], func=AF.Tanh, scale=0.5 * GELU_C),
                          reads=[g2_b], writes=[gp_b])
                    yield
                    kb.op("dve", lambda e: e.scalar_tensor_tensor(out=gp[:], in0=gp[:], scalar=1.0, in1=pq[:, 0:256],
                                                                  op0=ALU.add, op1=ALU.mult),
                          reads=[gp_b, pqb], writes=[gp_b])
                    kb.op("dve", lambda e: e.scalar_tensor_tensor(out=lru_mixT[:, cg, o0:o0 + 256], in0=gp[:], scalar=0.5,
                                                                  in1=ls[:], op0=ALU.mult, op1=ALU.mult),
                          reads=[gp_b, ls_b], writes=[b_lru])
                    yield

                items = [(cg, tc) for tc in range(8) for cg in range(4)]
                for r0 in range(0, len(items), NCH):
                    live = [lru_iter(cg, tc, sets[(r0 + i) % NCH]) for i, (cg, tc) in enumerate(items[r0:r0 + NCH])]
                    while live:
                        nxt = []
                        for g in live:
                            try:
                                next(g)
                                nxt.append(g)
                            except StopIteration:
                                pass
                        live = nxt
                if debug:
                    kb.dma(dbg["d_lru"], lru_mixT[:], d_const, reads=[b_lru])
                kb.barrier()

            with ExitStack() as sD:
                KT = [sb("KT0", [128, 2, T], BF, sD)] * 2
                KT_b = [kb.buf("KT")] * 2
                QT = [sb("QT0", [128, 2, TO], BF, sD)] * 2
                QT_b = [kb.buf("QT")] * 2
                aug_d = [kb.dsem(f"aug{i}") for i in range(3)]
                Vh2 = [sb(f"Vh{i}", [128, 32, 129], BF, sD) for i in range(2)]
                V_b2 = [kb.buf("V") for _ in range(2)]
                for i in range(2):
                    kb.op("pool", lambda e, i=i: e.memset(Vh2[i][:, :, 128:129], 1.0), writes=[V_b2[i]])
                pt = [sb(f"pt{i}", [128, 512], BF, sD) for i in range(3)]
                pt_b = [kb.buf("pt") for _ in range(3)]
                Osb = [sb(f"Osb{i}", [128, 4, 129], F32, sD) for i in range(2)]
                Osb_b = [kb.buf("Osb") for _ in range(2)]
                att = sb("att", [128, 4, 128], F32, sD)
                att_b = kb.buf("att")
                junk = sb("junk", [128, 128], F32, sD)
                junk_b = kb.buf("junk")
                sm = sb("sm", [128, 16], F32, sD)
                sm_b = kb.buf("sm")
                attb = sb("attb", [128, 4, 128], BF, sD)
                attb_b = kb.buf("attb")
                PO = [P[3], P[4], P[5], P[6]]
                PO_b = [Pb[3], Pb[4], Pb[5], Pb[6]]

                srot = [0]
                deferred = []

                def v_tasks(h, nb):
                    s = h % 2
                    w, wb = wqk_b[s], wqk_bb[s]
                    for blk in range(32):
                        if nb == 1:
                            pb, pbb = P[srot[0] % 3], Pb[srot[0] % 3]
                            srot[0] += 1
                        else:
                            pb, pbb = next_bank(0, nb)
                        for c in range(8):
                            kb.op("pe", lambda e, c=c: e.matmul(pb[:, 0:128], lhsT=hT[:, c, blk * 128:(blk + 1) * 128],
                                                                rhs=w[:, c, 256:384], start=(c == 0), stop=(c == 7)),
                                  reads=[wb, hT_b], writes=[pbb], signal=(c == 7))
                        if nb != 1 and blk % 2:
                            kb.op("act", lambda e: e.activation(out=Vh2[s][:, blk, 0:128], in_=pb[:, 0:128], func=AF.Copy),
                                  reads=[pbb], writes=[V_b2[s]])
                        else:
                            kb.op("dve", lambda e: e.tensor_copy(out=Vh2[s][:, blk, 0:128], in_=pb[:, 0:128]),
                                  reads=[pbb], writes=[V_b2[s]])
                        yield

                def k_proj(h, nb):
                    s = h % 2
                    w, wb = wqk_b[s], wqk_bb[s]
                    for m in range(2):
                        kb.dma(KT[s][64:68, m, :], kaug[h], aug_d[s], writes=[KT_b[s]])
                    for tc in range(8):
                        pb, pbb = next_bank(0, nb)
                        tsl = slice(tc * 512, (tc + 1) * 512)
                        for c in range(8):
                            kb.op("pe", lambda e, c=c: e.matmul(pb[:], lhsT=w[:, c, 128:256], rhs=hT[:, c, tsl],
                                                                start=(c == 0), stop=(c == 7)),
                                  reads=[wb, hT_b], writes=[pbb], signal=(c == 7))
                        kb.op("act", lambda e: e.activation(out=KT[s][0:64, 0, tsl], in_=pb[0:64, :], func=AF.Copy),
                              reads=[pbb], writes=[KT_b[s]])
                        kb.op("dve", lambda e: e.tensor_copy(out=KT[s][0:64, 1, tsl], in_=pb[64:128, :]),
                              reads=[pbb], writes=[KT_b[s]])

                def q_proj(h, nb):
                    s = h % 2
                    w, wb = wqk_b[s], wqk_bb[s]
                    for m in range(2):
                        kb.dma(QT[0][64:68, m, :], qaug[h], aug_d[2], writes=[QT_b[0]])
                    for tc in range(4):
                        pb, pbb = next_bank(0, nb)
                        tsl = slice(tc * 512, (tc + 1) * 512)
                        for c in range(8):
                            kb.op("pe", lambda e, c=c: e.matmul(pb[:], lhsT=w[:, c, 0:128], rhs=hTo[:, c, tsl],
                                                                start=(c == 0), stop=(c == 7)),
                                  reads=[wb, hTo_b], writes=[pbb], signal=(c == 7))
                        kb.op("act", lambda e: e.activation(out=QT[0][0:64, 0, tsl], in_=pb[0:64, :], func=AF.Copy,
                                                            scale=0.125),
                              reads=[pbb], writes=[QT_b[0]])
                        kb.op("dve", lambda e: e.tensor_scalar(out=QT[0][0:64, 1, tsl], in0=pb[64:128, :], scalar1=0.125,
                                                               scalar2=None, op0=ALU.mult),
                              reads=[pbb], writes=[QT_b[0]])

                for _ in v_tasks(0, 3):
                    pass
                k_proj(0, 3)
                q_proj(0, 3)
                for h in range(NH):
                    hs_ = h % 2
                    Vh, V_b = Vh2[hs_], V_b2[hs_]
                    nxt_tasks = None
                    if h + 1 < NH:
                        load_wqk(h + 1)
                        nxt_tasks = v_tasks(h + 1, 1)
                    tiles = [(I, m, kbk) for I in range(4) for m in range(2) for kbk in range(8 * I + 8)]

                    def geom(I, kbk):
                        i_min = max(4 * I, kbk // 2)
                        return i_min, (i_min - 4 * I) * 128, (kbk // 2) >= 4 * I

                    def emit_qk(t):
                        I, m, kbk = tiles[t]
                        i_min, col0, masked = geom(I, kbk)
                        sbank[t] = srot[0] % 3
                        srot[0] += 1
                        sps, spsb = P[sbank[t]], Pb[sbank[t]]
                        kT = KT[hs_][0:68, m, kbk * 128:(kbk + 1) * 128]
                        q0 = 4 * I * 128
                        rd = [KT_b[hs_], QT_b[hs_]]
                        if masked:
                            kb.op("pe", lambda e: e.matmul(sps[:, col0:col0 + 128], lhsT=kT,
                                                           rhs=QT[hs_][0:68, m, q0 + col0:q0 + col0 + 128],
                                                           start=True, stop=False),
                                  reads=rd, writes=[spsb], signal=False)
                            last = (col0 + 128 == 512)
                            kb.op("pe", lambda e: e.matmul(sps[:, col0:col0 + 128], lhsT=ident, rhs=maskAB[kbk % 2],
                                                           start=False, stop=True),
                                  reads=[b_const, b_sel], writes=[spsb], signal=last)
                            if not last:
                                kb.op("pe", lambda e: e.matmul(sps[:, col0 + 128:512], lhsT=kT,
                                                               rhs=QT[hs_][0:68, m, q0 + col0 + 128:q0 + 512],
                                                               start=True, stop=True),
                                      reads=rd, writes=[spsb])
                        else:
                            kb.op("pe", lambda e: e.matmul(sps[:, col0:512], lhsT=kT,
                                                           rhs=QT[hs_][0:68, m, q0 + col0:q0 + 512],
                                                           start=True, stop=True),
                                  reads=rd, writes=[spsb])

                    def emit_exp(t):
                        I, m, kbk = tiles[t]
                        i_min, col0, masked = geom(I, kbk)
                        sps, spsb = P[sbank[t]], Pb[sbank[t]]
                        ptt, pttb = pt[t % 3], pt_b[t % 3]
                        kb.op("act", lambda e: e.activation(out=ptt[:, col0:512], in_=sps[:, col0:512], func=AF.Exp),
                              reads=[spsb], writes=[pttb])

                    def emit_pv(t):
                        I, m, kbk = tiles[t]
                        i_min, col0, masked = geom(I, kbk)
                        ptt, pttb = pt[t % 3], pt_b[t % 3]
                        for i in range(i_min, 4 * I + 4):
                            qi = i - 4 * I
                            kb.op("pe", lambda e, qi=qi, i=i: e.matmul(
                                PO[qi][:, 0:129], lhsT=ptt[:, qi * 128:(qi + 1) * 128],
                                rhs=Vh[:, kbk, :], start=(kbk == 0), stop=(kbk == 2 * i + 1)),
                                  reads=[pttb, V_b], writes=[PO_b[qi]], signal=(i == 4 * I + 3))

                    sbank = {}
                    LA = 2
                    for d in deferred:
                        d[1]()
                    del deferred[:]
                    for t in range(min(LA, len(tiles))):
                        emit_qk(t)
                    for t in range(len(tiles)):
                        I, m, kbk = tiles[t]
                        emit_exp(t)
                        if t + LA < len(tiles):
                            emit_qk(t + LA)
                        emit_pv(t)
                        for d in list(deferred):
                            d[0] -= 1
                            if d[0] <= 0:
                                deferred.remove(d)
                                d[1]()
                        if nxt_tasks is not None and t % 3 == 2:
                            next(nxt_tasks, None)
                        if kbk != 8 * I + 7:
                            continue
                        for qi in range(4):
                            kb.op("dve", lambda e, qi=qi: e.tensor_copy(out=Osb[m][:, qi, :], in_=PO[qi][:, 0:129]),
                                  reads=[PO_b[qi]], writes=[Osb_b[m]])
                        if m != 1:
                            continue
                        kb.op("dve", lambda e: e.reciprocal(out=sm[:, 0:4], in_=Osb[0][:, :, 128]),
                              reads=[Osb_b[0]], writes=[sm_b])
                        kb.op("dve", lambda e: e.reciprocal(out=sm[:, 4:8], in_=Osb[1][:, :, 128]),
                              reads=[Osb_b[1], sm_b], writes=[sm_b])
                        kb.op("dve", lambda e: e.tensor_scalar(out=sm[:, 4:8], in0=sm[:, 4:8], scalar1=nlam[:, 0:1],
                                                               scalar2=None, op0=ALU.mult),
                              reads=[sm_b, b_misc], writes=[sm_b])
                        for qi in range(4):
                            kb.op("dve", lambda e, qi=qi: e.tensor_scalar(out=att[:, qi, :], in0=Osb[0][:, qi, 0:128],
                                                                          scalar1=sm[:, qi:qi + 1], scalar2=None,
                                                                          op0=ALU.mult),
                                  reads=[Osb_b[0], sm_b], writes=[att_b])
                            kb.op("dve", lambda e, qi=qi: e.scalar_tensor_tensor(
                                out=att[:, qi, :], in0=Osb[1][:, qi, 0:128], scalar=sm[:, 4 + qi:5 + qi],
                                in1=att[:, qi, :], op0=ALU.mult, op1=ALU.add),
                                  reads=[Osb_b[1], sm_b, att_b], writes=[att_b])
                            kb.op("dve", lambda e, qi=qi: e.scalar_tensor_tensor(
                                out=junk[:], in0=att[:, qi, :], scalar=1.0, in1=att[:, qi, :], op0=ALU.mult, op1=ALU.mult,
                                accum_out=sm[:, 8 + qi:9 + qi]),
                                  reads=[att_b, sm_b], writes=[junk_b, sm_b])
                        kb.op("dve", lambda e: e.tensor_scalar(out=sm[:, 12:16], in0=sm[:, 8:12],
                                                               scalar1=1.0 / (128 * (1.0 - LAMBDA_INIT) ** 2),
                                                               scalar2=EPS / (1.0 - LAMBDA_INIT) ** 2,
                                                               op0=ALU.mult, op1=ALU.add),
                              reads=[sm_b], writes=[sm_b])
                        kb.op("pool", lambda e: e.tensor_tensor(out=sm[:, 12:16], in0=sm[:, 12:16], in1=mhalf[:],
                                                                op=ALU.pow),
                              reads=[sm_b, b_misc], writes=[sm_b])
                        for qi in range(4):
                            kb.op("dve", lambda e, qi=qi: e.scalar_tensor_tensor(
                                out=attb[:, qi, :], in0=att[:, qi, :], scalar=sm[:, 12 + qi:13 + qi],
                                in1=pv[:, PC_SUBGR:PC_SUBGR + 128], op0=ALU.mult, op1=ALU.mult),
                                  reads=[att_b, sm_b, b_const], writes=[attb_b])
                        def finish(h=h, I=I):
                            for qi in range(4):
                                kb.op("pe", lambda e, qi=qi: e.transpose(out=PTbf[:, qi * 128:(qi + 1) * 128],
                                                                         in_=attb[:, qi, :], identity=ident),
                                      reads=[attb_b, b_const], writes=[PTb], signal=(qi == 3))
                            kb.op("dve", lambda e: e.tensor_copy(out=attn_mixT[:, h, I * 512:(I + 1) * 512],
                                                                 in_=PTbf[:, 0:512]),
                                  reads=[PTb], writes=[b_attn])
                        deferred.append([10, finish])
                        if t == len(tiles) - 1 and h + 1 < NH:
                            for _ in nxt_tasks:
                                pass
                            k_proj(h + 1, 3)
                            q_proj(h + 1, 3)
                            if h + 1 == NH - 1:
                                prefetch_x1()
                        if debug and h == 0 and I == 0:
                            kb.dma(dbg["d_kt"], KT[0][:, :, 0:512], d_const, reads=[KT_b[0]])
                            kb.dma(dbg["d_qt"], QT[0][:, :, 0:512], d_const, reads=[QT_b[0]])
                            kb.dma(dbg["d_v"], Vh[:, 0:4, :], d_const, reads=[V_b])
                            kb.dma(dbg["d_sm"], sm[:], d_const, reads=[sm_b])
                            kb.dma(dbg["d_osb"][:, 0], Osb[0][:], d_const, reads=[Osb_b[0]])
                            kb.dma(dbg["d_osb"][:, 1], Osb[1][:], d_const, reads=[Osb_b[1]])
                for d in deferred:
                    d[1]()
                del deferred[:]
                if debug:
                    kb.dma(dbg["d_attn"], attn_mixT[:], d_const, reads=[b_attn])
                kb.barrier()

            s2 = s1
            sq = sb("sqE", [128, 8, 512], BF, s2)
            sq_b = kb.buf("sq")
            rt = [sb(f"rtE{i}", [128, 512], F32, s2) for i in range(2)]
            rt_b = [kb.buf("rt") for _ in range(2)]
            wup_g = [sb(f"wup_g{i}", [128, 8, 512], BF, s2) for i in range(2)]
            wup_gb = [kb.buf("wup") for _ in range(2)]
            wdn_g = [sb(f"wdn_g{i}", [128, 4, D], BF, s2) for i in range(2)]
            wdn_gb = [kb.buf("wdn") for _ in range(2)]
            wqk_f32 = wqk_b[0][:].rearrange("p c n -> p (c n)").bitcast(F32)
            sqm = [wqk_f32[:, i * 512:(i + 1) * 512] for i in range(2)]
            sqm_b = [kb.buf("sqm") for _ in range(2)]

            def load_group(fg):
                s = fg % 2
                for c in range(8):
                    load_cast(wup_g[s][:, c, :], wup_v[:, c, fg * 512:(fg + 1) * 512], 512, pcol(PC_GMLP + c),
                              dst_buf=wup_gb[s])
                for j in range(4):
                    for hf in range(2):
                        load_cast(wdn_g[s][:, j, hf * 512:(hf + 1) * 512], wdn_v[:, fg, j, hf * 512:(hf + 1) * 512],
                                  512, None, dst_buf=wdn_gb[s])

            with ExitStack() as sE:
                wo_b = sb("wo_b", [128, 8, D], BF, sE)
                wo_bb = [kb.buf("wo") for _ in range(2)]
                for hf in range(2):
                    for c in range(8):
                        load_cast(wo_b[:, c, hf * 512:(hf + 1) * 512], wout_v[:, c, hf * 512:(hf + 1) * 512], 512,
                                  None, dst_buf=wo_bb[hf], cast_en=("pool" if c % 2 else "dve"))
                load_group(0)

                def e_part1(tt):
                    tsl = slice(tt * 512, (tt + 1) * 512)
                    kb.op("act", lambda e: e.activation(out=sq[:], in_=x1T[:, :, tsl], func=AF.Square),
                          reads=[x1_b[tt]], writes=[sq_b])

                def e_part2(tt):
                    s = tt % 2
                    tsl = slice(tt * 512, (tt + 1) * 512)
                    pb, pbb = next_bank(4, 7)
                    for c in range(8):
                        kb.op("pe", lambda e, c=c: e.matmul(pb[:], lhsT=ones_bf[:], rhs=sq[:, c, :],
                                                            start=(c == 0), stop=(c == 7)),
                              reads=[sq_b, b_misc], writes=[pbb], signal=(c == 7))
                    kb.op("act", lambda e: e.activation(out=rt[s][:], in_=pb[:], func=AF.Ln, bias=epsc[:, 0:1],
                                                        scale=1.0 / D),
                          reads=[pbb, b_misc], writes=[rt_b[s]])
                    kb.op("act", lambda e: e.activation(out=pb[:], in_=rt[s][:], func=AF.Exp, scale=-0.5),
                          reads=[rt_b[s]], writes=[pbb])
                    for c in range(8):
                        kb.op("dve", lambda e, c=c: e.tensor_tensor(out=hmT[:, c, tsl], in0=x1T[:, c, tsl], in1=pb[:],
                                                                    op=ALU.mult),
                              reads=[x1_b[tt], pbb], writes=[hm_b])

                for tt in range(4):
                    tsl = slice(tt * 512, (tt + 1) * 512)
                    for fc in range(8):
                        pb, pbb = next_bank(0, 4)
                        for kc in range(8):
                            rhs = attn_mixT[:, kc, tsl] if kc < 4 else lru_mixT[:, kc - 4, tsl]
                            kb.op("pe", lambda e, kc=kc, rhs=rhs: e.matmul(
                                pb[:], lhsT=wo_b[:, kc, fc * 128:(fc + 1) * 128], rhs=rhs,
                                start=(kc == 0), stop=(kc == 7)),
                                  reads=[wo_bb[fc // 4], b_attn, b_lru], writes=[pbb], signal=(kc == 7))
                        kb.op("dve", lambda e: e.tensor_tensor(out=x1T[:, fc, tsl], in0=pb[:], in1=x1T[:, fc, tsl],
                                                               op=ALU.add),
                              reads=[pbb, x1_b[tt]], writes=[x1_b[tt]])
                    if tt > 0:
                        e_part2(tt - 1)
                    e_part1(tt)
                e_part2(3)
                if debug:
                    kb.dma(dbg["d_x1"], x1T[:], d_const, reads=x1_b)

            with ExitStack() as sF:
                yb = [sb(f"yb{i}", [128, 8, 256], F32, sF) for i in range(2)]
                yb_b = [kb.buf("yb") for _ in range(2)]
                y_d = [kb.dsem(f"y{i}") for i in range(2)]

                def final_part1(tt):
                    tsl = slice(tt * 512, (tt + 1) * 512)
                    kb.op("act", lambda e: e.activation(out=sq[:], in_=x1T[:, :, tsl], func=AF.Square),
                          reads=[x1_b[tt]], writes=[sq_b])

                def final_part2(tt):
                    s = tt % 2
                    tsl = slice(tt * 512, (tt + 1) * 512)
                    pb, pbb = next_bank(0, 4)
                    for c in range(8):
                        kb.op("pe", lambda e, c=c: e.matmul(pb[:], lhsT=ones_bf[:], rhs=sq[:, c, :],
                                                            start=(c == 0), stop=(c == 7)),
                              reads=[sq_b, b_misc], writes=[pbb], signal=(c == 7))
                    kb.op("act", lambda e: e.activation(out=rt[s][:], in_=pb[:], func=AF.Ln, bias=epsc[:, 0:1],
                                                        scale=1.0 / D),
                          reads=[pbb, b_misc], writes=[rt_b[s]])
                    kb.op("act", lambda e: e.activation(out=pb[:], in_=rt[s][:], func=AF.Exp, scale=-0.5),
                          reads=[rt_b[s]], writes=[pbb])
                    for hf in range(2):
                        hsl = slice(tt * 512 + hf * 256, tt * 512 + (hf + 1) * 256)
                        for c in range(8):
                            kb.op("dve", lambda e, c=c: e.scalar_tensor_tensor(
                                out=yb[hf][:, c, :], in0=x1T[:, c, hsl], scalar=pcol(PC_GFIN + c),
                                in1=pb[:, hf * 256:(hf + 1) * 256], op0=ALU.mult, op1=ALU.mult),
                                  reads=[x1_b[tt], pbb, b_const], writes=[yb_b[hf]])
                        kb.dma(yT_v[:, :, hsl], yb[hf][:], y_d[hf], reads=[yb_b[hf]])

                actT = [attn_mixT, lru_mixT]
                act_b = [b_attn, b_lru]
                ui = 0
                di = 0
                for fg in range(8):
                    s = fg % 2
                    if fg + 1 < 8:
                        load_group(fg + 1)
                    for tt in range(4):
                        tsl = slice(tt * 512, (tt + 1) * 512)
                        for j in range(4):
                            pb, pbb = P[ui % 4], Pb[ui % 4]
                            q, qb_ = sqm[ui % 2], sqm_b[ui % 2]
                            ui += 1
                            for c in range(8):
                                kb.op("pe", lambda e, c=c: e.matmul(pb[:], lhsT=wup_g[s][:, c, j * 128:(j + 1) * 128],
                                                                    rhs=hmT[:, c, tsl], start=(c == 0), stop=(c == 7)),
                                      reads=[wup_gb[s], hm_b], writes=[pbb], signal=(c == 7))
                            kb.op("act", lambda e: e.activation(out=q[:], in_=pb[:], func=AF.Square),
                                  reads=[pbb], writes=[qb_])
                            kb.op("dve", lambda e: e.scalar_tensor_tensor(out=actT[s][:, j, tsl], in0=pb[:], scalar=0.0,
                                                                          in1=q[:], op0=ALU.is_gt, op1=ALU.mult),
                                  reads=[pbb, qb_], writes=[act_b[s]])
                    for tt in range(4):
                        tsl = slice(tt * 512, (tt + 1) * 512)
                        for fc in range(8):
                            pb, pbb = P[4 + di % 3], Pb[4 + di % 3]
                            di += 1
                            for j in range(4):
                                kb.op("pe", lambda e, j=j: e.matmul(pb[:], lhsT=wdn_g[s][:, j, fc * 128:(fc + 1) * 128],
                                                                    rhs=actT[s][:, j, tsl], start=(j == 0), stop=(j == 3)),
                                      reads=[wdn_gb[s], act_b[s]], writes=[pbb], signal=(j == 3))
                            kb.op("dve", lambda e: e.tensor_tensor(out=x1T[:, fc, tsl], in0=pb[:], in1=x1T[:, fc, tsl],
                                                                   op=ALU.add),
                                  reads=[pbb, x1_b[tt]], writes=[x1_b[tt]])
                            if fg == 7 and fc == 2 and tt > 0:
                                final_part2(tt - 1)
                        if fg == 7:
                            final_part1(tt)
                if True:
                    final_part2(3)
                if debug:
                    kb.dma(dbg["d_x2"], x1T[:], d_const, reads=x1_b)

            kb.barrier()
            spE = kb.E["sp"]
            for d in y_d:
                spE.eng.wait_ge(d.sem, d.count)
    return nc


_NC_CACHE = {}


def _slopes():
    return np.array([2.0 ** (-8.0 * (i + 1) / NH) for i in range(NH)], dtype=np.float32)


def _const_tables(j):
    bf = ml_dtypes.bfloat16
    ident = np.eye(128, dtype=np.float32)
    kk = np.arange(128)[:, None]
    qq = np.arange(128)[None, :]
    tri = np.where(kk > qq, NEG, 0.0).astype(np.float32)
    full = np.full((128, 128), NEG, np.float32)
    zero = np.zeros((128, 128), np.float32)
    mA, mB = (tri, full) if j == 0 else (zero, tri)
    s0 = ident * (1.0 if j == 0 else 0.0)
    s1 = ident * (0.0 if j == 0 else 1.0)
    cbf = np.concatenate([ident, mA, mB, s0, s1], axis=1).astype(bf)
    sl = _slopes()
    kpos = np.arange(T)
    kaug = np.zeros((NH, 4, T), np.float32)
    qaug = np.zeros((NH, 4, TO), np.float32)
    own = (np.arange(TO) // 128) * 256 + j * 128 + (np.arange(TO) % 128)
    for h in range(NH):
        kaug[h, 0] = sl[h] * (kpos % 128)
        kaug[h, 1] = sl[h] * (kpos - kpos % 128)
        kaug[h, 2] = 1.0
        kaug[h, 3] = 1.0
        qaug[h, 0] = 1.0
        qaug[h, 1] = 1.0
        qaug[h, 2] = -sl[h] * (own % 128)
        qaug[h, 3] = -sl[h] * (own - own % 128)
    return cbf, kaug.astype(bf), qaug.astype(bf), own


def _chunked(v):
    return np.ascontiguousarray(np.asarray(v, np.float32).reshape(8, 128).T)


def _prep_shared(inp):
    f = lambda k: np.asarray(inp[k], np.float32)
    w_in = f("w_in")[0]
    wq, wk, wvv = w_in[:, 0:512], w_in[:, 512:1024], w_in[:, 1024:1536]
    wqkv = np.stack([np.concatenate([wq[:, h * 128:(h + 1) * 128], wk[:, h * 128:(h + 1) * 128],
                                     wvv[:, h * 128:(h + 1) * 128]], axis=1) for h in range(NH)], 0)
    sh = {
        "wqkv": np.ascontiguousarray(wqkv),
        "wu": np.ascontiguousarray(w_in[:, 1536:2048]),
        "wg": np.ascontiguousarray(w_in[:, 2048:2560]),
        "wout": np.ascontiguousarray(f("w_out")[0]),
        "wup": np.ascontiguousarray(f("w_up")[0]),
        "wdn": np.ascontiguousarray(f("w_down")[0]),
    }
    pv = np.zeros((128, NPC), np.float32)
    pv[:, PC_GMIX:PC_GMIX + 8] = _chunked(f("norm_mix_g")[0])
    pv[:, PC_GMLP:PC_GMLP + 8] = _chunked(f("norm_mlp_g")[0])
    pv[:, PC_GFIN:PC_GFIN + 8] = _chunked(f("final_g"))
    cw = f("conv_w")[0]
    for cg in range(4):
        for k in range(4):
            pv[:, PC_CW + cg * 4 + k] = cw[k, cg * 128:(cg + 1) * 128]
        pv[:, PC_CB + cg] = f("conv_b")[0][cg * 128:(cg + 1) * 128]
        pv[:, PC_BRG + cg] = f("b_rg")[0].reshape(512)[cg * 128:(cg + 1) * 128]
        pv[:, PC_BIG + cg] = f("b_ig")[0].reshape(512)[cg * 128:(cg + 1) * 128]
        pv[:, PC_L + cg] = f("lru_L")[0][cg * 128:(cg + 1) * 128]
    pv[:, PC_SUBG] = f("subln_g")[0]
    pv[:, PC_SUBGR:PC_SUBGR + 128] = f("subln_g")[0][None, :]
    lam = np.stack([f("lambda_q1")[0], f("lambda_k1")[0], f("lambda_q2")[0], f("lambda_k2")[0]], 0).reshape(256)
    pv[:, PC_LAM:PC_LAM + 256] = lam[None, :]
    wgate = np.zeros((128, 8, 128), np.float32)
    wrg, wig = f("w_rg")[0], f("w_ig")[0]
    for cg in range(4):
        for nl in range(2):
            n = cg * 2 + nl
            wgate[nl * 64:(nl + 1) * 64, cg * 2 + 0, nl * 64:(nl + 1) * 64] = wrg[n]
            wgate[nl * 64:(nl + 1) * 64, cg * 2 + 1, nl * 64:(nl + 1) * 64] = wig[n]
    sh["wgate"] = wgate
    return sh, pv


def _in_maps(inp):
    x = np.asarray(inp["x"], np.float32)
    sh, pv = _prep_shared(inp)
    maps, owns = [], []
    for core in range(8):
        b, j = core // 2, core % 2
        cbf, kaug, qaug, own = _const_tables(j)
        p = pv.copy()
        p[:, PC_SEL] = 1.0 if j == 0 else 0.0
        p[:, PC_SEL + 1] = 0.0 if j == 0 else 1.0
        xT = np.ascontiguousarray(x[b].T)
        m = dict(sh)
        m.update({"xf": xT, "xo": np.ascontiguousarray(xT[:, own]), "pvec": p, "cbf": cbf, "kaug": kaug,
                  "qaug": qaug})
        maps.append(m)
        owns.append(own)
    return maps, owns


def kernel(**inputs):
    if "nc" not in _NC_CACHE:
        _NC_CACHE["nc"] = build_program()
    nc = _NC_CACHE["nc"]
    maps, owns = _in_maps(inputs)
    res = run_bass_kernel_spmd(nc, maps, core_ids=list(range(8)))
    B, S = inputs["x"].shape[0], inputs["x"].shape[1]
    out = np.empty((B, S, D), np.float32)
    for core in range(8):
        b = core // 2
        out[b, owns[core], :] = np.asarray(res.results[core]["yT"], np.float32).T
    return out
```
